# Optimizing a Trainium2 kernel written in Bass

```python
import math
import jax
import jax.numpy as jnp
from jax import lax
import numpy as np

D_MODEL = 2048
BATCH = 16
SEQ = 256
DEPTH = 4
DEC_BATCH = 2
DEC_SEQ = 2048
PAST_LEN = 512

GRID_W = 64
N_MIXERS = 3
EPS = 1e-6

ML_INNER = 2 * D_MODEL
ML_HEADS = 8
ML_DV = ML_INNER // ML_HEADS
ML_DK = ML_DV // 2
ML_QK = ML_HEADS * ML_DK
ML_CHUNK = 64
ML_SPLITS = (ML_QK, 2 * ML_QK, 2 * ML_QK + ML_INNER, 2 * ML_QK + 2 * ML_INNER, 2 * ML_QK + 3 * ML_INNER)
ML_IN_COLS = 2 * ML_QK + 3 * ML_INNER + 4 * ML_HEADS

SC_INNER = 2 * D_MODEL
SC_IN_COLS = 4 * SC_INNER

GD_DK = 128
GD_DV = 128
GD_QK_HEADS = D_MODEL // GD_DK
GD_V_HEADS = 2 * GD_QK_HEADS
GD_QK = GD_QK_HEADS * GD_DK
GD_INNER = GD_V_HEADS * GD_DV
GD_CONV_CH = 2 * GD_QK + GD_INNER
GD_IN_COLS = GD_CONV_CH + GD_INNER + 4 * GD_V_HEADS
GD_CHUNK = 64

CONV_W = 3
N_ML = (DEPTH + 2) // 3
N_SC = (DEPTH + 1) // 3
N_GD = DEPTH // 3

kernel_name = "bidir_mlstm_shortconv_gdn_diffusion_step"


def rmsnorm(x, g):
    xf = x.astype(jnp.float32)
    xf = xf * lax.rsqrt(jnp.mean(jnp.square(xf), axis=-1, keepdims=True) + EPS)
    return (xf * g.astype(jnp.float32)).astype(x.dtype)


def dwconv3(x, w, rows):
    b, t, ch = x.shape
    xr = x.reshape(b, rows, t // rows, ch)
    xp = jnp.pad(xr, ((0, 0), (0, 0), (1, 1), (0, 0)))
    y = xp[:, :, :-2] * w[0] + xp[:, :, 1:-1] * w[1] + xp[:, :, 2:] * w[2]
    return y.reshape(b, t, ch)


def to_heads(a, nh):
    b, t, _ = a.shape
    return a.reshape(b, t, nh, -1).transpose(0, 2, 1, 3)


def flip_t(a):
    return jnp.flip(a, axis=2)


def mlstm_scan(q, k, v, ig, lf, C0, n0, m0):
    bsz, nh, t, _ = q.shape
    L = ML_CHUNK
    nc = t // L
    chunks = lambda a: jnp.moveaxis(a.reshape(bsz, nh, nc, L, *a.shape[3:]), 2, 0)
    causal = jnp.tril(jnp.ones((L, L), bool))

    def step(carry, inp):
        C, n, m = carry
        qc, kc, vc, ic, fc = inp
        b = jnp.cumsum(fc, axis=-1)
        dmat = jnp.where(causal, b[..., :, None] - b[..., None, :] + ic[..., None, :], -jnp.inf)
        inter = b + m[..., None]
        mt = jnp.maximum(inter, jnp.max(dmat, axis=-1))
        s = jnp.einsum('bhtk,bhsk->bhts', qc, kc) * jnp.exp(dmat - mt[..., None])
        sc = jnp.exp(inter - mt)
        num = sc[..., None] * jnp.einsum('bhtk,bhkv->bhtv', qc, C) + jnp.einsum('bhts,bhsv->bhtv', s, vc)
        den = sc * jnp.einsum('bhtk,bhk->bht', qc, n) + jnp.sum(s, axis=-1)
        h = num / jnp.maximum(jnp.abs(den), jnp.exp(-mt))[..., None]
        bl = b[..., -1]
        a = bl[..., None] - b + ic
        m_new = jnp.maximum(bl + m, jnp.max(a, axis=-1))
        w = jnp.exp(a - m_new[..., None])
        dec = jnp.exp(bl + m - m_new)
        C_new = dec[..., None, None] * C + jnp.einsum('bhs,bhsk,bhsv->bhkv', w, kc, vc)
        n_new = dec[..., None] * n + jnp.einsum('bhs,bhsk->bhk', w, kc)
        return (C_new, n_new, m_new), h

    (C, n, m), hs = lax.scan(step, (C0, n0, m0), (chunks(q), chunks(k), chunks(v), chunks(ig), chunks(lf)))
    return jnp.moveaxis(hs, 0, 2).reshape(bsz, nh, t, -1), C, n, m


def mlstm_mixer(h, w_in, b_gate, g_head, w_out, init):
    bsz, t, _ = h.shape
    f32 = jnp.float32
    q, k, v, o, z, gates = jnp.split(h @ w_in, ML_SPLITS, axis=-1)
    q = to_heads(q, ML_HEADS).astype(f32) * (ML_DK ** -0.5)
    k = to_heads(k, ML_HEADS).astype(f32)
    v = to_heads(v, ML_HEADS).astype(f32)
    gates = (gates + b_gate).astype(f32).reshape(bsz, t, 4, ML_HEADS).transpose(2, 0, 3, 1)
    ig_f, ig_b = gates[0], gates[1]
    lf_f, lf_b = jax.nn.log_sigmoid(gates[2]), jax.nn.log_sigmoid(gates[3])
    C0, n0, m0 = init
    h_f, C_f, n_f, m_f = mlstm_scan(q, k, v, ig_f, lf_f, C0[:, 0], n0[:, 0], m0[:, 0])
    h_b, C_b, n_b, m_b = mlstm_scan(flip_t(q), flip_t(k), flip_t(v), flip_t(ig_b), flip_t(lf_b),
                                    C0[:, 1], n0[:, 1], m0[:, 1])
    hs = (h_f + flip_t(h_b)).transpose(0, 2, 1, 3)
    hs = rmsnorm(hs, g_head.reshape(ML_HEADS, ML_DV)).reshape(bsz, t, ML_INNER).astype(h.dtype)
    y = hs * jax.nn.sigmoid(o) * jax.nn.silu(z)
    return y @ w_out, ((C_f, C_b), (n_f, n_b), (m_f, m_b))


def shortconv_mixer(h, w_in, w_conv, w_out, rows):
    u, bg, cg, z = jnp.split(h @ w_in, 4, axis=-1)
    y = bg * dwconv3(cg * u, w_conv, rows) * jax.nn.silu(z)
    return y @ w_out


def gdn_scan(q, k, v, beta, g, S0):
    bsz, nh, t, dk = q.shape
    L = GD_CHUNK
    nc = t // L
    rs = lambda a: a.reshape(bsz, nh, nc, L, *a.shape[3:])
    q, k, v, beta, g = rs(q), rs(k), rs(v), rs(beta), rs(g)
    G = jnp.cumsum(g, axis=-1)
    incl = jnp.tril(jnp.ones((L, L), bool))
    strict = jnp.tril(jnp.ones((L, L), bool), -1)
    decay = jnp.where(incl, jnp.exp(jnp.where(incl, G[..., :, None] - G[..., None, :], 0.0)), 0.0)
    A = jnp.where(strict, beta[..., :, None] * jnp.einsum('bhntk,bhnsk->bhnts', k, k) * decay, 0.0)
    eG = jnp.exp(G)
    rhs = jnp.concatenate([(beta * eG)[..., None] * k, beta[..., None] * v], axis=-1)
    sol = lax.linalg.triangular_solve(A, rhs, left_side=True, lower=True, unit_diagonal=True)
    W, U0 = sol[..., :dk], sol[..., dk:]
    P = jnp.einsum('bhntk,bhnsk->bhnts', q, k) * decay
    kdec = jnp.exp(G[..., -1:] - G)[..., None] * k
    gL = G[..., -1]
    cf = lambda a: jnp.moveaxis(a, 2, 0)

    def step(S, inp):
        Wc, Uc, Pc, qc, eGc, kdc, gLc = inp
        U = Uc - jnp.einsum('bhlk,bhkv->bhlv', Wc, S)
        o = eGc[..., None] * jnp.einsum('bhlk,bhkv->bhlv', qc, S) + jnp.einsum('bhts,bhsv->bhtv', Pc, U)
        S = jnp.exp(gLc)[..., None, None] * S + jnp.einsum('bhlk,bhlv->bhkv', kdc, U)
        return S, o

    S, o = lax.scan(step, S0, (cf(W), cf(U0), cf(P), cf(q), cf(eG), cf(kdec), cf(gL)))
    return jnp.moveaxis(o, 0, 2).reshape(bsz, nh, t, -1), S


def gdn_mixer(h, w_in, w_conv, A_log, dt_bias, g_norm, w_out, rows, init):
    bsz, t, _ = h.shape
    f32 = jnp.float32
    qkv, z, ab = jnp.split(h @ w_in, (GD_CONV_CH, GD_CONV_CH + GD_INNER), axis=-1)
    qkv = jax.nn.silu(dwconv3(qkv, w_conv, rows))
    q, k, v = jnp.split(qkv, (GD_QK, 2 * GD_QK), axis=-1)

    def l2n(a):
        af = a.astype(f32)
        return af * lax.rsqrt(jnp.sum(af * af, axis=-1, keepdims=True) + EPS)

    rep = GD_V_HEADS // GD_QK_HEADS
    q = jnp.repeat(l2n(to_heads(q, GD_QK_HEADS)), rep, axis=1) * (GD_DK ** -0.5)
    k = jnp.repeat(l2n(to_heads(k, GD_QK_HEADS)), rep, axis=1)
    v = to_heads(v, GD_V_HEADS).astype(f32)
    ab = ab.astype(f32).reshape(bsz, t, 4, GD_V_HEADS).transpose(2, 0, 3, 1)
    A = jnp.exp(A_log.astype(f32))
    dtb = dt_bias.astype(f32)
    g_f = -A[0][:, None] * jax.nn.softplus(ab[0] + dtb[0][:, None])
    g_b = -A[1][:, None] * jax.nn.softplus(ab[1] + dtb[1][:, None])
    beta_f, beta_b = jax.nn.sigmoid(ab[2]), jax.nn.sigmoid(ab[3])
    o_f, S_f = gdn_scan(q, k, v, beta_f, g_f, init[:, 0])
    o_b, S_b = gdn_scan(flip_t(q), flip_t(k), flip_t(v), flip_t(beta_b), flip_t(g_b), init[:, 1])
    o = (o_f + flip_t(o_b)).transpose(0, 2, 1, 3)
    o = rmsnorm(o, g_norm).reshape(bsz, t, GD_INNER).astype(h.dtype) * jax.nn.silu(z)
    return o @ w_out, (S_f, S_b)


def setup_inputs(seed: int = 0) -> dict:
    key = jax.random.key(seed)
    ks = jax.random.split(key, 32)
    f32 = jnp.float32
    nrm = lambda k, shape, s: jax.random.normal(k, shape, f32) * s
    H = ML_HEADS
    lin = jnp.linspace(3.0, 6.0, H, dtype=f32)
    f_bias = jnp.broadcast_to(jnp.concatenate([lin, lin]), (N_ML, 2 * H))
    b_ml_gate = jnp.concatenate([nrm(ks[0], (N_ML, 2 * H), 0.1), f_bias + nrm(ks[1], (N_ML, 2 * H), 0.1)], axis=-1)
    dt = jnp.exp(jax.random.uniform(ks[2], (N_GD, 2, GD_V_HEADS), f32, math.log(1e-3), math.log(1e-1)))
    return {
        "x_prompt": nrm(ks[3], (BATCH, SEQ, D_MODEL), 1.0),
        "x_sample": nrm(ks[4], (DEC_BATCH, DEC_SEQ, D_MODEL), 1.0),
        "c": nrm(ks[5], (DEC_BATCH, D_MODEL), 1.0),
        "cache_ml_C": nrm(ks[6], (DEC_BATCH, N_ML, 2, ML_HEADS, ML_DK, ML_DV), 0.05),
        "cache_ml_n": nrm(ks[7], (DEC_BATCH, N_ML, 2, ML_HEADS, ML_DK), 0.05),
        "cache_ml_m": nrm(ks[8], (DEC_BATCH, N_ML, 2, ML_HEADS), 0.5),
        "cache_gd_S": nrm(ks[9], (DEC_BATCH, N_GD, 2, GD_V_HEADS, GD_DK, GD_DV), 0.05),
        "c_ctx": nrm(ks[10], (D_MODEL,), 1.0),
        "w_ada": nrm(ks[11], (DEPTH, D_MODEL, 3 * D_MODEL), 0.5 * D_MODEL ** -0.5),
        "b_ada": nrm(ks[12], (DEPTH, 3 * D_MODEL), 0.01),
        "g_norm": 1.0 + nrm(ks[13], (DEPTH, D_MODEL), 0.02),
        "w_ml_in": nrm(ks[14], (N_ML, D_MODEL, ML_IN_COLS), D_MODEL ** -0.5),
        "b_ml_gate": b_ml_gate,
        "g_ml_head": 1.0 + nrm(ks[15], (N_ML, ML_INNER), 0.02),
        "w_ml_out": nrm(ks[16], (N_ML, ML_INNER, D_MODEL), ML_INNER ** -0.5),
        "w_sc_in": nrm(ks[17], (N_SC, D_MODEL, SC_IN_COLS), D_MODEL ** -0.5),
        "w_sc_conv": nrm(ks[18], (N_SC, CONV_W, SC_INNER), CONV_W ** -0.5),
        "w_sc_out": nrm(ks[19], (N_SC, SC_INNER, D_MODEL), SC_INNER ** -0.5),
        "w_gd_in": nrm(ks[20], (N_GD, D_MODEL, GD_IN_COLS), D_MODEL ** -0.5),
        "w_gd_conv": nrm(ks[21], (N_GD, CONV_W, GD_CONV_CH), CONV_W ** -0.5),
        "gd_A_log": jnp.log(jax.random.uniform(ks[22], (N_GD, 2, GD_V_HEADS), f32, 1.0, 16.0)),
        "gd_dt_bias": dt + jnp.log(-jnp.expm1(-dt)),
        "g_gd_norm": 1.0 + nrm(ks[23], (N_GD, GD_DV), 0.02),
        "w_gd_out": nrm(ks[24], (N_GD, GD_INNER, D_MODEL), GD_INNER ** -0.5),
        "g_final": 1.0 + nrm(ks[25], (D_MODEL,), 0.02),
    }


def reference(x_prompt, x_sample, c, cache_ml_C, cache_ml_n, cache_ml_m, cache_gd_S, c_ctx,
              w_ada, b_ada, g_norm, w_ml_in, b_ml_gate, g_ml_head, w_ml_out,
              w_sc_in, w_sc_conv, w_sc_out, w_gd_in, w_gd_conv, gd_A_log, gd_dt_bias, g_gd_norm, w_gd_out,
              g_final):
    f32 = jnp.float32

    def stream(x, cond, rows, ml_init, gd_init, collect):
        ml_states, gd_states = [], []
        for l in range(DEPTH):
            mod = jax.nn.silu(cond) @ w_ada[l] + b_ada[l]
            shift, scale, gate = jnp.split(mod[:, None, :], 3, axis=-1)
            h = rmsnorm(x, g_norm[l]) * (1.0 + scale) + shift
            j = l // N_MIXERS
            kind = l % N_MIXERS
            if kind == 0:
                out, st = mlstm_mixer(h, w_ml_in[j], b_ml_gate[j], g_ml_head[j], w_ml_out[j], ml_init(j))
                if collect:
                    ml_states.append(st)
            elif kind == 1:
                out = shortconv_mixer(h, w_sc_in[j], w_sc_conv[j], w_sc_out[j], rows)
            else:
                out, st = gdn_mixer(h, w_gd_in[j], w_gd_conv[j], gd_A_log[j], gd_dt_bias[j], g_gd_norm[j],
                                    w_gd_out[j], rows, gd_init(j))
                if collect:
                    gd_states.append(st)
            x = x + gate * out
        return rmsnorm(x, g_final), ml_states, gd_states

    bp = x_prompt.shape[0]
    ml_zero = lambda j: (jnp.zeros((bp, 2, ML_HEADS, ML_DK, ML_DV), f32),
                         jnp.zeros((bp, 2, ML_HEADS, ML_DK), f32),
                         jnp.zeros((bp, 2, ML_HEADS), f32))
    gd_zero = lambda j: jnp.zeros((bp, 2, GD_V_HEADS, GD_DK, GD_DV), f32)
    y_prompt, ml_st, gd_st = stream(x_prompt, c_ctx[None, :], 1, ml_zero, gd_zero, True)

    rows = x_sample.shape[1] // GRID_W
    ml_cache = lambda j: (cache_ml_C[:, j].astype(f32), cache_ml_n[:, j].astype(f32), cache_ml_m[:, j].astype(f32))
    gd_cache = lambda j: cache_gd_S[:, j].astype(f32)
    y_sample, _, _ = stream(x_sample, c, rows, ml_cache, gd_cache, False)

    dt = x_prompt.dtype
    state_ml_C = jnp.stack([jnp.stack(s[0], axis=1) for s in ml_st], axis=1).astype(dt)
    state_ml_n = jnp.stack([jnp.stack(s[1], axis=1) for s in ml_st], axis=1).astype(dt)
    state_ml_m = jnp.stack([jnp.stack(s[2], axis=1) for s in ml_st], axis=1).astype(dt)
    state_gd_S = jnp.stack([jnp.stack(s, axis=1) for s in gd_st], axis=1).astype(dt)
    return (y_prompt, y_sample, state_ml_C, state_ml_n, state_ml_m, state_gd_S)
```

```python
import math
from contextlib import ExitStack
import numpy as np
import concourse.bass as bass
import concourse.mybir as mybir
from concourse.bass_utils import run_bass_kernel_spmd

F32 = mybir.dt.float32
BF16 = mybir.dt.bfloat16
AF = mybir.ActivationFunctionType
ALU = mybir.AluOpType
AX = mybir.AxisListType

D = 2048
KT = D // 128
EPS = 1e-6
INNER = 4096
ML_H, ML_DK, ML_DV = 8, 256, 512
ML_COLS = 16416
SC_COLS = 16384
GD_COLS = 12416
GD_HV, GD_HQ = 32, 16
CH = 128


class Sched:
    def __init__(self, nc, stack, ndma=30):
        self.nc = nc
        self.E = {"pe": nc.tensor, "dve": nc.vector, "act": nc.scalar, "pool": nc.gpsimd, "sp": nc.sync}
        self.sem = {e: stack.enter_context(nc.semaphore("s_" + e)) for e in ("pe", "dve", "act", "pool")}
        self.cnt = {e: 0 for e in self.sem}
        self.dsem = [stack.enter_context(nc.semaphore("d%d" % i)) for i in range(ndma)]
        self.dcnt = [0] * ndma
        self.drr = 0
        self.drr_pool = 0
        self.known = {e: {} for e in self.E}
        self.last_w = {}
        self.readers = {}
        self.nwaits = 0
        self.ninst = 0
        self.deferred = []
        self.round = 0

    def _semobj(self, key):
        return self.sem[key] if isinstance(key, str) else self.dsem[key]

    def _wait(self, eng, tok):
        key, val = tok
        if eng == "pe" and key == "pe":
            return
        if self.known[eng].get(key, 0) >= val:
            return
        self.E[eng].wait_ge(self._semobj(key), val)
        self.known[eng][key] = val
        self.nwaits += 1

    def _deps(self, eng, reads, writes):
        for r in reads:
            lw = self.last_w.get(r)
            if lw:
                self._wait(eng, lw)
        for w in writes:
            lw = self.last_w.get(w)
            if lw:
                self._wait(eng, lw)
            for key, val in self.readers.get(w, {}).items():
                self._wait(eng, (key, val))

    def _record(self, tok, reads, writes):
        key, val = tok
        for r in reads:
            d = self.readers.setdefault(r, {})
            d[key] = max(d.get(key, 0), val)
        for w in writes:
            self.last_w[w] = tok
            self.readers[w] = {}

    def op(self, eng, fn, reads=(), writes=()):
        self._deps(eng, reads, writes)
        inst = fn(self.E[eng])
        self.cnt[eng] += 1
        inst.then_inc(self.sem[eng], 1)
        self.ninst += 1
        self._record((eng, self.cnt[eng]), reads, writes)

    def dma(self, q, out, in_, reads=(), writes=(), slow=False):
        if self.deferred and reads:
            rs = set(reads)
            hit = [it for it in self.deferred if rs & set(it[2].get("writes", ()))]
            if hit:
                self.deferred = [it for it in self.deferred if not (rs & set(it[2].get("writes", ())))]
                for it in hit:
                    self.dma(*it[1], **it[2])
        self._deps(q, reads, writes)
        npool = 6
        if q == "pool":
            i = self.drr_pool
            self.drr_pool = (self.drr_pool + 1) % npool
        else:
            i = npool + self.drr
            self.drr = (self.drr + 1) % (len(self.dsem) - npool)
        if self.dcnt[i]:
            self._wait(q, (i, self.dcnt[i]))
        self.dcnt[i] += 16
        if slow:
            self.E[q].dma_start(out=out, in_=in_, allow_slow_non_contiguous=True).then_inc(self.dsem[i], 16)
        else:
            self.E[q].dma_start(out=out, in_=in_).then_inc(self.dsem[i], 16)
        self.ninst += 1
        self._record((i, self.dcnt[i]), reads, writes)

    def barrier(self):
        self.flush()
        for e in self.E:
            for i, c in enumerate(self.dcnt):
                if c:
                    self._wait(e, (i, c))
            for k, c in self.cnt.items():
                if c and k != e:
                    self._wait(e, (k, c))
        for e in self.sem:
            if self.cnt[e]:
                self._wait(e, (e, self.cnt[e])) if e != "pe" else None
        self.last_w = {}
        self.readers = {}

    def defer_dma(self, *a, **kw):
        self.deferred.append([self.round + 3, a, kw])

    def tick(self):
        self.round += 1
        keep = []
        for it in self.deferred:
            if it[0] <= self.round:
                self.dma(*it[1], **it[2])
            else:
                keep.append(it)
        self.deferred = keep

    def flush(self):
        for it in self.deferred:
            self.dma(*it[1], **it[2])
        self.deferred = []

    def finish(self, eng="sp"):
        for i, c in enumerate(self.dcnt):
            if c:
                self._wait(eng, (i, c))
        for e, c in self.cnt.items():
            if c:
                self._wait(eng, (e, c))


def run_streams(gens, S=None):
    gens = list(gens)
    while gens:
        for g in list(gens):
            try:
                next(g)
            except StopIteration:
                gens.remove(g)
        if S is not None:
            S.tick()
    if S is not None:
        S.flush()


class TN:
    def __init__(self, name, t):
        self.name = name
        self.t = t

    def __getitem__(self, k):
        return self.t[k]


class K:
    def __init__(self, seqs, layers, do_final=True):
        self.seqs = []
        t0 = 0
        npi = 0
        for s in seqs:
            s = dict(s)
            s["t0"] = t0
            t0 += s["T"]
            if s["kind"] == "p":
                s["pi"] = npi
                npi += 1
            self.seqs.append(s)
        self.NT = t0
        assert self.NT % 512 == 0
        self.layers = layers
        self.do_final = do_final
        self.n_p = npi
        self.Tmax = max(s["T"] for s in self.seqs)

    def build(self):
        nc = bass.Bass("TRN2", target_bir_lowering=False)
        self.nc = nc
        NT = self.NT
        dt = nc.dram_tensor
        ins = {}

        def inp(name, shape, dtype=F32):
            ins[name] = dt(name, list(shape), dtype, kind="ExternalInput").ap()
            return ins[name]

        def scratch(name, shape, dtype):
            return dt(name, list(shape), dtype, kind="Internal").ap()

        def outp(name, shape):
            return dt(name, list(shape), F32, kind="ExternalOutput").ap()

        self.ins = ins
        inp("x_in", [NT, D])
        inp("condT", [128, KT, 2])
        inp("ident", [128, 128])
        inp("tril", [128, 128])
        inp("triu", [128, 128])
        inp("ones", [128, 128])
        inp("w_ada", [4, D, 3 * D])
        inp("b_ada", [4, 3 * D])
        inp("g_norm", [4, D])
        inp("g_final", [1, D])
        kinds = set(l % 3 for l in self.layers)
        self.kinds = kinds
        if 1 in kinds:
            inp("w_sc_in", [D, SC_COLS])
            inp("w_sc_conv", [128, 32, 3])
            inp("w_sc_out", [INNER, D])
        if 0 in kinds:
            inp("w_ml_in", [2, D, ML_COLS])
            inp("b_mlg", [2, 8, 4])
            inp("g_ml_head", [2, INNER])
            inp("w_ml_out", [2, INNER, D])
            inp("mlC0", [2, 2, 8, ML_DK, ML_DV])
            inp("mln0", [2, 2, 8, ML_DK])
            inp("mlm0", [2, 2, 8])
            self.QT = scratch("QT", [2048, NT], BF16)
            self.KTd = scratch("KTd", [2048, NT], BF16)
            self.VOZ = scratch("VOZ", [3, NT, INNER], BF16)
            self.GATES = scratch("GATES", [4, 8, NT], F32)
            self.st_C = outp("st_C", [self.n_p, 2, 2, 8, ML_DK, ML_DV])
            self.st_n = outp("st_n", [self.n_p, 2, 2, 8, ML_DK])
            self.st_m = outp("st_m", [self.n_p, 2, 2, 8])
        if 2 in kinds:
            inp("w_gd_in", [D, GD_COLS])
            inp("w_gd_conv", [128, 64, 3])
            inp("gd_par", [128, 4])
            inp("g_gd_norm", [1, 128])
            inp("w_gd_out", [INNER, D])
            inp("gdS0", [2, 32, 128, 128])
            inp("trilS", [128, 128])
            inp("lvl", [128, 7, 128])
            inp("lvlT", [128, 7, 128])
            inp("triuS", [128, 128])
            self.GQT = scratch("GQT", [2048, NT], BF16)
            self.GKT = scratch("GKT", [2048, NT], BF16)
            self.GV = scratch("GV", [NT, INNER], BF16)
            self.GZ = scratch("GZ", [NT, INNER], BF16)
            self.GAB = scratch("GAB", [4, 32, NT], F32)
            self.st_S = outp("st_S", [self.n_p, 1, 2, 32, 128, 128])
        self.X = scratch("X", [NT, D], F32)
        self.YT = scratch("YT", [INNER, NT], BF16)
        self.MOD = scratch("MOD", [2, 3 * D], F32)
        self.OSD = scratch("OSD", [4, self.Tmax, 512], F32)
        self.y_out = outp("y_out", [NT, D])

        with ExitStack() as st:
            self.S = Sched(nc, st)
            self._uid = [0]

            def _mk(stk, name, shape, dtype):
                self._uid[0] += 1
                return TN(name, stk.enter_context(nc.sbuf_tensor("%s_%d" % (name, self._uid[0]), list(shape), dtype)))
            self.sbp = lambda stk: (lambda name, shape, dtype: _mk(stk, name, shape, dtype))
            sb = self.sbp(st)
            ps = lambda name, shape, dtype: TN(name, st.enter_context(nc.psum_tensor(name, list(shape), dtype)))
            self.ident = sb("ident_sb", [128, 128], F32)
            self.identb = sb("identb", [128, 128], BF16)
            self.tril = sb("tril_sb", [128, 128], F32)
            self.triu = sb("triu_sb", [128, 128], F32)
            self.ones = sb("ones_sb", [128, 128], F32)
            if 2 in self.kinds:
                self.trilS = sb("trilS_sb", [128, 128], F32)
                self.triuS = sb("triuS_sb", [128, 128], F32)
                self.lvl = sb("lvl_sb", [128, 7, 128], BF16)
                self.lvlT = sb("lvlT_sb", [128, 7, 128], BF16)
            self.condT = sb("condT_sb", [128, KT, 2], F32)
            self.scT = sb("scT", [128, KT, 2], BF16)
            self.small = sb("small", [128, 64], F32)
            self.stg = [sb("stg%d" % i, [128, 512], BF16) for i in range(4)]
            self.stgf = [sb("stgf%d" % i, [128, 512], F32) for i in range(2)]
            self.modst = [sb("modst%d" % i, [1, 2, 512], F32) for i in range(2)]
            self.pp = [ps("pp%d" % i, [128, 512], F32) for i in range(6)]
            self.ptb = ps("ptb", [128, 1024], BF16)
            self.pmisc = ps("pmisc", [128, 512], F32)
            self.ppi = 0
            self.stgi = 0
            self.emit()
            self.S.finish("sp")
        return nc

    def psum(self):
        p = self.pp[self.ppi % len(self.pp)]
        self.ppi += 1
        return p

    def psum_s(self, A):
        p = A["pp"][A["ppi"] % len(A["pp"])]
        A["ppi"] += 1
        return p

    def stage(self):
        p = self.stg[self.stgi % len(self.stg)]
        self.stgi += 1
        return p

    def emit(self):
        S, nc, ins = self.S, self.nc, self.ins
        S.dma("sp", self.ident[:], ins["ident"][:, :], writes=["ident_sb"])
        S.dma("sp", self.tril[:], ins["tril"][:, :], writes=["tril_sb"])
        S.dma("sp", self.triu[:], ins["triu"][:, :], writes=["triu_sb"])
        S.dma("sp", self.ones[:], ins["ones"][:, :], writes=["ones_sb"])
        S.dma("sp", self.condT[:], ins["condT"][:, :, :], writes=["condT_sb"])
        if "trilS" in ins:
            S.dma("sp", self.trilS[:], ins["trilS"][:, :], writes=["trilS_sb"])
            S.dma("sp", self.triuS[:], ins["triuS"][:, :], writes=["triuS_sb"])
            S.dma("pool", self.lvl[:], ins["lvl"][:, :, :], writes=["lvl"])
            S.dma("pool", self.lvlT[:], ins["lvlT"][:, :, :], writes=["lvl"])
        S.op("dve", lambda e: e.tensor_copy(self.identb[:], self.ident[:]), reads=["ident_sb"], writes=["identb"])
        S.op("act", lambda e: e.activation(out=self.scT[:], in_=self.condT[:], func=AF.Silu),
             reads=["condT_sb"], writes=["scT"])
        S.dma("sp", self.X[:, :], ins["x_in"][:, :], writes=["X"])
        for l in self.layers:
            self.layer(l)
        if self.do_final:
            self.final_norm()
        else:
            S.dma("sp", self.y_out[:, :], self.X[:, :], reads=["X"], writes=["y_out"])

    def layer(self, l):
        S, ins = self.S, self.ins
        kind = l % 3
        j = l // 3
        with ExitStack() as stk:
            sb = self.sbp(stk)
            self.wbuf = [sb("wbuf%d" % i, [128, KT, 512], BF16) for i in range(2)]
            self.compute_mod(l)
            self.hT = sb("hT", [128, KT, self.NT], BF16)
            with ExitStack() as stkA:
                self.phase_a(l, self.sbp(stkA))
                S.barrier()
            if kind == 1:
                with ExitStack() as stk2:
                    self.sc_mixer(self.sbp(stk2))
                    S.barrier()
            elif kind == 0:
                self.ml_proj(j)
                S.barrier()
            else:
                with ExitStack() as stk2:
                    self.gd_proj(self.sbp(stk2))
                    S.barrier()
        if kind == 0:
            with ExitStack() as stk:
                self.ml_scan(j, self.sbp(stk))
                S.barrier()
        if kind == 2:
            with ExitStack() as stk:
                self.gd_scan(self.sbp(stk))
                S.barrier()
        wout = {0: lambda: ins["w_ml_out"][j], 1: lambda: ins["w_sc_out"], 2: lambda: ins["w_gd_out"]}[kind]()
        with ExitStack() as stk:
            self.phase_d(l, wout, self.sbp(stk))
            S.barrier()

    def compute_mod(self, l):
        S, ins = self.S, self.ins
        for cb in range(12):
            wb = self.wbuf[cb % 2]
            src = ins["w_ada"][l, :, cb * 512:(cb + 1) * 512].rearrange("(kt p) c -> p kt c", p=128)
            S.dma("pool", wb[:], src, writes=[wb.name])
            bb = self.stgf[cb % 2]
            S.dma("sp", bb[0:1, :], ins["b_ada"][l:l + 1, cb * 512:(cb + 1) * 512], writes=[bb.name])
            ms = self.modst[cb % 2]
            for c in range(2):
                p = self.psum()

                def mm(e, p=p, wb=wb, c=c):
                    for kt in range(KT):
                        r = e.matmul(p[0:1, :], self.scT[:, kt, c:c + 1], wb[:, kt, :], start=(kt == 0),
                                     stop=(kt == KT - 1))
                    return r
                S.op("pe", mm, reads=["scT", wb.name], writes=[p.name])
                S.op("dve", lambda e, p=p, bb=bb, ms=ms, c=c: e.tensor_tensor(
                    out=ms[0:1, c, :], in0=p[0:1, :], in1=bb[0:1, :], op=ALU.add),
                    reads=[p.name, bb.name], writes=[ms.name])
                S.dma("sp", self.MOD[c:c + 1, cb * 512:(cb + 1) * 512], ms[0:1, c, :],
                      reads=[ms.name], writes=["MOD"])

    def tile_cond(self, tt):
        tok = tt * 128
        for s in self.seqs:
            if s["t0"] <= tok < s["t0"] + s["T"]:
                return s["cond"]
        raise AssertionError

    def rstd_col(self, ss, rs, n, key="small"):
        S = self.S
        S.op("dve", lambda e: e.tensor_scalar(out=rs, in0=ss, scalar1=1.0 / n, scalar2=EPS,
                                              op0=ALU.mult, op1=ALU.add), reads=[key], writes=[key])
        S.op("act", lambda e: e.activation(out=rs, in_=rs, func=AF.Sqrt), reads=[key], writes=[key])
        S.op("dve", lambda e: e.reciprocal(out=rs, in_=rs), reads=[key], writes=[key])

    def phase_a(self, l, sb):
        S, ins = self.S, self.ins
        big = [sb("big%d" % i, [128, D], F32) for i in range(4)]
        hbfs = [sb("hbf%d" % i, [128, D], BF16) for i in range(2)]
        amod = sb("amod", [128, D], F32)
        shiftb = sb("shiftb", [128, D], F32)
        sm2 = sb("sm2", [128, 8], F32)
        NTT = self.NT // 128
        cur = None
        for tt in range(NTT):
            c = self.tile_cond(tt)
            par = tt % 2
            if c != cur:
                cur = c
                S.dma("sp", shiftb[:], self.MOD[c:c + 1, 0:D].partition_broadcast(128), reads=["MOD"],
                      writes=["shiftb"])
                S.dma("sp", amod[:], self.MOD[c:c + 1, D:2 * D].partition_broadcast(128), reads=["MOD"],
                      writes=["amod"])
                S.dma("sp", big[2][:], ins["g_norm"][l:l + 1, :].partition_broadcast(128), writes=["big2"])
                S.op("dve", lambda e: e.scalar_tensor_tensor(out=amod[:], in0=amod[:], scalar=1.0, in1=big[2][:],
                                                             op0=ALU.add, op1=ALU.mult),
                     reads=["amod", "big2"], writes=["amod"])
            xt = big[par]
            junk = big[2 + par]
            hbf = hbfs[par]
            S.dma("sp", xt[:], self.X[tt * 128:(tt + 1) * 128, :], reads=["X"], writes=[xt.name])
            ss = sm2[:, 2 * par:2 * par + 1]
            rs = sm2[:, 2 * par + 1:2 * par + 2]
            skey = "sm2_%d" % par
            S.op("act", lambda e, xt=xt: e.activation(out=junk[:], in_=xt[:], func=AF.Square, accum_out=ss),
                 reads=[xt.name], writes=[junk.name, skey])
            self.rstd_col(ss, rs, D, key=skey)
            S.op("dve", lambda e, xt=xt: e.scalar_tensor_tensor(out=junk[:], in0=xt[:], scalar=rs, in1=amod[:],
                                                                op0=ALU.mult, op1=ALU.mult),
                 reads=[xt.name, skey, "amod"], writes=[junk.name])
            S.op("dve", lambda e: e.tensor_tensor(out=hbf[:], in0=junk[:], in1=shiftb[:], op=ALU.add),
                 reads=[junk.name, "shiftb"], writes=[hbf.name])
            for g in range(KT // 8):
                def tr(e, g=g):
                    for jj in range(8):
                        kt = g * 8 + jj
                        r = e.transpose(self.ptb[:, jj * 128:(jj + 1) * 128], hbf[:, kt * 128:(kt + 1) * 128],
                                        self.identb[:])
                    return r
                S.op("pe", tr, reads=[hbf.name, "identb"], writes=["ptb"])
                S.op("act", lambda e, g=g, tt=tt: e.copy(
                    out=self.hT[:, g * 8:(g + 1) * 8, tt * 128:(tt + 1) * 128],
                    in_=self.ptb[:, :].rearrange("p (j t) -> p j t", j=8)), reads=["ptb"], writes=["hT"])

    def load_w(self, W, col0, ncols, i):
        wb = self.wbuf[i % 2]
        src = W[:, col0:col0 + ncols].rearrange("(kt p) c -> p kt c", p=128)
        self.S.dma("pool", wb[:, :, 0:ncols], src, writes=[wb.name])
        return wb

    def proj_fm(self, wb, c0, m, tb, p):
        def mm(e):
            for kt in range(KT):
                r = e.matmul(p[0:m, :], wb[:, kt, c0:c0 + m], self.hT[:, kt, tb * 512:(tb + 1) * 512],
                             start=(kt == 0), stop=(kt == KT - 1))
            return r
        self.S.op("pe", mm, reads=[wb.name, "hT"], writes=[p.name])

    def proj_tm(self, wb, tt, p):
        def mm(e):
            for kt in range(KT):
                r = e.matmul(p[:, :], self.hT[:, kt, tt * 128:(tt + 1) * 128], wb[:, kt, :],
                             start=(kt == 0), stop=(kt == KT - 1))
            return r
        self.S.op("pe", mm, reads=[wb.name, "hT"], writes=[p.name])

    def evac(self, i, out, in_, reads, writes, scale=None):
        S = self.S
        if i % 2 == 0:
            if scale is None:
                S.op("act", lambda e: e.copy(out=out, in_=in_), reads=reads, writes=writes)
            else:
                S.op("act", lambda e: e.activation(out=out, in_=in_, func=AF.Copy, scale=scale), reads=reads,
                     writes=writes)
        else:
            if scale is None:
                S.op("dve", lambda e: e.tensor_copy(out, in_), reads=reads, writes=writes)
            else:
                S.op("dve", lambda e: e.tensor_scalar(out=out, in0=in_, scalar1=scale, scalar2=None, op0=ALU.mult),
                     reads=reads, writes=writes)

    def sc_mixer(self, sb):
        S, ins = self.S, self.ins
        NTB = self.NT // 512
        W = ins["w_sc_in"]
        wconv = sb("wconv", [128, 32, 3], F32)
        t2k = [sb("t2k%d" % i, [128, 512], F32) for i in range(4)]
        S.dma("sp", wconv[:], ins["w_sc_conv"][:, :, :], writes=["wconv"])
        for ct in range(32):
            wb = self.wbuf[ct % 2]
            wk = wb.name
            for jj in range(4):
                src = W[:, jj * INNER + ct * 128: jj * INNER + (ct + 1) * 128].rearrange("(kt p) c -> p kt c", p=128)
                S.dma("pool", wb[:, :, jj * 128:(jj + 1) * 128], src, writes=[wk])
            for tb in range(NTB):
                roww = self.roww_of_block(tb)
                pl = [self.psum() for _ in range(4)]
                pu, pB, pC, pz = pl
                for jj, p in enumerate(pl):
                    self.proj_fm(wb, jj * 128, 128, tb, p)
                u_sb, cu, acc, sz = t2k
                S.op("act", lambda e, pu=pu: e.copy(out=u_sb[:], in_=pu[:, :]), reads=[pu.name], writes=["t2k0"])
                S.op("act", lambda e, pz=pz: e.activation(out=sz[:], in_=pz[:, :], func=AF.Silu),
                     reads=[pz.name], writes=["t2k3"])
                S.op("dve", lambda e, pC=pC: e.tensor_tensor(out=cu[:], in0=pC[:, :], in1=u_sb[:], op=ALU.mult),
                     reads=[pC.name, "t2k0"], writes=["t2k1"])
                w0 = wconv[:, ct, 0:1]
                w1 = wconv[:, ct, 1:2]
                w2 = wconv[:, ct, 2:3]
                S.op("dve", lambda e, w1=w1: e.tensor_scalar(out=acc[:], in0=cu[:], scalar1=w1, scalar2=None,
                                                             op0=ALU.mult), reads=["t2k1", "wconv"], writes=["t2k2"])
                cu3 = cu[:].rearrange("p (r w) -> p r w", w=roww)
                acc3 = acc[:].rearrange("p (r w) -> p r w", w=roww)
                S.op("dve", lambda e, w0=w0, cu3=cu3, acc3=acc3: e.scalar_tensor_tensor(
                    out=acc3[:, :, 1:], in0=cu3[:, :, :-1], scalar=w0, in1=acc3[:, :, 1:], op0=ALU.mult, op1=ALU.add),
                    reads=["t2k1", "t2k2", "wconv"], writes=["t2k2"])
                S.op("dve", lambda e, w2=w2, cu3=cu3, acc3=acc3: e.scalar_tensor_tensor(
                    out=acc3[:, :, :-1], in0=cu3[:, :, 1:], scalar=w2, in1=acc3[:, :, :-1], op0=ALU.mult, op1=ALU.add),
                    reads=["t2k1", "t2k2", "wconv"], writes=["t2k2"])
                S.op("dve", lambda e, pB=pB: e.tensor_tensor(out=acc[:], in0=pB[:, :], in1=acc[:], op=ALU.mult),
                     reads=[pB.name, "t2k2"], writes=["t2k2"])
                yb = self.stage()
                S.op("dve", lambda e, yb=yb: e.tensor_tensor(out=yb[:], in0=acc[:], in1=sz[:], op=ALU.mult),
                     reads=["t2k2", "t2k3"], writes=[yb.name])
                S.dma("sp", self.YT[ct * 128:(ct + 1) * 128, tb * 512:(tb + 1) * 512], yb[:],
                      reads=[yb.name], writes=["YT"])

    def roww_of_block(self, tb):
        tok = tb * 512
        for s in self.seqs:
            if s["t0"] <= tok < s["t0"] + s["T"]:
                assert 512 % s["roww"] == 0 and (tok - s["t0"]) % s["roww"] == 0
                return s["roww"]
        raise AssertionError

    def ml_proj(self, j):
        S, ins = self.S, self.ins
        W = ins["w_ml_in"][j]
        NTB = self.NT // 512
        NTT = self.NT // 128
        wi = 0
        ei = 0
        for g in range(8):
            wb = self.load_w(W, g * 512, 512, wi)
            wi += 1
            for ci in range(4):
                col = g * 512 + ci * 128
                dst = self.QT if col < 2048 else self.KTd
                r0 = col % 2048
                for tb in range(NTB):
                    p = self.psum()
                    self.proj_fm(wb, ci * 128, 128, tb, p)
                    sg = self.stage()
                    self.evac(ei, sg[:], p[:, :], [p.name], [sg.name], scale=(ML_DK ** -0.5) if col < 2048 else None)
                    ei += 1
                    S.dma("sp", dst[r0:r0 + 128, tb * 512:(tb + 1) * 512], sg[:], reads=[sg.name],
                          writes=[dst.name])
        for g in range(24):
            wb = self.load_w(W, 4096 + g * 512, 512, wi)
            wi += 1
            which = g // 8
            c0 = (g % 8) * 512
            for tt in range(NTT):
                p = self.psum()
                self.proj_tm(wb, tt, p)
                sg = self.stage()
                self.evac(ei, sg[:], p[:, :], [p.name], [sg.name])
                ei += 1
                S.dma("sp", self.VOZ[which, tt * 128:(tt + 1) * 128, c0:c0 + 512], sg[:], reads=[sg.name],
                      writes=["VOZ"])
        wb = self.load_w(W, 16384, 32, wi)
        S.dma("sp", self.small[0:8, 8:12], ins["b_mlg"][j], writes=["small_bg"])
        for typ in range(4):
            for tb in range(NTB):
                p = self.psum()
                self.proj_fm(wb, typ * 8, 8, tb, p)
                sg = self.stgf[(typ * NTB + tb) % 2]
                S.op("dve", lambda e, p=p, sg=sg, typ=typ: e.tensor_scalar(
                    out=sg[0:8, :], in0=p[0:8, :], scalar1=self.small[0:8, 8 + typ:9 + typ], scalar2=None,
                    op0=ALU.add), reads=[p.name, "small_bg"], writes=[sg.name])
                S.dma("sp", self.GATES[typ, :, tb * 512:(tb + 1) * 512], sg[0:8, :], reads=[sg.name],
                      writes=["GATES"])

    def ml_scan(self, j, sb):
        S, ins = self.S, self.ins
        A = {}
        A["gt"] = sb("gt", [128, 4, 128], F32)
        A["ee"] = sb("ee", [128, 2, 128], F32)
        A["nb"] = sb("nb", [128, 2, 128], F32)
        A["uu"] = sb("uu", [128, 2, 128], F32)
        A["ww"] = sb("ww", [128, 2, 128], F32)
        A["fl"] = sb("fl", [128, 2, 128], F32)
        A["cols"] = sb("cols", [128, 16], F32)
        A["rows"] = sb("rows", [1, 12, 128], F32)
        A["mh"] = sb("mh", [1, 2, 17, 8], F32)
        A["toksc"] = sb("toksc", [128, 4, 128], F32)
        A["decB"] = sb("decB", [128, 2, 128], F32)
        A["diag"] = sb("diag", [128, 128], F32)
        NS = 4
        AS = []
        for k in range(NS):
            B = dict(A)
            g = lambda n, shp, dt=F32, B=B, k=k: B.__setitem__(n, sb("%s_s%d" % (n, k), shp, dt))
            B["HSb"] = [sb("HSb%d_s%d" % (i, k), [128, 512], F32) for i in range(2)]
            B["sid"] = k
            B["Cst"] = [sb("Cst%d_s%d" % (d, k), [128, 2, 513], F32) for d in range(2)]
            g("Cb", [128, 2, 513], BF16)
            for n in ("qT", "kT"):
                B[n] = [sb("%s%d_s%d" % (n, i, k), [128, 2, 128], BF16) for i in range(2)]
            g("ktm", [128, 256], BF16)
            for n in ("v", "o", "z", "ytr"):
                B[n] = [sb("%s%d_s%d" % (n, i, k), [128, 512], BF16) for i in range(2)]
            g("vx", [128, 520], BF16)
            g("sm", [128, 128], BF16)
            g("hs", [128, 512])
            g("sig", [128, 512])
            g("sz", [128, 512])
            g("ybf", [128, 512], BF16)
            g("sc8", [128, 8])
            g("ghead", [128, 512])
            B["uc"] = 0
            AS.append(B)
        for s in self.seqs:
            self.ml_gates(j, s, A)

            def stream(k, s=s):
                for h in range(k, ML_H, NS):
                    S.dma("sp", AS[k]["ghead"][:], ins["g_ml_head"][j:j + 1, h * 512:(h + 1) * 512].partition_broadcast(128),
                          writes=[AS[k]["ghead"].name])
                    for d in (1, 0):
                        yield from self.ml_head_dir(j, s, h, d, AS[k])
            run_streams([stream(k) for k in range(NS)], S)

    def ml_gates(self, j, s, A):
        S, ins = self.S, self.ins
        T, t0 = s["T"], s["t0"]
        nch = T // 128
        R = nch * 8
        gt, ee, nb, uu, ww, fl, cols, rows, mh = (A[k] for k in ("gt", "ee", "nb", "uu", "ww", "fl", "cols", "rows", "mh"))
        for c in range(nch):
            for typ in range(4):
                S.dma("sp", gt[c * 8:(c + 1) * 8, typ, :], self.GATES[typ, :, t0 + c * 128: t0 + (c + 1) * 128],
                      reads=["GATES"], writes=["gt"])
        S.op("act", lambda e: e.activation(out=ee[0:R], in_=gt[0:R, 2:4, :], func=AF.Exp, scale=-1.0),
             reads=["gt"], writes=["ee"])
        S.op("dve", lambda e: e.tensor_scalar(out=ee[0:R], in0=ee[0:R], scalar1=1.0, scalar2=None, op0=ALU.add),
             reads=["ee"], writes=["ee"])
        S.op("act", lambda e: e.activation(out=ee[0:R], in_=ee[0:R], func=AF.Ln), reads=["ee"], writes=["ee"])
        S.op("dve", lambda e: e.tensor_tensor_scan(out=nb[0:R, 0, :], data0=self.ones[0:R, :], data1=ee[0:R, 0, :],
                                                   initial=0.0, op0=ALU.mult, op1=ALU.add),
             reads=["ee", "ones_sb"], writes=["nb"])
        S.op("dve", lambda e: e.tensor_tensor_scan(out=nb[0:R, 1, :], data0=self.ones[0:R, :], data1=ee[0:R, 1, :],
                                                   initial=0.0, op0=ALU.mult, op1=ALU.add),
             reads=["ee", "ones_sb"], writes=["nb"])
        S.op("dve", lambda e: e.tensor_copy(cols[0:R, 8:9], nb[0:R, 1, 127:128]), reads=["nb"], writes=["cols"])
        S.op("dve", lambda e: e.tensor_tensor(out=nb[0:R, 1, :], in0=ee[0:R, 1, :], in1=nb[0:R, 1, :],
                                              op=ALU.subtract), reads=["ee", "nb"], writes=["nb"])
        S.op("dve", lambda e: e.tensor_scalar(out=nb[0:R, 1, :], in0=nb[0:R, 1, :], scalar1=cols[0:R, 8:9],
                                              scalar2=None, op0=ALU.add), reads=["nb", "cols"], writes=["nb"])
        S.op("dve", lambda e: e.tensor_tensor(out=uu[0:R], in0=gt[0:R, 0:2, :], in1=nb[0:R], op=ALU.add),
             reads=["gt", "nb"], writes=["uu"])
        S.op("dve", lambda e: e.tensor_reduce(out=cols[0:R, 0:2], in_=uu[0:R], axis=AX.X, op=ALU.max),
             reads=["uu"], writes=["cols"])
        S.op("dve", lambda e: e.tensor_copy(cols[0:R, 2:3], nb[0:R, 0, 127:128]), reads=["nb"], writes=["cols"])
        S.op("dve", lambda e: e.tensor_copy(cols[0:R, 3:4], nb[0:R, 1, 0:1]), reads=["nb"], writes=["cols"])
        pm = self.pmisc
        for q in range(4):
            S.op("pe", lambda e, q=q: e.transpose(pm[0:1, q * 128:q * 128 + R], cols[0:R, q:q + 1],
                                                  self.ident[0:R, 0:R]), reads=["cols", "ident_sb"],
                 writes=["pmisc"])
        S.op("dve", lambda e: e.tensor_copy(rows[0:1, 0:4, 0:R], pm[0:1, :].rearrange("p (q r) -> p q r", q=4)[:, :, 0:R]),
             reads=["pmisc"], writes=["rows"])
        for d in range(2):
            if s["kind"] == "s":
                S.dma("sp", mh[0:1, d, 0, :], ins["mlm0"][j:j + 1, d, :], writes=["mh"])
            else:
                S.op("dve", lambda e, d=d: e.memset(mh[0:1, d, 0, :], 0.0), writes=["mh"])
            order = list(range(nch)) if d == 0 else list(range(nch - 1, -1, -1))
            for idx, c in enumerate(order):
                cs = slice(c * 8, (c + 1) * 8)
                S.op("dve", lambda e, d=d, idx=idx, cs=cs: e.tensor_tensor(
                    out=rows[0:1, 4 + d, cs], in0=mh[0:1, d, idx, :], in1=rows[0:1, d, cs], op=ALU.max),
                    reads=["mh", "rows"], writes=["rows"])
                S.op("dve", lambda e, d=d, idx=idx, cs=cs: e.tensor_tensor(
                    out=rows[0:1, 6 + d, cs], in0=mh[0:1, d, idx, :], in1=rows[0:1, 4 + d, cs], op=ALU.subtract),
                    reads=["mh", "rows"], writes=["rows"])
                S.op("dve", lambda e, d=d, idx=idx, cs=cs: e.tensor_tensor(
                    out=mh[0:1, d, idx + 1, :], in0=rows[0:1, 4 + d, cs], in1=rows[0:1, 2 + d, cs], op=ALU.subtract),
                    reads=["mh", "rows"], writes=["mh"])
            if s["kind"] == "p":
                S.dma("sp", self.st_m[s["pi"]:s["pi"] + 1, j, d, :], mh[0:1, d, nch, :], reads=["mh"],
                      writes=["st_m"])
        S.op("act", lambda e: e.activation(out=rows[0:1, 6:8, 0:R], in_=rows[0:1, 6:8, 0:R], func=AF.Exp),
             reads=["rows"], writes=["rows"])
        S.op("dve", lambda e: e.tensor_scalar(out=rows[0:1, 4:6, 0:R], in0=rows[0:1, 4:6, 0:R], scalar1=-1.0,
                                              scalar2=None, op0=ALU.mult), reads=["rows"], writes=["rows"])
        for q in range(4):
            S.op("pe", lambda e, q=q: e.transpose(pm[0:R, 256 + q:257 + q], rows[0:1, 4 + q, 0:R],
                                                  self.ident[0:1, 0:1]), reads=["rows", "ident_sb"],
                 writes=["pmisc"])
        S.op("dve", lambda e: e.tensor_copy(cols[0:R, 4:8], pm[0:R, 256:260]), reads=["pmisc"], writes=["cols"])
        for d in range(2):
            S.op("act", lambda e, d=d: e.activation(out=ww[0:R, d, :], in_=uu[0:R, d, :], func=AF.Exp,
                                                    bias=cols[0:R, 4 + d:5 + d], scale=1.0),
                 reads=["uu", "cols"], writes=["ww"])
            S.op("act", lambda e, d=d: e.activation(out=fl[0:R, d, :], in_=nb[0:R, d, :], func=AF.Exp,
                                                    bias=cols[0:R, 4 + d:5 + d], scale=1.0),
                 reads=["nb", "cols"], writes=["fl"])
        for q in range(4):
            src = ww if q < 2 else fl
            S.op("pe", lambda e, q=q, src=src: e.transpose(pm[:, q * 128:q * 128 + R], src[0:R, q % 2, :],
                                                           self.ident[0:R, 0:R]),
                 reads=[src.name, "ident_sb"], writes=["pmisc"])
        S.op("act", lambda e: e.copy(out=A["toksc"][:, :, 0:R],
                                     in_=pm[:, :].rearrange("p (q r) -> p q r", q=4)[:, :, 0:R]),
             reads=["pmisc"], writes=["toksc"])
        for d in range(2):
            S.op("dve", lambda e, d=d: e.tensor_scalar(out=A["diag"][0:R, 0:R], in0=self.ident[0:R, 0:R],
                                                       scalar1=cols[0:R, 6 + d:7 + d], scalar2=None, op0=ALU.mult),
                 reads=["ident_sb", "cols"], writes=["diag"])
            p = self.psum()
            S.op("pe", lambda e, p=p: e.matmul(p[:, 0:R], self.ones[0:R, :], A["diag"][0:R, 0:R], start=True,
                                               stop=True), reads=["ones_sb", "diag"], writes=[p.name])
            S.op("act", lambda e, p=p, d=d: e.copy(out=A["decB"][:, d, 0:R], in_=p[:, 0:R]), reads=[p.name],
                 writes=["decB"])

    def ml_head_dir(self, j, s, h, d, A):
        S, ins = self.S, self.ins
        T, t0 = s["T"], s["t0"]
        nch = T // 128
        Cst = A["Cst"][d]
        Cb = A["Cb"]
        if s["kind"] == "s":
            S.dma("sp", Cst[:, :, 0:512], ins["mlC0"][j, d, h].rearrange("(dt p) v -> p dt v", p=128),
                  writes=[Cst.name])
            S.dma("sp", Cst[:, :, 512], ins["mln0"][j, d, h].rearrange("(dt p) -> p dt", p=128),
                  writes=[Cst.name], slow=True)
        else:
            S.op("dve", lambda e: e.memset(Cst[:], 0.0), writes=[Cst.name])
        order = list(range(nch)) if d == 0 else list(range(nch - 1, -1, -1))
        mask = self.tril if d == 0 else self.triu
        for c in order:
            u = A["uc"]
            A["uc"] += 1
            tok0 = t0 + c * 128
            col = c * 8 + h
            qT, kT, v = A["qT"][u % 2], A["kT"][u % 2], A["v"][u % 2]
            ktm, vx, sm = A["ktm"], A["vx"], A["sm"]
            toksc, decB = A["toksc"], A["decB"]
            S.dma("sp", qT[:], self.QT[h * 256:(h + 1) * 256, tok0:tok0 + 128].rearrange("(dt p) t -> p dt t", p=128),
                  reads=["QT"], writes=[qT.name])
            S.dma("sp", kT[:], self.KTd[h * 256:(h + 1) * 256, tok0:tok0 + 128].rearrange("(dt p) t -> p dt t", p=128),
                  reads=["KTd"], writes=[kT.name])
            S.dma("sp", v[:], self.VOZ[0, tok0:tok0 + 128, h * 512:(h + 1) * 512], reads=["VOZ"], writes=[v.name])
            yield
            def trk(e, kT=kT):
                for dt_ in range(2):
                    r = e.transpose(self.ptb[:, dt_ * 128:(dt_ + 1) * 128], kT[:, dt_, :], self.identb[:])
                return r
            S.op("pe", trk, reads=[kT.name, "identb"], writes=["ptb"])
            S.op("act", lambda e: e.copy(out=ktm[:], in_=self.ptb[:, 0:256]), reads=["ptb"], writes=[A["ktm"].name])
            wcol = toksc[:, d, col:col + 1]
            S.op("dve", lambda e, v=v, wcol=wcol: e.tensor_scalar(out=vx[:, 0:512], in0=v[:], scalar1=wcol,
                                                                  scalar2=None, op0=ALU.mult),
                 reads=[v.name, "toksc"], writes=[A["vx"].name])
            S.op("dve", lambda e, wcol=wcol: e.tensor_copy(vx[:, 512:513], wcol), reads=["toksc"], writes=[A["vx"].name])
            yield
            S.op("dve", lambda e, col=col: e.tensor_scalar(out=Cst[:], in0=Cst[:], scalar1=decB[:, d, col:col + 1],
                                                           scalar2=None, op0=ALU.mult),
                 reads=[Cst.name, "decB"], writes=[Cst.name])
            S.op("act", lambda e: e.copy(out=Cb[:], in_=Cst[:]), reads=[Cst.name], writes=[A["Cb"].name])
            yield
            pS = self.psum()

            def mmS(e, pS=pS, kT=kT, qT=qT):
                for dt_ in range(2):
                    r = e.matmul(pS[:, 0:128], kT[:, dt_, :], qT[:, dt_, :], start=(dt_ == 0), stop=(dt_ == 1))
                return r
            S.op("pe", mmS, reads=[kT.name, qT.name], writes=[pS.name])
            S.op("dve", lambda e, pS=pS: e.tensor_tensor(out=sm[:], in0=pS[:, 0:128], in1=mask[:], op=ALU.mult),
                 reads=[pS.name, mask.name], writes=[A["sm"].name])
            yield
            pnum = self.psum()
            pden = self.psum()

            def mmN(e, pnum=pnum, pden=pden, qT=qT):
                e.matmul(pnum[:, :], sm[:], vx[:, 0:512], start=True, stop=False)
                e.matmul(pnum[:, :], qT[:, 0, :], Cb[:, 0, 0:512], start=False, stop=False)
                e.matmul(pnum[:, :], qT[:, 1, :], Cb[:, 1, 0:512], start=False, stop=True)
                e.matmul(pden[:, 0:1], sm[:], vx[:, 512:513], start=True, stop=False)
                e.matmul(pden[:, 0:1], qT[:, 0, :], Cb[:, 0, 512:513], start=False, stop=False)
                return e.matmul(pden[:, 0:1], qT[:, 1, :], Cb[:, 1, 512:513], start=False, stop=True)
            S.op("pe", mmN, reads=[A["sm"].name, A["vx"].name, qT.name, A["Cb"].name], writes=[pnum.name, pden.name])
            rr = A["sc8"][:, 4:5]
            S.op("act", lambda e, pden=pden: e.activation(out=rr, in_=pden[:, 0:1], func=AF.Abs),
                 reads=[pden.name], writes=[A["sc8"].name + "_rr"])
            S.op("dve", lambda e, col=col: e.tensor_tensor(out=rr, in0=rr, in1=toksc[:, 2 + d, col:col + 1],
                                                           op=ALU.max), reads=[A["sc8"].name + "_rr", "toksc"],
                 writes=[A["sc8"].name + "_rr"])
            S.op("dve", lambda e: e.reciprocal(out=rr, in_=rr), reads=[A["sc8"].name + "_rr"], writes=[A["sc8"].name + "_rr"])
            hsb = A["HSb"][u % 2]
            hsd = self.OSD[A["sid"], c * 128:(c + 1) * 128, :]
            hskey = "OSD%d_%d" % (A["sid"], c)
            if d == 1:
                S.op("dve", lambda e, pnum=pnum, hsb=hsb: e.tensor_scalar(out=hsb[:], in0=pnum[:, :], scalar1=rr,
                                                                          scalar2=None, op0=ALU.mult),
                     reads=[pnum.name, A["sc8"].name + "_rr"], writes=[hsb.name])
                S.defer_dma("sp", hsd, hsb[:], reads=[hsb.name], writes=[hskey])
            else:
                self.ml_finalize(j, h, c, tok0, pnum, rr, A, u)
            yield
            pD0, pD1 = self.psum(), self.psum()

            def mmD(e, pD0=pD0, pD1=pD1):
                e.matmul(pD0[:, :], ktm[:, 0:128], vx[:, 0:512], start=True, stop=True)
                e.matmul(pD1[:, :], ktm[:, 128:256], vx[:, 0:512], start=True, stop=True)
                e.matmul(self.pmisc[:, 300:301], ktm[:, 0:128], vx[:, 512:513], start=True, stop=True)
                return e.matmul(self.pmisc[:, 301:302], ktm[:, 128:256], vx[:, 512:513], start=True, stop=True)
            S.op("pe", mmD, reads=[A["ktm"].name, A["vx"].name], writes=[pD0.name, pD1.name, "pmisc"])
            S.op("dve", lambda e, pD0=pD0: e.tensor_tensor(out=Cst[:, 0, 0:512], in0=Cst[:, 0, 0:512], in1=pD0[:, :],
                                                           op=ALU.add), reads=[pD0.name, Cst.name], writes=[Cst.name])
            S.op("dve", lambda e, pD1=pD1: e.tensor_tensor(out=Cst[:, 1, 0:512], in0=Cst[:, 1, 0:512], in1=pD1[:, :],
                                                           op=ALU.add), reads=[pD1.name, Cst.name], writes=[Cst.name])
            S.op("dve", lambda e: e.tensor_tensor(out=Cst[:, :, 512], in0=Cst[:, :, 512], in1=self.pmisc[:, 300:302],
                                                  op=ALU.add), reads=["pmisc", Cst.name], writes=[Cst.name])
        if s["kind"] == "p":
            pi = s["pi"]
            S.dma("sp", self.st_C[pi, j, d, h].rearrange("(dt p) v -> p dt v", p=128), Cst[:, :, 0:512],
                  reads=[Cst.name], writes=["st_C"])
            S.dma("sp", self.st_n[pi, j, d, h].rearrange("(dt p) -> p dt", p=128), Cst[:, :, 512],
                  reads=[Cst.name], writes=["st_n"], slow=True)

    def ml_finalize(self, j, h, c, tok0, pnum, rr, A, u):
        S = self.S
        hs, sig, sz, ybf = A["hs"], A["sig"], A["sz"], A["ybf"]
        o, z = A["o"][u % 2], A["z"][u % 2]
        ytr = A["ytr"][u % 2]
        S.dma("sp", o[:], self.VOZ[1, tok0:tok0 + 128, h * 512:(h + 1) * 512], reads=["VOZ"], writes=[o.name])
        S.dma("sp", z[:], self.VOZ[2, tok0:tok0 + 128, h * 512:(h + 1) * 512], reads=["VOZ"], writes=[z.name])
        hsb = A["HSb"][u % 2]
        hskey = "OSD%d_%d" % (A["sid"], c)
        S.dma("sp", hsb[:], self.OSD[A["sid"], c * 128:(c + 1) * 128, :], reads=[hskey], writes=[hsb.name])
        S.op("dve", lambda e: e.scalar_tensor_tensor(out=hs[:], in0=pnum[:, :], scalar=rr, in1=hsb[:],
                                                     op0=ALU.mult, op1=ALU.add),
             reads=[pnum.name, A["sc8"].name + "_rr", hsb.name], writes=[A["hs"].name])
        ss = A["sc8"][:, 0:1]
        rs = A["sc8"][:, 1:2]
        skey = A["sc8"].name
        S.op("act", lambda e: e.activation(out=sig[:], in_=hs[:], func=AF.Square, accum_out=ss),
             reads=[A["hs"].name], writes=[A["sig"].name, skey])
        self.rstd_col(ss, rs, ML_DV, key=skey)
        S.op("dve", lambda e: e.scalar_tensor_tensor(out=hs[:], in0=hs[:], scalar=rs,
                                                     in1=A["ghead"][:, :], op0=ALU.mult,
                                                     op1=ALU.mult), reads=[A["hs"].name, skey, A["ghead"].name],
             writes=[A["hs"].name])
        S.op("act", lambda e: e.activation(out=sig[:], in_=o[:], func=AF.Sigmoid), reads=[o.name], writes=[A["sig"].name])
        S.op("act", lambda e: e.activation(out=sz[:], in_=z[:], func=AF.Silu), reads=[z.name], writes=[A["sz"].name])
        S.op("dve", lambda e: e.tensor_tensor(out=hs[:], in0=hs[:], in1=sig[:], op=ALU.mult),
             reads=[A["hs"].name, A["sig"].name], writes=[A["hs"].name])
        S.op("dve", lambda e: e.tensor_tensor(out=ybf[:], in0=hs[:], in1=sz[:], op=ALU.mult),
             reads=[A["hs"].name, A["sz"].name], writes=[A["ybf"].name])

        def tr(e):
            for i in range(4):
                r = e.transpose(self.ptb[:, 512 + i * 128:512 + (i + 1) * 128], ybf[:, i * 128:(i + 1) * 128],
                                self.identb[:])
            return r
        S.op("pe", tr, reads=[A["ybf"].name, "identb"], writes=["ptb"])
        S.op("act", lambda e: e.copy(out=ytr[:], in_=self.ptb[:, 512:1024]), reads=["ptb"], writes=[ytr.name])
        S.defer_dma("sp", self.YT[h * 512:(h + 1) * 512, tok0:tok0 + 128].rearrange("(i p) t -> p i t", p=128),
                    ytr[:].rearrange("p (i t) -> p i t", i=4), reads=[ytr.name], writes=["YT"])

    def gd_proj(self, sb):
        S, ins = self.S, self.ins
        W = ins["w_gd_in"]
        NTB = self.NT // 512
        NTT = self.NT // 128
        wconv = sb("gwconv", [128, 64, 3], F32)
        accs = [sb("gacc%d" % i, [128, 512], F32) for i in range(2)]
        sacts = [sb("gsact%d" % i, [128, 512], F32) for i in range(2)]
        sqs = [sb("gsq%d" % i, [128, 512], F32) for i in range(2)]
        rns = [sb("grn%d" % i, [128, 512], F32) for i in range(2)]
        sbfs = [sb("gsbf%d" % i, [128, 512], BF16) for i in range(2)]
        gi = 0
        S.dma("sp", wconv[:], ins["w_gd_conv"][:, :, :], writes=["gwconv"])
        wi = 0
        ei = 0
        for g in range(16):
            wb = self.load_w(W, g * 512, 512, wi)
            wi += 1
            for ci in range(4):
                ct = g * 4 + ci
                for tb in range(NTB):
                    roww = self.roww_of_block(tb)
                    acc, sact, sq, rn, sbf = accs[gi % 2], sacts[gi % 2], sqs[gi % 2], rns[gi % 2], sbfs[gi % 2]
                    gi += 1
                    p = self.psum()
                    self.proj_fm(wb, ci * 128, 128, tb, p)
                    w0, w1, w2 = (wconv[:, ct, i:i + 1] for i in range(3))
                    p3 = p[:, :].rearrange("p (r w) -> p r w", w=roww)
                    acc3 = acc[:].rearrange("p (r w) -> p r w", w=roww)
                    S.op("dve", lambda e, p=p, w1=w1, acc=acc: e.tensor_scalar(out=acc[:], in0=p[:, :], scalar1=w1, scalar2=None,
                                                                     op0=ALU.mult), reads=[p.name, "gwconv"],
                         writes=[acc.name])
                    S.op("dve", lambda e, p3=p3, acc3=acc3, w0=w0: e.scalar_tensor_tensor(
                        out=acc3[:, :, 1:], in0=p3[:, :, :-1], scalar=w0, in1=acc3[:, :, 1:], op0=ALU.mult,
                        op1=ALU.add), reads=[p.name, acc.name, "gwconv"], writes=[acc.name])
                    S.op("dve", lambda e, p3=p3, acc3=acc3, w2=w2: e.scalar_tensor_tensor(
                        out=acc3[:, :, :-1], in0=p3[:, :, 1:], scalar=w2, in1=acc3[:, :, :-1], op0=ALU.mult,
                        op1=ALU.add), reads=[p.name, acc.name, "gwconv"], writes=[acc.name])
                    if ct < 32:
                        S.op("act", lambda e, sact=sact, acc=acc: e.activation(out=sact[:], in_=acc[:], func=AF.Silu), reads=[acc.name],
                             writes=[sact.name])
                        S.op("act", lambda e, sq=sq, sact=sact: e.activation(out=sq[:], in_=sact[:], func=AF.Square), reads=[sact.name],
                             writes=[sq.name])
                        p2 = self.psum()
                        S.op("pe", lambda e, p2=p2, sq=sq: e.matmul(p2[:, :], self.ones[:, :], sq[:], start=True, stop=True),
                             reads=["ones_sb", sq.name], writes=[p2.name])
                        S.op("dve", lambda e, p2=p2, rn=rn: e.tensor_scalar(out=rn[:], in0=p2[:, :], scalar1=EPS, scalar2=None,
                                                                     op0=ALU.add), reads=[p2.name], writes=[rn.name])
                        S.op("act", lambda e, rn=rn: e.activation(out=rn[:], in_=rn[:], func=AF.Ln), reads=[rn.name],
                             writes=[rn.name])
                        S.op("act", lambda e, rn=rn: e.activation(out=rn[:], in_=rn[:], func=AF.Exp, scale=-0.5),
                             reads=[rn.name], writes=[rn.name])
                        sg = self.stage()
                        scl = (128 ** -0.5) if ct < 16 else 1.0
                        S.op("dve", lambda e, sg=sg, scl=scl, sact=sact, rn=rn: e.scalar_tensor_tensor(
                            out=sg[:], in0=sact[:], scalar=scl, in1=rn[:], op0=ALU.mult, op1=ALU.mult),
                            reads=[sact.name, rn.name], writes=[sg.name])
                        dst = self.GQT if ct < 16 else self.GKT
                        r0 = (ct % 16) * 128
                        S.dma("sp", dst[r0:r0 + 128, tb * 512:(tb + 1) * 512], sg[:], reads=[sg.name],
                              writes=[dst.name])
                    else:
                        hv = ct - 32
                        S.op("act", lambda e, sbf=sbf, acc=acc: e.activation(out=sbf[:], in_=acc[:], func=AF.Silu), reads=[acc.name],
                             writes=[sbf.name])

                        def tr(e, sbf=sbf):
                            for i in range(4):
                                r = e.transpose(self.ptb[:, i * 128:(i + 1) * 128], sbf[:, i * 128:(i + 1) * 128],
                                                self.identb[:])
                            return r
                        S.op("pe", tr, reads=[sbf.name, "identb"], writes=["ptb"])
                        sg = self.stage()
                        self.evac(ei, sg[:], self.ptb[:, 0:512], ["ptb"], [sg.name])
                        ei += 1
                        S.dma("sp", self.GV[tb * 512:(tb + 1) * 512, hv * 128:(hv + 1) * 128].rearrange(
                            "(a p) v -> p a v", p=128), sg[:].rearrange("p (a v) -> p a v", a=4), reads=[sg.name],
                            writes=["GV"])
        for g in range(8):
            wb = self.load_w(W, 8192 + g * 512, 512, wi)
            wi += 1
            for tt in range(NTT):
                p = self.psum()
                self.proj_tm(wb, tt, p)
                sg = self.stage()
                self.evac(ei, sg[:], p[:, :], [p.name], [sg.name])
                ei += 1
                S.dma("sp", self.GZ[tt * 128:(tt + 1) * 128, g * 512:(g + 1) * 512], sg[:], reads=[sg.name],
                      writes=["GZ"])
        wb = self.load_w(W, 12288, 128, wi)
        for typ in range(4):
            for tb in range(NTB):
                p = self.psum()
                self.proj_fm(wb, typ * 32, 32, tb, p)
                sg = self.stgf[(typ * NTB + tb) % 2]
                S.op("act", lambda e, p=p, sg=sg: e.copy(out=sg[0:32, :], in_=p[0:32, :]), reads=[p.name],
                     writes=[sg.name])
                S.dma("sp", self.GAB[typ, :, tb * 512:(tb + 1) * 512], sg[0:32, :], reads=[sg.name],
                      writes=["GAB"])

    def gd_scan(self, sb):
        S, ins = self.S, self.ins
        A = {}
        nrt = max(1, self.Tmax // 512)
        f = lambda n, shp, dt=F32: A.__setitem__(n, sb("g_" + n, shp, dt))
        f("ga", [128, 4, 128])
        f("sp", [128, 2, 128])
        f("ng", [128, 2, 128])
        f("nG", [128, 2, 128])
        f("bt", [128, 2, 128])
        f("kd", [128, 2, 128])
        f("gc", [128, 16])
        f("par", [128, 4])
        f("toksc", [128, nrt * 2 * 5, 128])
        f("decB", [128, nrt * 2, 128])
        f("diag", [128, 128])
        f("gnB", [128, 128])
        NS = 3
        AS = []
        for k in range(NS):
            B = dict(A)
            g = lambda n, shp, dt=F32, B=B, k=k: B.__setitem__(n, sb("g_%s_s%d" % (n, k), shp, dt))
            B["OSb"] = [sb("g_OSb%d_s%d" % (i, k), [128, 512], F32) for i in range(2)]
            B["sid"] = k
            B["pp"] = self.pp[2 * k:2 * k + 2]
            B["ppi"] = 0
            g("Sst", [128, 4, 128])
            g("Sb", [128, 4, 128], BF16)
            g("diagG", [128, 4, 128])
            g("dd", [128, 4, 128])
            g("dec", [128, 4, 128])
            g("t1", [128, 4, 128])
            g("ktm", [128, 2, 128], BF16)
            for n in ("P", "PT", "N", "kb", "vb", "WTn", "U", "kdk", "y"):
                g(n, [128, 4, 128], BF16)
            for n in ("Q", "Tm", "TTm", "kT", "qT", "v", "z", "ytr", "Lb", "LTb", "Yb", "Y2b"):
                B[n] = [sb("g_%s%d_s%d" % (n, i, k), [128, 2, 128] if n in ("kT", "qT") else [128, 4, 128], BF16)
                        for i in range(2)]
            g("o", [128, 4, 128])
            g("sq", [128, 4, 128])
            g("ssq", [128, 8])
            B["uc"] = 0
            AS.append(B)
        S.dma("sp", A["gnB"][:], ins["g_gd_norm"][0:1, :].partition_broadcast(128), writes=["g_gnB"])
        S.dma("sp", A["par"][:], ins["gd_par"][:, :], writes=["g_par"])
        S.op("act", lambda e: e.activation(out=A["par"][:, 0:2], in_=A["par"][:, 0:2], func=AF.Exp),
             reads=["g_par"], writes=["g_par"])
        for s in self.seqs:
            self.gd_gates(s, A)

            def stream(k, s=s):
                for hg in range(k, 8, NS):
                    for d in (1, 0):
                        yield from self.gd_group_dir(s, hg, d, AS[k])
            run_streams([stream(k) for k in range(NS)], S)

    def gd_gates(self, s, A):
        S, ins = self.S, self.ins
        T, t0 = s["T"], s["t0"]
        nch = T // 128
        ga, sp, ng, nG, bt, kd, gc, par, toksc, decB = (A[k] for k in (
            "ga", "sp", "ng", "nG", "bt", "kd", "gc", "par", "toksc", "decB"))
        pm = self.pmisc
        for rt in range((nch + 3) // 4):
            ncl = min(4, nch - rt * 4)
            R = ncl * 32
            for cl in range(ncl):
                c = rt * 4 + cl
                for typ in range(4):
                    S.dma("sp", ga[cl * 32:(cl + 1) * 32, typ, :], self.GAB[typ, :, t0 + c * 128:t0 + (c + 1) * 128],
                          reads=["GAB"], writes=["g_ga"])
            for d in range(2):
                S.op("act", lambda e, d=d: e.activation(out=sp[0:R, d, :], in_=ga[0:R, d, :], func=AF.Exp,
                                                        bias=par[0:R, 2 + d:3 + d], scale=1.0),
                     reads=["g_ga", "g_par"], writes=["g_sp"])
            S.op("dve", lambda e: e.tensor_scalar(out=sp[0:R], in0=sp[0:R], scalar1=1.0, scalar2=None, op0=ALU.add),
                 reads=["g_sp"], writes=["g_sp"])
            S.op("act", lambda e: e.activation(out=sp[0:R], in_=sp[0:R], func=AF.Ln), reads=["g_sp"],
                 writes=["g_sp"])
            for d in range(2):
                S.op("dve", lambda e, d=d: e.tensor_scalar(out=ng[0:R, d, :], in0=sp[0:R, d, :],
                                                           scalar1=par[0:R, d:d + 1], scalar2=None, op0=ALU.mult),
                     reads=["g_sp", "g_par"], writes=["g_ng"])
            S.op("act", lambda e: e.activation(out=bt[0:R], in_=ga[0:R, 2:4, :], func=AF.Sigmoid),
                 reads=["g_ga"], writes=["g_bt"])
            for d in range(2):
                S.op("dve", lambda e, d=d: e.tensor_tensor_scan(out=nG[0:R, d, :], data0=self.ones[0:R, :],
                                                                data1=ng[0:R, d, :], initial=0.0, op0=ALU.mult,
                                                                op1=ALU.add),
                     reads=["g_ng", "ones_sb"], writes=["g_nG"])
            S.op("dve", lambda e: e.tensor_copy(gc[0:R, 8:9], nG[0:R, 1, 127:128]), reads=["g_nG"], writes=["g_gc"])
            S.op("dve", lambda e: e.tensor_tensor(out=nG[0:R, 1, :], in0=ng[0:R, 1, :], in1=nG[0:R, 1, :],
                                                  op=ALU.subtract), reads=["g_ng", "g_nG"], writes=["g_nG"])
            S.op("dve", lambda e: e.tensor_scalar(out=nG[0:R, 1, :], in0=nG[0:R, 1, :], scalar1=gc[0:R, 8:9],
                                                  scalar2=None, op0=ALU.add), reads=["g_nG", "g_gc"],
                 writes=["g_nG"])
            S.op("dve", lambda e: e.tensor_copy(gc[0:R, 0:1], nG[0:R, 0, 127:128]), reads=["g_nG"], writes=["g_gc"])
            S.op("dve", lambda e: e.tensor_copy(gc[0:R, 1:2], nG[0:R, 1, 0:1]), reads=["g_nG"], writes=["g_gc"])
            S.op("act", lambda e: e.activation(out=gc[0:R, 4:6], in_=gc[0:R, 0:2], func=AF.Exp, scale=-1.0),
                 reads=["g_gc"], writes=["g_gc"])
            S.op("dve", lambda e: e.tensor_scalar(out=gc[0:R, 2:4], in0=gc[0:R, 0:2], scalar1=-1.0, scalar2=None,
                                                  op0=ALU.mult), reads=["g_gc"], writes=["g_gc"])
            for d in range(2):
                S.op("act", lambda e, d=d: e.activation(out=kd[0:R, d, :], in_=nG[0:R, d, :], func=AF.Exp,
                                                        bias=gc[0:R, 2 + d:3 + d], scale=1.0),
                     reads=["g_nG", "g_gc"], writes=["g_kd"])
            for d in range(2):
                base = (rt * 2 + d) * 5
                for q, src in enumerate((nG, bt, kd)):
                    S.op("pe", lambda e, q=q, src=src, d=d: e.transpose(pm[:, q * 128:q * 128 + R], src[0:R, d, :],
                                                                        self.ident[0:R, 0:R]),
                         reads=[src.name, "ident_sb"], writes=["pmisc"])
                S.op("act", lambda e, base=base: e.copy(out=toksc[:, base:base + 3, 0:R],
                                                        in_=pm[:, 0:384].rearrange("p (q r) -> p q r", q=3)[:, :, 0:R]),
                     reads=["pmisc"], writes=["g_toksc"])
                S.op("act", lambda e, base=base: e.activation(out=toksc[:, base + 3, 0:R], in_=toksc[:, base, 0:R],
                                                              func=AF.Exp, scale=-1.0), reads=["g_toksc"],
                     writes=["g_toksc"])
                S.op("dve", lambda e, base=base: e.tensor_tensor(out=toksc[:, base + 4, 0:R], in0=toksc[:, base + 1, 0:R],
                                                                 in1=toksc[:, base + 3, 0:R], op=ALU.mult),
                     reads=["g_toksc"], writes=["g_toksc"])
                S.op("dve", lambda e, d=d: e.tensor_scalar(out=A["diag"][0:R, 0:R], in0=self.ident[0:R, 0:R],
                                                           scalar1=gc[0:R, 4 + d:5 + d], scalar2=None, op0=ALU.mult),
                     reads=["ident_sb", "g_gc"], writes=["g_diag"])
                p = self.psum()
                S.op("pe", lambda e, p=p: e.matmul(p[:, 0:R], self.ones[0:R, :], A["diag"][0:R, 0:R], start=True,
                                                   stop=True), reads=["ones_sb", "g_diag"], writes=[p.name])
                S.op("act", lambda e, p=p, rt=rt, d=d: e.copy(out=decB[:, rt * 2 + d, 0:R], in_=p[:, 0:R]),
                     reads=[p.name], writes=["g_decB"])

    def gd_group_dir(self, s, hg, d, A):
        S, ins = self.S, self.ins
        T, t0 = s["T"], s["t0"]
        nch = T // 128
        Sst, Sb = A["Sst"], A["Sb"]
        toksc, decB = A["toksc"], A["decB"]
        hv0 = hg * 4
        b4 = lambda ap: ap.unsqueeze(2).to_broadcast([128, 4, 128])
        m4 = lambda ap: ap.unsqueeze(1).to_broadcast([128, 4, 128])
        if s["kind"] == "s":
            S.dma("sp", Sst[:], ins["gdS0"][d, hv0:hv0 + 4].rearrange("i k v -> k i v"), writes=[A["Sst"].name])
        else:
            S.op("dve", lambda e: e.memset(Sst[:], 0.0), writes=[A["Sst"].name])
        S.op("act", lambda e: e.copy(out=Sb[:], in_=Sst[:]), reads=[A["Sst"].name], writes=[A["Sb"].name])
        order = list(range(nch)) if d == 0 else list(range(nch - 1, -1, -1))
        mincl = self.triu if d == 0 else self.tril
        mstr = self.triuS if d == 0 else self.trilS
        for c in order:
            u = A["uc"]
            A["uc"] += 1
            tok0 = t0 + c * 128
            rt, cl = c // 4, c % 4
            base = (rt * 2 + d) * 5
            r0 = cl * 32 + hv0
            col = lambda q: toksc[:, base + q, r0:r0 + 4]
            kT, qT, v = A["kT"][u % 2], A["qT"][u % 2], A["v"][u % 2]
            S.dma("sp", kT[:], self.GKT[hg * 256:(hg + 1) * 256, tok0:tok0 + 128].rearrange("(a p) t -> p a t", p=128),
                  reads=["GKT"], writes=[kT.name])
            S.dma("sp", qT[:], self.GQT[hg * 256:(hg + 1) * 256, tok0:tok0 + 128].rearrange("(a p) t -> p a t", p=128),
                  reads=["GQT"], writes=[qT.name])
            S.dma("sp", v[:], self.GV[tok0:tok0 + 128, hv0 * 128:(hv0 + 4) * 128].rearrange("t (i v) -> t i v", i=4),
                  reads=["GV"], writes=[v.name])
            ktm = A["ktm"]

            def trk(e, kT=kT):
                for a in range(2):
                    r = e.transpose(self.ptb[:, a * 128:(a + 1) * 128], kT[:, a, :], self.identb[:])
                return r
            S.op("pe", trk, reads=[kT.name, "identb"], writes=["ptb"])
            S.op("act", lambda e: e.copy(out=ktm[:], in_=self.ptb[:, 0:256].rearrange("p (a k) -> p a k", a=2)),
                 reads=["ptb"], writes=[A["ktm"].name])
            yield
            pK = self.psum_s(A)
            pQ = pK

            def mmG(e, pK=pK, kT=kT, qT=qT):
                for a in range(2):
                    e.matmul(pK[:, a * 128:(a + 1) * 128], kT[:, a, :], kT[:, a, :], start=True, stop=True)
                for a in range(2):
                    r = e.matmul(pK[:, 256 + a * 128:256 + (a + 1) * 128], qT[:, a, :], kT[:, a, :], start=True,
                                 stop=True)
                return r
            S.op("pe", mmG, reads=[kT.name, qT.name], writes=[pK.name])
            yield
            diagG, dd, dec, t1 = A["diagG"], A["dd"], A["dec"], A["t1"]
            S.op("dve", lambda e: e.tensor_tensor(out=diagG[:], in0=m4(self.ident[:, :]), in1=b4(col(0)), op=ALU.mult),
                 reads=["ident_sb", "g_toksc"], writes=[A["diagG"].name])
            pG = self.psum_s(A)
            S.op("pe", lambda e, pG=pG: e.matmul(pG[:, :], self.ones[:, :], diagG[:].rearrange("p i s -> p (i s)"),
                                                 start=True, stop=True), reads=["ones_sb", A["diagG"].name],
                 writes=[pG.name])
            pG3 = pG[:, :].rearrange("p (i s) -> p i s", i=4)
            S.op("dve", lambda e, pG3=pG3: e.tensor_tensor(out=dd[:], in0=pG3, in1=b4(col(0)), op=ALU.subtract),
                 reads=[pG.name, "g_toksc"], writes=[A["dd"].name])
            S.op("dve", lambda e: e.tensor_scalar(out=dd[:], in0=dd[:], scalar1=0.0, scalar2=None, op0=ALU.min),
                 reads=[A["dd"].name], writes=[A["dd"].name])
            S.op("act", lambda e: e.activation(out=dec[:], in_=dd[:], func=AF.Exp), reads=[A["dd"].name],
                 writes=[A["dec"].name])
            yield
            P, PT, N = A["P"], A["PT"], A["N"]
            pQ4 = pQ[:, 256:512].rearrange("p (a s) -> p a s", a=2).unsqueeze(2).to_broadcast([128, 2, 2, 128])
            pK4 = pK[:, 0:256].rearrange("p (a s) -> p a s", a=2).unsqueeze(2).to_broadcast([128, 2, 2, 128])
            v4 = lambda t: t[:].rearrange("p (a b) s -> p a b s", a=2)
            S.op("dve", lambda e, pQ4=pQ4: e.tensor_tensor(out=v4(t1), in0=pQ4, in1=v4(dec), op=ALU.mult),
                 reads=[pQ.name, A["dec"].name], writes=[A["t1"].name])
            S.op("pool", lambda e: e.tensor_tensor(out=P[:], in0=t1[:], in1=m4(mincl[:, :]), op=ALU.mult),
                 reads=[A["t1"].name, mincl.name], writes=[A["P"].name])
            S.op("dve", lambda e, pK4=pK4: e.tensor_tensor(out=v4(t1), in0=pK4, in1=v4(dec), op=ALU.mult),
                 reads=[pK.name, A["dec"].name], writes=[A["t1"].name])
            S.op("dve", lambda e: e.tensor_tensor(out=t1[:], in0=t1[:], in1=m4(mstr[:, :]), op=ALU.mult),
                 reads=[A["t1"].name, mstr.name], writes=[A["t1"].name])
            S.op("dve", lambda e: e.scalar_tensor_tensor(out=N[:], in0=t1[:], scalar=-1.0, in1=b4(col(1)),
                                                         op0=ALU.mult, op1=ALU.mult),
                 reads=[A["t1"].name, "g_toksc"], writes=[A["N"].name])
            yield
            Qs = A["Q"]

            def trN(e):
                for i in range(4):
                    e.transpose(self.ptb[:, i * 128:(i + 1) * 128], N[:, i, :], self.identb[:])
                for i in range(4):
                    r = e.transpose(self.ptb[:, 512 + i * 128:512 + (i + 1) * 128], P[:, i, :], self.identb[:])
                return r
            S.op("pe", trN, reads=[A["N"].name, A["P"].name, "identb"], writes=["ptb"])
            ptN = self.ptb[:, 0:512].rearrange("p (i s) -> p i s", i=4)
            ptP = self.ptb[:, 512:1024].rearrange("p (i s) -> p i s", i=4)
            Q = Qs[u % 2]
            S.op("act", lambda e: e.copy(out=Q[:], in_=ptN), reads=["ptb"], writes=[Q.name])
            S.op("act", lambda e: e.copy(out=PT[:], in_=ptP), reads=["ptb"], writes=[A["PT"].name])
            import os as _os
            stop = _os.environ.get("GD_STOP", "")
            if stop == "g1":
                continue
            Tm, TTm = A["Tm"], A["TTm"]
            T, TT = Tm[0], TTm[0]
            for lev in range(7):
                mL = self.lvl[:, lev, :] if d == 0 else self.lvlT[:, lev, :]
                mLT = self.lvlT[:, lev, :] if d == 0 else self.lvl[:, lev, :]
                Tn, TTn = Tm[(lev + 1) % 2], TTm[(lev + 1) % 2]
                Lb, LTb, Yb, Y2b = A["Lb"][lev % 2], A["LTb"][lev % 2], A["Yb"][lev % 2], A["Y2b"][lev % 2]
                last = lev == 6
                if not last:
                    S.op("pool", lambda e, mL=mL: e.tensor_tensor(out=Lb[:], in0=N[:], in1=m4(mL), op=ALU.mult),
                         reads=[A["N"].name, "lvl"], writes=[Lb.name])
                S.op("pool", lambda e, mLT=mLT: e.tensor_tensor(out=LTb[:], in0=Q[:], in1=m4(mLT), op=ALU.mult),
                     reads=[Q.name, "lvl"], writes=[LTb.name])
                if lev == 0:
                    S.op("dve", lambda e, Tn=Tn: e.tensor_tensor(out=Tn[:], in0=Lb[:], in1=m4(self.ident[:, :]),
                                                                 op=ALU.add),
                         reads=[Lb.name, "ident_sb"], writes=[Tn.name])
                    S.op("dve", lambda e, TTn=TTn: e.tensor_tensor(out=TTn[:], in0=LTb[:], in1=m4(self.ident[:, :]),
                                                                   op=ALU.add),
                         reads=[LTb.name, "ident_sb"], writes=[TTn.name])
                    T, TT = Tn, TTn
                    continue
                yield
                py, py2 = self.psum_s(A), self.psum_s(A)

                def mmy(e, py=py, py2=py2, T=T, TT=TT, last=last):
                    for i in range(4):
                        r = e.matmul(py[:, i * 128:(i + 1) * 128], LTb[:, i, :], T[:, i, :], start=True, stop=True)
                    if not last:
                        for i in range(4):
                            r = e.matmul(py2[:, i * 128:(i + 1) * 128], Lb[:, i, :], TT[:, i, :], start=True,
                                         stop=True)
                    return r
                S.op("pe", mmy, reads=[Lb.name, LTb.name, T.name, TT.name], writes=[py.name, py2.name])
                S.op("act", lambda e, py=py: e.copy(out=Yb[:].rearrange("p i s -> p (i s)"), in_=py[:, :]),
                     reads=[py.name], writes=[Yb.name])
                if not last:
                    S.op("act", lambda e, py2=py2: e.copy(out=Y2b[:].rearrange("p i s -> p (i s)"), in_=py2[:, :]),
                         reads=[py2.name], writes=[Y2b.name])
                yield
                px, pxt = self.psum_s(A), self.psum_s(A)

                def mmx(e, px=px, pxt=pxt, T=T, TT=TT, last=last):
                    if not last:
                        for i in range(4):
                            e.matmul(px[:, i * 128:(i + 1) * 128], Y2b[:, i, :], T[:, i, :], start=True, stop=True)
                    for i in range(4):
                        r = e.matmul(pxt[:, i * 128:(i + 1) * 128], Yb[:, i, :], TT[:, i, :], start=True, stop=True)
                    return r
                S.op("pe", mmx, reads=[Yb.name, Y2b.name, T.name, TT.name], writes=[px.name, pxt.name])
                if not last:
                    S.op("dve", lambda e, px=px, T=T, Tn=Tn: e.tensor_tensor(
                        out=Tn[:].rearrange("p i s -> p (i s)"), in0=px[:, :], in1=T[:].rearrange("p i s -> p (i s)"),
                        op=ALU.add), reads=[px.name, T.name], writes=[Tn.name])
                S.op("dve", lambda e, pxt=pxt, TT=TT, TTn=TTn: e.tensor_tensor(
                    out=TTn[:].rearrange("p i s -> p (i s)"), in0=pxt[:, :], in1=TT[:].rearrange("p i s -> p (i s)"),
                    op=ALU.add), reads=[pxt.name, TT.name], writes=[TTn.name])
                T, TT = Tn, TTn
            R = TT
            if stop == "g2":
                continue
            yield
            kb, vb, WTn, U, kdk = A["kb"], A["vb"], A["WTn"], A["U"], A["kdk"]
            k4 = ktm[:].unsqueeze(2).to_broadcast([128, 2, 2, 128])
            S.op("pool", lambda e: e.tensor_tensor(out=v4(kb), in0=k4,
                                                  in1=col(4).rearrange("p (a b) -> p a b", a=2).unsqueeze(3).to_broadcast(
                                                      [128, 2, 2, 128]), op=ALU.mult),
                 reads=[A["ktm"].name, "g_toksc"], writes=[A["kb"].name])
            S.op("pool", lambda e, v=v: e.tensor_tensor(out=vb[:], in0=v[:], in1=b4(col(1)), op=ALU.mult),
                 reads=[v.name, "g_toksc"], writes=[A["vb"].name])
            S.op("pool", lambda e: e.tensor_tensor(out=v4(kdk), in0=k4,
                                                  in1=col(2).rearrange("p (a b) -> p a b", a=2).unsqueeze(3).to_broadcast(
                                                      [128, 2, 2, 128]), op=ALU.mult),
                 reads=[A["ktm"].name, "g_toksc"], writes=[A["kdk"].name])
            pW = self.psum_s(A)

            def mmW(e, pW=pW, R=R):
                for i in range(4):
                    r = e.matmul(pW[:, i * 128:(i + 1) * 128], kb[:, i, :], R[:, i, :], start=True, stop=True)
                return r
            S.op("pe", mmW, reads=[A["kb"].name, R.name], writes=[pW.name])
            S.op("act", lambda e, pW=pW: e.activation(out=WTn[:].rearrange("p i s -> p (i s)"), in_=pW[:, :],
                                                      func=AF.Copy, scale=-1.0), reads=[pW.name], writes=[A["WTn"].name])
            yield
            pU = self.psum_s(A)

            def mmU(e, pU=pU, R=R):
                for i in range(4):
                    e.matmul(pU[:, i * 128:(i + 1) * 128], R[:, i, :], vb[:, i, :], start=True, stop=False)
                    r = e.matmul(pU[:, i * 128:(i + 1) * 128], WTn[:, i, :], Sb[:, i, :], start=False, stop=True)
                return r
            S.op("pe", mmU, reads=[R.name, A["vb"].name, A["WTn"].name, A["Sb"].name], writes=[pU.name])
            S.op("act", lambda e, pU=pU: e.copy(out=U[:].rearrange("p i s -> p (i s)"), in_=pU[:, :]),
                 reads=[pU.name], writes=[A["U"].name])
            yield
            pO1, pO2 = self.psum_s(A), self.psum_s(A)

            def mmO(e, pO1=pO1, pO2=pO2, qT=qT):
                for i in range(4):
                    e.matmul(pO1[:, i * 128:(i + 1) * 128], qT[:, i // 2, :], Sb[:, i, :], start=True, stop=True)
                for i in range(4):
                    r = e.matmul(pO2[:, i * 128:(i + 1) * 128], PT[:, i, :], U[:, i, :], start=True, stop=True)
                return r
            S.op("pe", mmO, reads=[qT.name, A["Sb"].name, A["PT"].name, A["U"].name], writes=[pO1.name, pO2.name])
            o = A["o"]
            o3 = lambda p: p[:, :].rearrange("p (i s) -> p i s", i=4)
            S.op("dve", lambda e, pO1=pO1: e.tensor_tensor(out=o[:], in0=o3(pO1), in1=b4(col(3)), op=ALU.mult),
                 reads=[pO1.name, "g_toksc"], writes=[A["o"].name])
            osb = A["OSb"][u % 2]
            OSc = osb[:].rearrange("p (i s) -> p i s", i=4)
            osd = self.OSD[A["sid"], c * 128:(c + 1) * 128, :]
            oskey = "OSD%d_%d" % (A["sid"], c)
            if d == 1:
                S.op("dve", lambda e, pO2=pO2, OSc=OSc: e.tensor_tensor(out=OSc, in0=o3(pO2), in1=o[:], op=ALU.add),
                     reads=[pO2.name, A["o"].name], writes=[osb.name])
                S.defer_dma("sp", osd, osb[:], reads=[osb.name], writes=[oskey])
            else:
                S.dma("sp", osb[:], osd, reads=[oskey], writes=[osb.name])
                S.op("dve", lambda e, pO2=pO2: e.tensor_tensor(out=o[:], in0=o3(pO2), in1=o[:], op=ALU.add),
                     reads=[pO2.name, A["o"].name], writes=[A["o"].name])
                S.op("dve", lambda e, OSc=OSc: e.tensor_tensor(out=o[:], in0=o[:], in1=OSc, op=ALU.add),
                     reads=[A["o"].name, osb.name], writes=[A["o"].name])
                self.gd_finalize(hv0, tok0, A, u)
            yield
            pS = self.psum_s(A)

            def mmS(e, pS=pS):
                for i in range(4):
                    r = e.matmul(pS[:, i * 128:(i + 1) * 128], kdk[:, i, :], U[:, i, :], start=True, stop=True)
                return r
            S.op("pe", mmS, reads=[A["kdk"].name, A["U"].name], writes=[pS.name])
            S.op("dve", lambda e, rt=rt, r0=r0: e.tensor_tensor(out=Sst[:], in0=Sst[:],
                                                                in1=b4(decB[:, rt * 2 + d, r0:r0 + 4]), op=ALU.mult),
                 reads=[A["Sst"].name, "g_decB"], writes=[A["Sst"].name])
            S.op("dve", lambda e, pS=pS: e.tensor_tensor(out=Sst[:], in0=Sst[:], in1=o3(pS), op=ALU.add),
                 reads=[A["Sst"].name, pS.name], writes=[A["Sst"].name])
            S.op("act", lambda e: e.copy(out=Sb[:], in_=Sst[:]), reads=[A["Sst"].name], writes=[A["Sb"].name])
        if s["kind"] == "p":
            S.dma("sp", self.st_S[s["pi"], 0, d, hv0:hv0 + 4].rearrange("i k v -> k i v"), Sst[:],
                  reads=[A["Sst"].name], writes=["st_S"])

    def gd_finalize(self, hv0, tok0, A, u):
        S = self.S
        o, sq, ssq, y = A["o"], A["sq"], A["ssq"], A["y"]
        z, ytr = A["z"][u % 2], A["ytr"][u % 2]
        b4 = lambda ap: ap.unsqueeze(2).to_broadcast([128, 4, 128])
        m4 = lambda ap: ap.unsqueeze(1).to_broadcast([128, 4, 128])
        S.dma("sp", z[:], self.GZ[tok0:tok0 + 128, hv0 * 128:(hv0 + 4) * 128].rearrange("t (i v) -> t i v", i=4),
              reads=["GZ"], writes=[z.name])
        S.op("act", lambda e: e.activation(out=sq[:], in_=o[:], func=AF.Square), reads=[A["o"].name], writes=[A["sq"].name])
        S.op("dve", lambda e: e.tensor_reduce(out=ssq[:, 0:4], in_=sq[:], axis=AX.X, op=ALU.add), reads=[A["sq"].name],
             writes=[A["ssq"].name])
        S.op("dve", lambda e: e.tensor_scalar(out=ssq[:, 0:4], in0=ssq[:, 0:4], scalar1=1.0 / 128, scalar2=EPS,
                                              op0=ALU.mult, op1=ALU.add), reads=[A["ssq"].name], writes=[A["ssq"].name])
        S.op("act", lambda e: e.activation(out=ssq[:, 0:4], in_=ssq[:, 0:4], func=AF.Sqrt), reads=[A["ssq"].name],
             writes=[A["ssq"].name])
        S.op("dve", lambda e: e.reciprocal(out=ssq[:, 0:4], in_=ssq[:, 0:4]), reads=[A["ssq"].name], writes=[A["ssq"].name])
        S.op("dve", lambda e: e.tensor_tensor(out=o[:], in0=o[:], in1=b4(ssq[:, 0:4]), op=ALU.mult),
             reads=[A["o"].name, A["ssq"].name], writes=[A["o"].name])
        S.op("dve", lambda e: e.tensor_tensor(out=o[:], in0=o[:], in1=m4(A["gnB"][:, :]), op=ALU.mult),
             reads=[A["o"].name, "g_gnB"], writes=[A["o"].name])
        S.op("act", lambda e, z=z: e.activation(out=sq[:], in_=z[:], func=AF.Silu), reads=[z.name], writes=[A["sq"].name])
        S.op("dve", lambda e: e.tensor_tensor(out=y[:], in0=o[:], in1=sq[:], op=ALU.mult), reads=[A["o"].name, A["sq"].name],
             writes=[A["y"].name])

        def tr(e):
            for i in range(4):
                r = e.transpose(self.ptb[:, i * 128:(i + 1) * 128], y[:, i, :], self.identb[:])
            return r
        S.op("pe", tr, reads=[A["y"].name, "identb"], writes=["ptb"])
        S.op("act", lambda e: e.copy(out=ytr[:].rearrange("p i s -> p (i s)"), in_=self.ptb[:, 0:512]),
             reads=["ptb"], writes=[ytr.name])
        S.defer_dma("sp", self.YT[hv0 * 128:(hv0 + 4) * 128, tok0:tok0 + 128].rearrange("(i p) t -> p i t", p=128),
                    ytr[:], reads=[ytr.name], writes=["YT"])

    def phase_d(self, l, wout, sb):
        S = self.S
        NTT = self.NT // 128
        gateb = sb("gateb", [128, D], F32)
        wd = sb("wd", [128, 32, D], BF16)
        ytile = [sb("ytile%d" % i, [128, 32, 128], BF16) for i in range(3)]
        xbs = [sb("xb%d" % i, [128, D], F32) for i in range(2)]
        tmp = [sb("tmpd%d" % i, [128, 512], F32) for i in range(2)]
        src = wout.rearrange("(kt p) c -> p kt c", p=128)
        for q in range(8):
            S.dma("pool", wd[:, q * 4:(q + 1) * 4, :], src[:, q * 4:(q + 1) * 4, :], writes=["wd_%d" % q])
        wkeys = ["wd_%d" % q for q in range(8)]
        cur = None
        k = 0
        for tt in range(NTT):
            c = self.tile_cond(tt)
            if c != cur:
                cur = c
                S.dma("sp", gateb[:], self.MOD[c:c + 1, 2 * D:3 * D].partition_broadcast(128), reads=["MOD"],
                      writes=["gateb"])
            yt = ytile[tt % 3]
            xb = xbs[tt % 2]
            S.dma("sp", yt[:], self.YT[:, tt * 128:(tt + 1) * 128].rearrange("(kt p) t -> p kt t", p=128),
                  reads=["YT"], writes=[yt.name])
            S.dma("sp", xb[:], self.X[tt * 128:(tt + 1) * 128, :], reads=["X"], writes=[xb.name])
            for cg in range(4):
                p = self.psum()

                def mm(e, p=p, yt=yt, cg=cg):
                    for kt in range(32):
                        r = e.matmul(p[:, :], yt[:, kt, :], wd[:, kt, cg * 512:(cg + 1) * 512], start=(kt == 0),
                                     stop=(kt == 31))
                    return r
                S.op("pe", mm, reads=[yt.name] + wkeys, writes=[p.name])
                tm = tmp[k % 2]
                k += 1
                S.op("dve", lambda e, p=p, cg=cg, tm=tm: e.tensor_tensor(
                    out=tm[:], in0=p[:, :], in1=gateb[:, cg * 512:(cg + 1) * 512], op=ALU.mult),
                    reads=[p.name, "gateb"], writes=[tm.name])
                S.op("pool", lambda e, xb=xb, cg=cg, tm=tm: e.tensor_tensor(
                    out=xb[:, cg * 512:(cg + 1) * 512], in0=xb[:, cg * 512:(cg + 1) * 512], in1=tm[:], op=ALU.add),
                    reads=[xb.name, tm.name], writes=[xb.name])
            S.dma("sp", self.X[tt * 128:(tt + 1) * 128, :], xb[:], reads=[xb.name], writes=["X"])

    def final_norm(self):
        S, ins = self.S, self.ins
        NTT = self.NT // 128
        with ExitStack() as stk:
            sb = self.sbp(stk)
            big = [sb("fbig%d" % i, [128, D], F32) for i in range(3)]
            gB = sb("gfin", [128, D], F32)
            S.dma("sp", gB[:], ins["g_final"][0:1, :].partition_broadcast(128), writes=["gfin"])
            for tt in range(NTT):
                xt = big[tt % 2]
                S.dma("sp", xt[:], self.X[tt * 128:(tt + 1) * 128, :], reads=["X"], writes=[xt.name])
                ss = self.small[:, 0:1]
                rs = self.small[:, 1:2]
                junk = big[2]
                S.op("act", lambda e, xt=xt: e.activation(out=junk[:], in_=xt[:], func=AF.Square, accum_out=ss),
                     reads=[xt.name], writes=["fbig2", "small"])
                self.rstd_col(ss, rs, D)
                S.op("dve", lambda e, xt=xt: e.scalar_tensor_tensor(out=xt[:], in0=xt[:], scalar=rs, in1=gB[:],
                                                                    op0=ALU.mult, op1=ALU.mult),
                     reads=[xt.name, "small", "gfin"], writes=[xt.name])
                S.dma("sp", self.y_out[tt * 128:(tt + 1) * 128, :], xt[:], reads=[xt.name], writes=["y_out"])
            S.barrier()


def host_consts():
    s = np.arange(128)
    tril = (s[:, None] <= s[None, :]).astype(np.float32)
    triu = (s[:, None] >= s[None, :]).astype(np.float32)
    trilS = (s[:, None] < s[None, :]).astype(np.float32)
    triuS = (s[:, None] > s[None, :]).astype(np.float32)
    lvl = np.zeros((128, 7, 128), np.float32)
    for li in range(7):
        b = 1 << li
        for i in range(0, 128, 2 * b):
            lvl[i + b:i + 2 * b, li, i:i + b] = 1.0
    lvlT = np.ascontiguousarray(lvl.transpose(2, 1, 0))
    return {"ident": np.eye(128, dtype=np.float32), "tril": tril, "triu": triu, "trilS": trilS, "triuS": triuS,
            "lvl": lvl, "lvlT": lvlT,
            "ones": np.ones((128, 128), np.float32)}


def make_inputs(inp, core, x_in=None, b=None):
    if b is None:
        b = core // 4
    cond = np.stack([inp["c_ctx"], inp["c"][b]], 0)
    condT = np.ascontiguousarray(cond.reshape(2, 16, 128).transpose(2, 1, 0))
    im = dict(host_consts())
    if x_in is None:
        xp = inp["x_prompt"][2 * core:2 * core + 2].reshape(512, D)
        xs = inp["x_sample"][b]
        x_in = np.concatenate([xp, xs], 0)
    im.update(
        x_in=np.ascontiguousarray(x_in), condT=condT, w_ada=inp["w_ada"], b_ada=inp["b_ada"],
        g_norm=inp["g_norm"], g_final=inp["g_final"][None, :],
        w_sc_in=inp["w_sc_in"][0],
        w_sc_conv=np.ascontiguousarray(inp["w_sc_conv"][0].reshape(3, 32, 128).transpose(2, 1, 0)),
        w_sc_out=inp["w_sc_out"][0],
        w_ml_in=inp["w_ml_in"],
        b_mlg=np.ascontiguousarray(inp["b_ml_gate"].reshape(2, 4, 8).transpose(0, 2, 1)),
        g_ml_head=inp["g_ml_head"], w_ml_out=inp["w_ml_out"],
        mlC0=inp["cache_ml_C"][b], mln0=inp["cache_ml_n"][b], mlm0=inp["cache_ml_m"][b],
        w_gd_in=inp["w_gd_in"][0],
        w_gd_conv=np.ascontiguousarray(inp["w_gd_conv"][0].reshape(3, 64, 128).transpose(2, 1, 0)),
        gd_par=np.ascontiguousarray(np.tile(np.concatenate([inp["gd_A_log"][0].T, inp["gd_dt_bias"][0].T], 1), (4, 1))),
        g_gd_norm=inp["g_gd_norm"], w_gd_out=inp["w_gd_out"][0], gdS0=inp["cache_gd_S"][b, 0],
    )
    return im


SEQS = [dict(T=256, cond=0, roww=256, kind="p"), dict(T=256, cond=0, roww=256, kind="p"),
        dict(T=2048, cond=1, roww=64, kind="s")]


def kernel(**inputs):
    inp = {k: np.ascontiguousarray(np.asarray(v)) for k, v in inputs.items()}
    kb = K(SEQS, [0, 1, 2, 3], do_final=True)
    nc = kb.build()
    in_maps = []
    for core in range(8):
        im = make_inputs(inp, core)
        in_maps.append({k: np.ascontiguousarray(v, dtype=np.float32) for k, v in im.items() if k in kb.ins})
    res = run_bass_kernel_spmd(nc, in_maps, core_ids=list(range(8)))
    r = res.results
    y_prompt = np.concatenate([r[c]["y_out"][:512].reshape(2, 256, D) for c in range(8)], 0)
    y_sample = np.stack([r[0]["y_out"][512:], r[4]["y_out"][512:]], 0)
    st_C = np.concatenate([r[c]["st_C"] for c in range(8)], 0)
    st_n = np.concatenate([r[c]["st_n"] for c in range(8)], 0)
    st_m = np.concatenate([r[c]["st_m"] for c in range(8)], 0)
    st_S = np.concatenate([r[c]["st_S"] for c in range(8)], 0)
    return (y_prompt.astype(np.float32), y_sample.astype(np.float32), st_C.astype(np.float32),
            st_n.astype(np.float32), st_m.astype(np.float32), st_S.astype(np.float32))
```

```python
import math
from contextlib import ExitStack
import numpy as np
import concourse.bass as bass
import concourse.mybir as mybir
from concourse.bass_utils import run_bass_kernel_spmd

F32 = mybir.dt.float32
BF16 = mybir.dt.bfloat16
AF = mybir.ActivationFunctionType
ALU = mybir.AluOpType
AX = mybir.AxisListType

D = 2048
KT = D // 128
EPS = 1e-6
INNER = 4096
ML_H, ML_DK, ML_DV = 8, 256, 512
ML_COLS = 16416
SC_COLS = 16384
GD_COLS = 12416
GD_HV, GD_HQ = 32, 16
CH = 128


class Sched:
    def __init__(self, nc, stack, ndma=30):
        self.nc = nc
        self.E = {"pe": nc.tensor, "dve": nc.vector, "act": nc.scalar, "pool": nc.gpsimd, "sp": nc.sync}
        self.sem = {e: stack.enter_context(nc.semaphore("s_" + e)) for e in ("pe", "dve", "act", "pool")}
        self.cnt = {e: 0 for e in self.sem}
        self.dsem = [stack.enter_context(nc.semaphore("d%d" % i)) for i in range(ndma)]
        self.dcnt = [0] * ndma
        self.drr = 0
        self.drr_pool = 0
        self.known = {e: {} for e in self.E}
        self.last_w = {}
        self.readers = {}
        self.nwaits = 0
        self.ninst = 0
        self.deferred = []
        self.round = 0

    def _semobj(self, key):
        return self.sem[key] if isinstance(key, str) else self.dsem[key]

    def _wait(self, eng, tok):
        key, val = tok
        if eng == "pe" and key == "pe":
            return
        if self.known[eng].get(key, 0) >= val:
            return
        self.E[eng].wait_ge(self._semobj(key), val)
        self.known[eng][key] = val
        self.nwaits += 1

    def _deps(self, eng, reads, writes):
        for r in reads:
            lw = self.last_w.get(r)
            if lw:
                self._wait(eng, lw)
        for w in writes:
            lw = self.last_w.get(w)
            if lw:
                self._wait(eng, lw)
            for key, val in self.readers.get(w, {}).items():
                self._wait(eng, (key, val))

    def _record(self, tok, reads, writes):
        key, val = tok
        for r in reads:
            d = self.readers.setdefault(r, {})
            d[key] = max(d.get(key, 0), val)
        for w in writes:
            self.last_w[w] = tok
            self.readers[w] = {}

    def op(self, eng, fn, reads=(), writes=()):
        self._deps(eng, reads, writes)
        inst = fn(self.E[eng])
        self.cnt[eng] += 1
        inst.then_inc(self.sem[eng], 1)
        self.ninst += 1
        self._record((eng, self.cnt[eng]), reads, writes)

    def dma(self, q, out, in_, reads=(), writes=(), slow=False):
        if self.deferred and reads:
            rs = set(reads)
            hit = [it for it in self.deferred if rs & set(it[2].get("writes", ()))]
            if hit:
                self.deferred = [it for it in self.deferred if not (rs & set(it[2].get("writes", ())))]
                for it in hit:
                    self.dma(*it[1], **it[2])
        self._deps(q, reads, writes)
        npool = 6
        if q == "pool":
            i = self.drr_pool
            self.drr_pool = (self.drr_pool + 1) % npool
        else:
            i = npool + self.drr
            self.drr = (self.drr + 1) % (len(self.dsem) - npool)
        if self.dcnt[i]:
            self._wait(q, (i, self.dcnt[i]))
        self.dcnt[i] += 16
        if slow:
            self.E[q].dma_start(out=out, in_=in_, allow_slow_non_contiguous=True).then_inc(self.dsem[i], 16)
        else:
            self.E[q].dma_start(out=out, in_=in_).then_inc(self.dsem[i], 16)
        self.ninst += 1
        self._record((i, self.dcnt[i]), reads, writes)

    def barrier(self):
        self.flush()
        for e in self.E:
            for i, c in enumerate(self.dcnt):
                if c:
                    self._wait(e, (i, c))
            for k, c in self.cnt.items():
                if c and k != e:
                    self._wait(e, (k, c))
        for e in self.sem:
            if self.cnt[e]:
                self._wait(e, (e, self.cnt[e])) if e != "pe" else None
        self.last_w = {}
        self.readers = {}

    def defer_dma(self, *a, **kw):
        self.deferred.append([self.round + 3, a, kw])

    def tick(self):
        self.round += 1
        keep = []
        for it in self.deferred:
            if it[0] <= self.round:
                self.dma(*it[1], **it[2])
            else:
                keep.append(it)
        self.deferred = keep

    def flush(self):
        for it in self.deferred:
            self.dma(*it[1], **it[2])
        self.deferred = []

    def finish(self, eng="sp"):
        for i, c in enumerate(self.dcnt):
            if c:
                self._wait(eng, (i, c))
        for e, c in self.cnt.items():
            if c:
                self._wait(eng, (e, c))


def run_streams(gens, S=None):
    gens = list(gens)
    while gens:
        for g in list(gens):
            try:
                next(g)
            except StopIteration:
                gens.remove(g)
        if S is not None:
            S.tick()
    if S is not None:
        S.flush()


class TN:
    def __init__(self, name, t):
        self.name = name
        self.t = t

    def __getitem__(self, k):
        return self.t[k]


class K:
    def __init__(self, seqs, layers, do_final=True):
        self.seqs = []
        t0 = 0
        npi = 0
        for s in seqs:
            s = dict(s)
            s["t0"] = t0
            t0 += s["T"]
            if s["kind"] == "p":
                s["pi"] = npi
                npi += 1
            self.seqs.append(s)
        self.NT = t0
        assert self.NT % 512 == 0
        self.layers = layers
        self.do_final = do_final
        self.n_p = npi
        self.Tmax = max(s["T"] for s in self.seqs)

    def build(self):
        nc = bass.Bass("TRN2", target_bir_lowering=False)
        self.nc = nc
        NT = self.NT
        dt = nc.dram_tensor
        ins = {}

        def inp(name, shape, dtype=F32):
            ins[name] = dt(name, list(shape), dtype, kind="ExternalInput").ap()
            return ins[name]

        def scratch(name, shape, dtype):
            return dt(name, list(shape), dtype, kind="Internal").ap()

        def outp(name, shape):
            return dt(name, list(shape), F32, kind="ExternalOutput").ap()

        self.ins = ins
        inp("x_in", [NT, D])
        inp("condT", [128, KT, 2])
        inp("ident", [128, 128])
        inp("tril", [128, 128])
        inp("triu", [128, 128])
        inp("ones", [128, 128])
        inp("w_ada", [4, D, 3 * D])
        inp("b_ada", [4, 3 * D])
        inp("g_norm", [4, D])
        inp("g_final", [1, D])
        kinds = set(l % 3 for l in self.layers)
        self.kinds = kinds
        if 1 in kinds:
            inp("w_sc_in", [D, SC_COLS])
            inp("w_sc_conv", [128, 32, 3])
            inp("w_sc_out", [INNER, D])
        if 0 in kinds:
            inp("w_ml_in", [2, D, ML_COLS])
            inp("b_mlg", [2, 8, 4])
            inp("g_ml_head", [2, INNER])
            inp("w_ml_out", [2, INNER, D])
            inp("mlC0", [2, 2, 8, ML_DK, ML_DV])
            inp("mln0", [2, 2, 8, ML_DK])
            inp("mlm0", [2, 2, 8])
            self.QT = scratch("QT", [2048, NT], BF16)
            self.KTd = scratch("KTd", [2048, NT], BF16)
            self.VOZ = scratch("VOZ", [3, NT, INNER], BF16)
            self.GATES = scratch("GATES", [4, 8, NT], F32)
            self.st_C = outp("st_C", [self.n_p, 2, 2, 8, ML_DK, ML_DV])
            self.st_n = outp("st_n", [self.n_p, 2, 2, 8, ML_DK])
            self.st_m = outp("st_m", [self.n_p, 2, 2, 8])
        if 2 in kinds:
            inp("w_gd_in", [D, GD_COLS])
            inp("w_gd_conv", [128, 64, 3])
            inp("gd_par", [128, 4])
            inp("g_gd_norm", [1, 128])
            inp("w_gd_out", [INNER, D])
            inp("gdS0", [2, 32, 128, 128])
            inp("trilS", [128, 128])
            inp("lvl", [128, 7, 128])
            inp("lvlT", [128, 7, 128])
            inp("triuS", [128, 128])
            self.GQT = scratch("GQT", [2048, NT], BF16)
            self.GKT = scratch("GKT", [2048, NT], BF16)
            self.GV = scratch("GV", [NT, INNER], BF16)
            self.GZ = scratch("GZ", [NT, INNER], BF16)
            self.GAB = scratch("GAB", [4, 32, NT], F32)
            self.st_S = outp("st_S", [self.n_p, 1, 2, 32, 128, 128])
        self.X = scratch("X", [NT, D], F32)
        self.YT = scratch("YT", [INNER, NT], BF16)
        self.MOD = scratch("MOD", [2, 3 * D], F32)
        self.OSD = scratch("OSD", [4, self.Tmax, 512], F32)
        self.y_out = outp("y_out", [NT, D])

        with ExitStack() as st:
            self.S = Sched(nc, st)
            self._uid = [0]

            def _mk(stk, name, shape, dtype):
                self._uid[0] += 1
                return TN(name, stk.enter_context(nc.sbuf_tensor("%s_%d" % (name, self._uid[0]), list(shape), dtype)))
            self.sbp = lambda stk: (lambda name, shape, dtype: _mk(stk, name, shape, dtype))
            sb = self.sbp(st)
            ps = lambda name, shape, dtype: TN(name, st.enter_context(nc.psum_tensor(name, list(shape), dtype)))
            self.ident = sb("ident_sb", [128, 128], F32)
            self.identb = sb("identb", [128, 128], BF16)
            self.tril = sb("tril_sb", [128, 128], F32)
            self.triu = sb("triu_sb", [128, 128], F32)
            self.ones = sb("ones_sb", [128, 128], F32)
            if 2 in self.kinds:
                self.trilS = sb("trilS_sb", [128, 128], F32)
                self.triuS = sb("triuS_sb", [128, 128], F32)
                self.lvl = sb("lvl_sb", [128, 7, 128], BF16)
                self.lvlT = sb("lvlT_sb", [128, 7, 128], BF16)
            self.condT = sb("condT_sb", [128, KT, 2], F32)
            self.scT = sb("scT", [128, KT, 2], BF16)
            self.small = sb("small", [128, 64], F32)
            self.stg = [sb("stg%d" % i, [128, 512], BF16) for i in range(4)]
            self.stgf = [sb("stgf%d" % i, [128, 512], F32) for i in range(2)]
            self.modst = [sb("modst%d" % i, [1, 2, 512], F32) for i in range(2)]
            self.pp = [ps("pp%d" % i, [128, 512], F32) for i in range(6)]
            self.ptb = ps("ptb", [128, 1024], BF16)
            self.pmisc = ps("pmisc", [128, 512], F32)
            self.ppi = 0
            self.stgi = 0
            self.emit()
            self.S.finish("sp")
        return nc

    def psum(self):
        p = self.pp[self.ppi % len(self.pp)]
        self.ppi += 1
        return p

    def psum_s(self, A):
        p = A["pp"][A["ppi"] % len(A["pp"])]
        A["ppi"] += 1
        return p

    def stage(self):
        p = self.stg[self.stgi % len(self.stg)]
        self.stgi += 1
        return p

    def emit(self):
        S, nc, ins = self.S, self.nc, self.ins
        S.dma("sp", self.ident[:], ins["ident"][:, :], writes=["ident_sb"])
        S.dma("sp", self.tril[:], ins["tril"][:, :], writes=["tril_sb"])
        S.dma("sp", self.triu[:], ins["triu"][:, :], writes=["triu_sb"])
        S.dma("sp", self.ones[:], ins["ones"][:, :], writes=["ones_sb"])
        S.dma("sp", self.condT[:], ins["condT"][:, :, :], writes=["condT_sb"])
        if "trilS" in ins:
            S.dma("sp", self.trilS[:], ins["trilS"][:, :], writes=["trilS_sb"])
            S.dma("sp", self.triuS[:], ins["triuS"][:, :], writes=["triuS_sb"])
            S.dma("pool", self.lvl[:], ins["lvl"][:, :, :], writes=["lvl"])
            S.dma("pool", self.lvlT[:], ins["lvlT"][:, :, :], writes=["lvl"])
        S.op("dve", lambda e: e.tensor_copy(self.identb[:], self.ident[:]), reads=["ident_sb"], writes=["identb"])
        S.op("act", lambda e: e.activation(out=self.scT[:], in_=self.condT[:], func=AF.Silu),
             reads=["condT_sb"], writes=["scT"])
        S.dma("sp", self.X[:, :], ins["x_in"][:, :], writes=["X"])
        for l in self.layers:
            self.layer(l)
        if self.do_final:
            self.final_norm()
        else:
            S.dma("sp", self.y_out[:, :], self.X[:, :], reads=["X"], writes=["y_out"])

    def layer(self, l):
        S, ins = self.S, self.ins
        kind = l % 3
        j = l // 3
        with ExitStack() as stk:
            sb = self.sbp(stk)
            self.wbuf = [sb("wbuf%d" % i, [128, KT, 512], BF16) for i in range(2)]
            self.compute_mod(l)
            self.hT = sb("hT", [128, KT, self.NT], BF16)
            self.phase_a(l, sb)
            if kind == 1:
                with ExitStack() as stk2:
                    self.sc_mixer(self.sbp(stk2))
                    S.barrier()
            elif kind == 0:
                self.ml_proj(j)
                S.barrier()
            else:
                with ExitStack() as stk2:
                    self.gd_proj(self.sbp(stk2))
                    S.barrier()
        if kind == 0:
            with ExitStack() as stk:
                self.ml_scan(j, self.sbp(stk))
                S.barrier()
        if kind == 2:
            with ExitStack() as stk:
                self.gd_scan(self.sbp(stk))
                S.barrier()
        wout = {0: lambda: ins["w_ml_out"][j], 1: lambda: ins["w_sc_out"], 2: lambda: ins["w_gd_out"]}[kind]()
        with ExitStack() as stk:
            self.phase_d(l, wout, self.sbp(stk))
            S.barrier()

    def compute_mod(self, l):
        S, ins = self.S, self.ins
        for cb in range(12):
            wb = self.wbuf[cb % 2]
            src = ins["w_ada"][l, :, cb * 512:(cb + 1) * 512].rearrange("(kt p) c -> p kt c", p=128)
            S.dma("pool", wb[:], src, writes=[wb.name])
            bb = self.stgf[cb % 2]
            S.dma("sp", bb[0:1, :], ins["b_ada"][l:l + 1, cb * 512:(cb + 1) * 512], writes=[bb.name])
            ms = self.modst[cb % 2]
            for c in range(2):
                p = self.psum()

                def mm(e, p=p, wb=wb, c=c):
                    for kt in range(KT):
                        r = e.matmul(p[0:1, :], self.scT[:, kt, c:c + 1], wb[:, kt, :], start=(kt == 0),
                                     stop=(kt == KT - 1))
                    return r
                S.op("pe", mm, reads=["scT", wb.name], writes=[p.name])
                S.op("dve", lambda e, p=p, bb=bb, ms=ms, c=c: e.tensor_tensor(
                    out=ms[0:1, c, :], in0=p[0:1, :], in1=bb[0:1, :], op=ALU.add),
                    reads=[p.name, bb.name], writes=[ms.name])
                S.dma("sp", self.MOD[c:c + 1, cb * 512:(cb + 1) * 512], ms[0:1, c, :],
                      reads=[ms.name], writes=["MOD"])

    def tile_cond(self, tt):
        tok = tt * 128
        for s in self.seqs:
            if s["t0"] <= tok < s["t0"] + s["T"]:
                return s["cond"]
        raise AssertionError

    def rstd_col(self, ss, rs, n, key="small"):
        S = self.S
        S.op("dve", lambda e: e.tensor_scalar(out=rs, in0=ss, scalar1=1.0 / n, scalar2=EPS,
                                              op0=ALU.mult, op1=ALU.add), reads=[key], writes=[key])
        S.op("act", lambda e: e.activation(out=rs, in_=rs, func=AF.Sqrt), reads=[key], writes=[key])
        S.op("dve", lambda e: e.reciprocal(out=rs, in_=rs), reads=[key], writes=[key])

    def phase_a(self, l, sb):
        S, ins = self.S, self.ins
        big = [sb("big%d" % i, [128, D], F32) for i in range(3)]
        hbf = sb("hbf", [128, D], BF16)
        amod = sb("amod", [128, D], F32)
        shiftb = sb("shiftb", [128, D], F32)
        NTT = self.NT // 128
        cur = None
        for tt in range(NTT):
            c = self.tile_cond(tt)
            if c != cur:
                cur = c
                S.dma("sp", shiftb[:], self.MOD[c:c + 1, 0:D].partition_broadcast(128), reads=["MOD"],
                      writes=["shiftb"])
                S.dma("sp", amod[:], self.MOD[c:c + 1, D:2 * D].partition_broadcast(128), reads=["MOD"],
                      writes=["amod"])
                S.dma("sp", big[2][:], ins["g_norm"][l:l + 1, :].partition_broadcast(128), writes=["big2"])
                S.op("dve", lambda e: e.scalar_tensor_tensor(out=amod[:], in0=amod[:], scalar=1.0, in1=big[2][:],
                                                             op0=ALU.add, op1=ALU.mult),
                     reads=["amod", "big2"], writes=["amod"])
            xt = big[tt % 2]
            S.dma("sp", xt[:], self.X[tt * 128:(tt + 1) * 128, :], reads=["X"], writes=[xt.name])
            ss = self.small[:, 0:1]
            rs = self.small[:, 1:2]
            junk = big[2]
            S.op("act", lambda e, xt=xt: e.activation(out=junk[:], in_=xt[:], func=AF.Square, accum_out=ss),
                 reads=[xt.name], writes=["big2", "small"])
            self.rstd_col(ss, rs, D)
            S.op("dve", lambda e, xt=xt: e.scalar_tensor_tensor(out=junk[:], in0=xt[:], scalar=rs, in1=amod[:],
                                                                op0=ALU.mult, op1=ALU.mult),
                 reads=[xt.name, "small", "amod"], writes=["big2"])
            S.op("dve", lambda e: e.tensor_tensor(out=hbf[:], in0=junk[:], in1=shiftb[:], op=ALU.add),
                 reads=["big2", "shiftb"], writes=["hbf"])
            for g in range(KT // 8):
                def tr(e, g=g):
                    for jj in range(8):
                        kt = g * 8 + jj
                        r = e.transpose(self.ptb[:, jj * 128:(jj + 1) * 128], hbf[:, kt * 128:(kt + 1) * 128],
                                        self.identb[:])
                    return r
                S.op("pe", tr, reads=["hbf", "identb"], writes=["ptb"])
                S.op("act", lambda e, g=g, tt=tt: e.copy(
                    out=self.hT[:, g * 8:(g + 1) * 8, tt * 128:(tt + 1) * 128],
                    in_=self.ptb[:, :].rearrange("p (j t) -> p j t", j=8)), reads=["ptb"], writes=["hT"])

    def load_w(self, W, col0, ncols, i):
        wb = self.wbuf[i % 2]
        src = W[:, col0:col0 + ncols].rearrange("(kt p) c -> p kt c", p=128)
        self.S.dma("pool", wb[:, :, 0:ncols], src, writes=[wb.name])
        return wb

    def proj_fm(self, wb, c0, m, tb, p):
        def mm(e):
            for kt in range(KT):
                r = e.matmul(p[0:m, :], wb[:, kt, c0:c0 + m], self.hT[:, kt, tb * 512:(tb + 1) * 512],
                             start=(kt == 0), stop=(kt == KT - 1))
            return r
        self.S.op("pe", mm, reads=[wb.name, "hT"], writes=[p.name])

    def proj_tm(self, wb, tt, p):
        def mm(e):
            for kt in range(KT):
                r = e.matmul(p[:, :], self.hT[:, kt, tt * 128:(tt + 1) * 128], wb[:, kt, :],
                             start=(kt == 0), stop=(kt == KT - 1))
            return r
        self.S.op("pe", mm, reads=[wb.name, "hT"], writes=[p.name])

    def evac(self, i, out, in_, reads, writes, scale=None):
        S = self.S
        if i % 2 == 0:
            if scale is None:
                S.op("act", lambda e: e.copy(out=out, in_=in_), reads=reads, writes=writes)
            else:
                S.op("act", lambda e: e.activation(out=out, in_=in_, func=AF.Copy, scale=scale), reads=reads,
                     writes=writes)
        else:
            if scale is None:
                S.op("dve", lambda e: e.tensor_copy(out, in_), reads=reads, writes=writes)
            else:
                S.op("dve", lambda e: e.tensor_scalar(out=out, in0=in_, scalar1=scale, scalar2=None, op0=ALU.mult),
                     reads=reads, writes=writes)

    def sc_mixer(self, sb):
        S, ins = self.S, self.ins
        NTB = self.NT // 512
        W = ins["w_sc_in"]
        wconv = sb("wconv", [128, 32, 3], F32)
        t2k = [sb("t2k%d" % i, [128, 512], F32) for i in range(4)]
        S.dma("sp", wconv[:], ins["w_sc_conv"][:, :, :], writes=["wconv"])
        for ct in range(32):
            wb = self.wbuf[ct % 2]
            wk = wb.name
            for jj in range(4):
                src = W[:, jj * INNER + ct * 128: jj * INNER + (ct + 1) * 128].rearrange("(kt p) c -> p kt c", p=128)
                S.dma("pool", wb[:, :, jj * 128:(jj + 1) * 128], src, writes=[wk])
            for tb in range(NTB):
                roww = self.roww_of_block(tb)
                pl = [self.psum() for _ in range(4)]
                pu, pB, pC, pz = pl
                for jj, p in enumerate(pl):
                    self.proj_fm(wb, jj * 128, 128, tb, p)
                u_sb, cu, acc, sz = t2k
                S.op("act", lambda e, pu=pu: e.copy(out=u_sb[:], in_=pu[:, :]), reads=[pu.name], writes=["t2k0"])
                S.op("act", lambda e, pz=pz: e.activation(out=sz[:], in_=pz[:, :], func=AF.Silu),
                     reads=[pz.name], writes=["t2k3"])
                S.op("dve", lambda e, pC=pC: e.tensor_tensor(out=cu[:], in0=pC[:, :], in1=u_sb[:], op=ALU.mult),
                     reads=[pC.name, "t2k0"], writes=["t2k1"])
                w0 = wconv[:, ct, 0:1]
                w1 = wconv[:, ct, 1:2]
                w2 = wconv[:, ct, 2:3]
                S.op("dve", lambda e, w1=w1: e.tensor_scalar(out=acc[:], in0=cu[:], scalar1=w1, scalar2=None,
                                                             op0=ALU.mult), reads=["t2k1", "wconv"], writes=["t2k2"])
                cu3 = cu[:].rearrange("p (r w) -> p r w", w=roww)
                acc3 = acc[:].rearrange("p (r w) -> p r w", w=roww)
                S.op("dve", lambda e, w0=w0, cu3=cu3, acc3=acc3: e.scalar_tensor_tensor(
                    out=acc3[:, :, 1:], in0=cu3[:, :, :-1], scalar=w0, in1=acc3[:, :, 1:], op0=ALU.mult, op1=ALU.add),
                    reads=["t2k1", "t2k2", "wconv"], writes=["t2k2"])
                S.op("dve", lambda e, w2=w2, cu3=cu3, acc3=acc3: e.scalar_tensor_tensor(
                    out=acc3[:, :, :-1], in0=cu3[:, :, 1:], scalar=w2, in1=acc3[:, :, :-1], op0=ALU.mult, op1=ALU.add),
                    reads=["t2k1", "t2k2", "wconv"], writes=["t2k2"])
                S.op("dve", lambda e, pB=pB: e.tensor_tensor(out=acc[:], in0=pB[:, :], in1=acc[:], op=ALU.mult),
                     reads=[pB.name, "t2k2"], writes=["t2k2"])
                yb = self.stage()
                S.op("dve", lambda e, yb=yb: e.tensor_tensor(out=yb[:], in0=acc[:], in1=sz[:], op=ALU.mult),
                     reads=["t2k2", "t2k3"], writes=[yb.name])
                S.dma("sp", self.YT[ct * 128:(ct + 1) * 128, tb * 512:(tb + 1) * 512], yb[:],
                      reads=[yb.name], writes=["YT"])

    def roww_of_block(self, tb):
        tok = tb * 512
        for s in self.seqs:
            if s["t0"] <= tok < s["t0"] + s["T"]:
                assert 512 % s["roww"] == 0 and (tok - s["t0"]) % s["roww"] == 0
                return s["roww"]
        raise AssertionError

    def ml_proj(self, j):
        S, ins = self.S, self.ins
        W = ins["w_ml_in"][j]
        NTB = self.NT // 512
        NTT = self.NT // 128
        wi = 0
        ei = 0
        for g in range(8):
            wb = self.load_w(W, g * 512, 512, wi)
            wi += 1
            for ci in range(4):
                col = g * 512 + ci * 128
                dst = self.QT if col < 2048 else self.KTd
                r0 = col % 2048
                for tb in range(NTB):
                    p = self.psum()
                    self.proj_fm(wb, ci * 128, 128, tb, p)
                    sg = self.stage()
                    self.evac(ei, sg[:], p[:, :], [p.name], [sg.name], scale=(ML_DK ** -0.5) if col < 2048 else None)
                    ei += 1
                    S.dma("sp", dst[r0:r0 + 128, tb * 512:(tb + 1) * 512], sg[:], reads=[sg.name],
                          writes=[dst.name])
        for g in range(24):
            wb = self.load_w(W, 4096 + g * 512, 512, wi)
            wi += 1
            which = g // 8
            c0 = (g % 8) * 512
            for tt in range(NTT):
                p = self.psum()
                self.proj_tm(wb, tt, p)
                sg = self.stage()
                self.evac(ei, sg[:], p[:, :], [p.name], [sg.name])
                ei += 1
                S.dma("sp", self.VOZ[which, tt * 128:(tt + 1) * 128, c0:c0 + 512], sg[:], reads=[sg.name],
                      writes=["VOZ"])
        wb = self.load_w(W, 16384, 32, wi)
        S.dma("sp", self.small[0:8, 8:12], ins["b_mlg"][j], writes=["small_bg"])
        for typ in range(4):
            for tb in range(NTB):
                p = self.psum()
                self.proj_fm(wb, typ * 8, 8, tb, p)
                sg = self.stgf[(typ * NTB + tb) % 2]
                S.op("dve", lambda e, p=p, sg=sg, typ=typ: e.tensor_scalar(
                    out=sg[0:8, :], in0=p[0:8, :], scalar1=self.small[0:8, 8 + typ:9 + typ], scalar2=None,
                    op0=ALU.add), reads=[p.name, "small_bg"], writes=[sg.name])
                S.dma("sp", self.GATES[typ, :, tb * 512:(tb + 1) * 512], sg[0:8, :], reads=[sg.name],
                      writes=["GATES"])

    def ml_scan(self, j, sb):
        S, ins = self.S, self.ins
        A = {}
        A["gt"] = sb("gt", [128, 4, 128], F32)
        A["ee"] = sb("ee", [128, 2, 128], F32)
        A["nb"] = sb("nb", [128, 2, 128], F32)
        A["uu"] = sb("uu", [128, 2, 128], F32)
        A["ww"] = sb("ww", [128, 2, 128], F32)
        A["fl"] = sb("fl", [128, 2, 128], F32)
        A["cols"] = sb("cols", [128, 16], F32)
        A["rows"] = sb("rows", [1, 12, 128], F32)
        A["mh"] = sb("mh", [1, 2, 17, 8], F32)
        A["toksc"] = sb("toksc", [128, 4, 128], F32)
        A["decB"] = sb("decB", [128, 2, 128], F32)
        A["diag"] = sb("diag", [128, 128], F32)
        NS = 4
        AS = []
        for k in range(NS):
            B = dict(A)
            g = lambda n, shp, dt=F32, B=B, k=k: B.__setitem__(n, sb("%s_s%d" % (n, k), shp, dt))
            B["HSb"] = [sb("HSb%d_s%d" % (i, k), [128, 512], F32) for i in range(2)]
            B["sid"] = k
            B["Cst"] = [sb("Cst%d_s%d" % (d, k), [128, 2, 513], F32) for d in range(2)]
            g("Cb", [128, 2, 513], BF16)
            for n in ("qT", "kT"):
                B[n] = [sb("%s%d_s%d" % (n, i, k), [128, 2, 128], BF16) for i in range(2)]
            g("ktm", [128, 256], BF16)
            for n in ("v", "o", "z", "ytr"):
                B[n] = [sb("%s%d_s%d" % (n, i, k), [128, 512], BF16) for i in range(2)]
            g("vx", [128, 520], BF16)
            g("sm", [128, 128], BF16)
            g("hs", [128, 512])
            g("sig", [128, 512])
            g("sz", [128, 512])
            g("ybf", [128, 512], BF16)
            g("sc8", [128, 8])
            g("ghead", [128, 512])
            B["uc"] = 0
            AS.append(B)
        for s in self.seqs:
            self.ml_gates(j, s, A)

            def stream(k, s=s):
                for h in range(k, ML_H, NS):
                    S.dma("sp", AS[k]["ghead"][:], ins["g_ml_head"][j:j + 1, h * 512:(h + 1) * 512].partition_broadcast(128),
                          writes=[AS[k]["ghead"].name])
                    for d in (1, 0):
                        yield from self.ml_head_dir(j, s, h, d, AS[k])
            run_streams([stream(k) for k in range(NS)], S)

    def ml_gates(self, j, s, A):
        S, ins = self.S, self.ins
        T, t0 = s["T"], s["t0"]
        nch = T // 128
        R = nch * 8
        gt, ee, nb, uu, ww, fl, cols, rows, mh = (A[k] for k in ("gt", "ee", "nb", "uu", "ww", "fl", "cols", "rows", "mh"))
        for c in range(nch):
            for typ in range(4):
                S.dma("sp", gt[c * 8:(c + 1) * 8, typ, :], self.GATES[typ, :, t0 + c * 128: t0 + (c + 1) * 128],
                      reads=["GATES"], writes=["gt"])
        S.op("act", lambda e: e.activation(out=ee[0:R], in_=gt[0:R, 2:4, :], func=AF.Exp, scale=-1.0),
             reads=["gt"], writes=["ee"])
        S.op("dve", lambda e: e.tensor_scalar(out=ee[0:R], in0=ee[0:R], scalar1=1.0, scalar2=None, op0=ALU.add),
             reads=["ee"], writes=["ee"])
        S.op("act", lambda e: e.activation(out=ee[0:R], in_=ee[0:R], func=AF.Ln), reads=["ee"], writes=["ee"])
        S.op("dve", lambda e: e.tensor_tensor_scan(out=nb[0:R, 0, :], data0=self.ones[0:R, :], data1=ee[0:R, 0, :],
                                                   initial=0.0, op0=ALU.mult, op1=ALU.add),
             reads=["ee", "ones_sb"], writes=["nb"])
        S.op("dve", lambda e: e.tensor_tensor_scan(out=nb[0:R, 1, :], data0=self.ones[0:R, :], data1=ee[0:R, 1, :],
                                                   initial=0.0, op0=ALU.mult, op1=ALU.add),
             reads=["ee", "ones_sb"], writes=["nb"])
        S.op("dve", lambda e: e.tensor_copy(cols[0:R, 8:9], nb[0:R, 1, 127:128]), reads=["nb"], writes=["cols"])
        S.op("dve", lambda e: e.tensor_tensor(out=nb[0:R, 1, :], in0=ee[0:R, 1, :], in1=nb[0:R, 1, :],
                                              op=ALU.subtract), reads=["ee", "nb"], writes=["nb"])
        S.op("dve", lambda e: e.tensor_scalar(out=nb[0:R, 1, :], in0=nb[0:R, 1, :], scalar1=cols[0:R, 8:9],
                                              scalar2=None, op0=ALU.add), reads=["nb", "cols"], writes=["nb"])
        S.op("dve", lambda e: e.tensor_tensor(out=uu[0:R], in0=gt[0:R, 0:2, :], in1=nb[0:R], op=ALU.add),
             reads=["gt", "nb"], writes=["uu"])
        S.op("dve", lambda e: e.tensor_reduce(out=cols[0:R, 0:2], in_=uu[0:R], axis=AX.X, op=ALU.max),
             reads=["uu"], writes=["cols"])
        S.op("dve", lambda e: e.tensor_copy(cols[0:R, 2:3], nb[0:R, 0, 127:128]), reads=["nb"], writes=["cols"])
        S.op("dve", lambda e: e.tensor_copy(cols[0:R, 3:4], nb[0:R, 1, 0:1]), reads=["nb"], writes=["cols"])
        pm = self.pmisc
        for q in range(4):
            S.op("pe", lambda e, q=q: e.transpose(pm[0:1, q * 128:q * 128 + R], cols[0:R, q:q + 1],
                                                  self.ident[0:R, 0:R]), reads=["cols", "ident_sb"],
                 writes=["pmisc"])
        S.op("dve", lambda e: e.tensor_copy(rows[0:1, 0:4, 0:R], pm[0:1, :].rearrange("p (q r) -> p q r", q=4)[:, :, 0:R]),
             reads=["pmisc"], writes=["rows"])
        for d in range(2):
            if s["kind"] == "s":
                S.dma("sp", mh[0:1, d, 0, :], ins["mlm0"][j:j + 1, d, :], writes=["mh"])
            else:
                S.op("dve", lambda e, d=d: e.memset(mh[0:1, d, 0, :], 0.0), writes=["mh"])
            order = list(range(nch)) if d == 0 else list(range(nch - 1, -1, -1))
            for idx, c in enumerate(order):
                cs = slice(c * 8, (c + 1) * 8)
                S.op("dve", lambda e, d=d, idx=idx, cs=cs: e.tensor_tensor(
                    out=rows[0:1, 4 + d, cs], in0=mh[0:1, d, idx, :], in1=rows[0:1, d, cs], op=ALU.max),
                    reads=["mh", "rows"], writes=["rows"])
                S.op("dve", lambda e, d=d, idx=idx, cs=cs: e.tensor_tensor(
                    out=rows[0:1, 6 + d, cs], in0=mh[0:1, d, idx, :], in1=rows[0:1, 4 + d, cs], op=ALU.subtract),
                    reads=["mh", "rows"], writes=["rows"])
                S.op("dve", lambda e, d=d, idx=idx, cs=cs: e.tensor_tensor(
                    out=mh[0:1, d, idx + 1, :], in0=rows[0:1, 4 + d, cs], in1=rows[0:1, 2 + d, cs], op=ALU.subtract),
                    reads=["mh", "rows"], writes=["mh"])
            if s["kind"] == "p":
                S.dma("sp", self.st_m[s["pi"]:s["pi"] + 1, j, d, :], mh[0:1, d, nch, :], reads=["mh"],
                      writes=["st_m"])
        S.op("act", lambda e: e.activation(out=rows[0:1, 6:8, 0:R], in_=rows[0:1, 6:8, 0:R], func=AF.Exp),
             reads=["rows"], writes=["rows"])
        S.op("dve", lambda e: e.tensor_scalar(out=rows[0:1, 4:6, 0:R], in0=rows[0:1, 4:6, 0:R], scalar1=-1.0,
                                              scalar2=None, op0=ALU.mult), reads=["rows"], writes=["rows"])
        for q in range(4):
            S.op("pe", lambda e, q=q: e.transpose(pm[0:R, 256 + q:257 + q], rows[0:1, 4 + q, 0:R],
                                                  self.ident[0:1, 0:1]), reads=["rows", "ident_sb"],
                 writes=["pmisc"])
        S.op("dve", lambda e: e.tensor_copy(cols[0:R, 4:8], pm[0:R, 256:260]), reads=["pmisc"], writes=["cols"])
        for d in range(2):
            S.op("act", lambda e, d=d: e.activation(out=ww[0:R, d, :], in_=uu[0:R, d, :], func=AF.Exp,
                                                    bias=cols[0:R, 4 + d:5 + d], scale=1.0),
                 reads=["uu", "cols"], writes=["ww"])
            S.op("act", lambda e, d=d: e.activation(out=fl[0:R, d, :], in_=nb[0:R, d, :], func=AF.Exp,
                                                    bias=cols[0:R, 4 + d:5 + d], scale=1.0),
                 reads=["nb", "cols"], writes=["fl"])
        for q in range(4):
            src = ww if q < 2 else fl
            S.op("pe", lambda e, q=q, src=src: e.transpose(pm[:, q * 128:q * 128 + R], src[0:R, q % 2, :],
                                                           self.ident[0:R, 0:R]),
                 reads=[src.name, "ident_sb"], writes=["pmisc"])
        S.op("act", lambda e: e.copy(out=A["toksc"][:, :, 0:R],
                                     in_=pm[:, :].rearrange("p (q r) -> p q r", q=4)[:, :, 0:R]),
             reads=["pmisc"], writes=["toksc"])
        for d in range(2):
            S.op("dve", lambda e, d=d: e.tensor_scalar(out=A["diag"][0:R, 0:R], in0=self.ident[0:R, 0:R],
                                                       scalar1=cols[0:R, 6 + d:7 + d], scalar2=None, op0=ALU.mult),
                 reads=["ident_sb", "cols"], writes=["diag"])
            p = self.psum()
            S.op("pe", lambda e, p=p: e.matmul(p[:, 0:R], self.ones[0:R, :], A["diag"][0:R, 0:R], start=True,
                                               stop=True), reads=["ones_sb", "diag"], writes=[p.name])
            S.op("act", lambda e, p=p, d=d: e.copy(out=A["decB"][:, d, 0:R], in_=p[:, 0:R]), reads=[p.name],
                 writes=["decB"])

    def ml_head_dir(self, j, s, h, d, A):
        S, ins = self.S, self.ins
        T, t0 = s["T"], s["t0"]
        nch = T // 128
        Cst = A["Cst"][d]
        Cb = A["Cb"]
        if s["kind"] == "s":
            S.dma("sp", Cst[:, :, 0:512], ins["mlC0"][j, d, h].rearrange("(dt p) v -> p dt v", p=128),
                  writes=[Cst.name])
            S.dma("sp", Cst[:, :, 512], ins["mln0"][j, d, h].rearrange("(dt p) -> p dt", p=128),
                  writes=[Cst.name], slow=True)
        else:
            S.op("dve", lambda e: e.memset(Cst[:], 0.0), writes=[Cst.name])
        order = list(range(nch)) if d == 0 else list(range(nch - 1, -1, -1))
        mask = self.tril if d == 0 else self.triu
        for c in order:
            u = A["uc"]
            A["uc"] += 1
            tok0 = t0 + c * 128
            col = c * 8 + h
            qT, kT, v = A["qT"][u % 2], A["kT"][u % 2], A["v"][u % 2]
            ktm, vx, sm = A["ktm"], A["vx"], A["sm"]
            toksc, decB = A["toksc"], A["decB"]
            S.dma("sp", qT[:], self.QT[h * 256:(h + 1) * 256, tok0:tok0 + 128].rearrange("(dt p) t -> p dt t", p=128),
                  reads=["QT"], writes=[qT.name])
            S.dma("sp", kT[:], self.KTd[h * 256:(h + 1) * 256, tok0:tok0 + 128].rearrange("(dt p) t -> p dt t", p=128),
                  reads=["KTd"], writes=[kT.name])
            S.dma("sp", v[:], self.VOZ[0, tok0:tok0 + 128, h * 512:(h + 1) * 512], reads=["VOZ"], writes=[v.name])
            yield
            def trk(e, kT=kT):
                for dt_ in range(2):
                    r = e.transpose(self.ptb[:, dt_ * 128:(dt_ + 1) * 128], kT[:, dt_, :], self.identb[:])
                return r
            S.op("pe", trk, reads=[kT.name, "identb"], writes=["ptb"])
            S.op("act", lambda e: e.copy(out=ktm[:], in_=self.ptb[:, 0:256]), reads=["ptb"], writes=[A["ktm"].name])
            wcol = toksc[:, d, col:col + 1]
            S.op("dve", lambda e, v=v, wcol=wcol: e.tensor_scalar(out=vx[:, 0:512], in0=v[:], scalar1=wcol,
                                                                  scalar2=None, op0=ALU.mult),
                 reads=[v.name, "toksc"], writes=[A["vx"].name])
            S.op("dve", lambda e, wcol=wcol: e.tensor_copy(vx[:, 512:513], wcol), reads=["toksc"], writes=[A["vx"].name])
            yield
            S.op("dve", lambda e, col=col: e.tensor_scalar(out=Cst[:], in0=Cst[:], scalar1=decB[:, d, col:col + 1],
                                                           scalar2=None, op0=ALU.mult),
                 reads=[Cst.name, "decB"], writes=[Cst.name])
            S.op("act", lambda e: e.copy(out=Cb[:], in_=Cst[:]), reads=[Cst.name], writes=[A["Cb"].name])
            yield
            pS = self.psum()

            def mmS(e, pS=pS, kT=kT, qT=qT):
                for dt_ in range(2):
                    r = e.matmul(pS[:, 0:128], kT[:, dt_, :], qT[:, dt_, :], start=(dt_ == 0), stop=(dt_ == 1))
                return r
            S.op("pe", mmS, reads=[kT.name, qT.name], writes=[pS.name])
            S.op("dve", lambda e, pS=pS: e.tensor_tensor(out=sm[:], in0=pS[:, 0:128], in1=mask[:], op=ALU.mult),
                 reads=[pS.name, mask.name], writes=[A["sm"].name])
            yield
            pnum = self.psum()
            pden = self.psum()

            def mmN(e, pnum=pnum, pden=pden, qT=qT):
                e.matmul(pnum[:, :], sm[:], vx[:, 0:512], start=True, stop=False)
                e.matmul(pnum[:, :], qT[:, 0, :], Cb[:, 0, 0:512], start=False, stop=False)
                e.matmul(pnum[:, :], qT[:, 1, :], Cb[:, 1, 0:512], start=False, stop=True)
                e.matmul(pden[:, 0:1], sm[:], vx[:, 512:513], start=True, stop=False)
                e.matmul(pden[:, 0:1], qT[:, 0, :], Cb[:, 0, 512:513], start=False, stop=False)
                return e.matmul(pden[:, 0:1], qT[:, 1, :], Cb[:, 1, 512:513], start=False, stop=True)
            S.op("pe", mmN, reads=[A["sm"].name, A["vx"].name, qT.name, A["Cb"].name], writes=[pnum.name, pden.name])
            rr = A["sc8"][:, 4:5]
            S.op("act", lambda e, pden=pden: e.activation(out=rr, in_=pden[:, 0:1], func=AF.Abs),
                 reads=[pden.name], writes=[A["sc8"].name + "_rr"])
            S.op("dve", lambda e, col=col: e.tensor_tensor(out=rr, in0=rr, in1=toksc[:, 2 + d, col:col + 1],
                                                           op=ALU.max), reads=[A["sc8"].name + "_rr", "toksc"],
                 writes=[A["sc8"].name + "_rr"])
            S.op("dve", lambda e: e.reciprocal(out=rr, in_=rr), reads=[A["sc8"].name + "_rr"], writes=[A["sc8"].name + "_rr"])
            hsb = A["HSb"][u % 2]
            hsd = self.OSD[A["sid"], c * 128:(c + 1) * 128, :]
            hskey = "OSD%d_%d" % (A["sid"], c)
            if d == 1:
                S.op("dve", lambda e, pnum=pnum, hsb=hsb: e.tensor_scalar(out=hsb[:], in0=pnum[:, :], scalar1=rr,
                                                                          scalar2=None, op0=ALU.mult),
                     reads=[pnum.name, A["sc8"].name + "_rr"], writes=[hsb.name])
                S.defer_dma("sp", hsd, hsb[:], reads=[hsb.name], writes=[hskey])
            else:
                self.ml_finalize(j, h, c, tok0, pnum, rr, A, u)
            yield
            pD0, pD1 = self.psum(), self.psum()

            def mmD(e, pD0=pD0, pD1=pD1):
                e.matmul(pD0[:, :], ktm[:, 0:128], vx[:, 0:512], start=True, stop=True)
                e.matmul(pD1[:, :], ktm[:, 128:256], vx[:, 0:512], start=True, stop=True)
                e.matmul(self.pmisc[:, 300:301], ktm[:, 0:128], vx[:, 512:513], start=True, stop=True)
                return e.matmul(self.pmisc[:, 301:302], ktm[:, 128:256], vx[:, 512:513], start=True, stop=True)
            S.op("pe", mmD, reads=[A["ktm"].name, A["vx"].name], writes=[pD0.name, pD1.name, "pmisc"])
            S.op("dve", lambda e, pD0=pD0: e.tensor_tensor(out=Cst[:, 0, 0:512], in0=Cst[:, 0, 0:512], in1=pD0[:, :],
                                                           op=ALU.add), reads=[pD0.name, Cst.name], writes=[Cst.name])
            S.op("dve", lambda e, pD1=pD1: e.tensor_tensor(out=Cst[:, 1, 0:512], in0=Cst[:, 1, 0:512], in1=pD1[:, :],
                                                           op=ALU.add), reads=[pD1.name, Cst.name], writes=[Cst.name])
            S.op("dve", lambda e: e.tensor_tensor(out=Cst[:, :, 512], in0=Cst[:, :, 512], in1=self.pmisc[:, 300:302],
                                                  op=ALU.add), reads=["pmisc", Cst.name], writes=[Cst.name])
        if s["kind"] == "p":
            pi = s["pi"]
            S.dma("sp", self.st_C[pi, j, d, h].rearrange("(dt p) v -> p dt v", p=128), Cst[:, :, 0:512],
                  reads=[Cst.name], writes=["st_C"])
            S.dma("sp", self.st_n[pi, j, d, h].rearrange("(dt p) -> p dt", p=128), Cst[:, :, 512],
                  reads=[Cst.name], writes=["st_n"], slow=True)

    def ml_finalize(self, j, h, c, tok0, pnum, rr, A, u):
        S = self.S
        hs, sig, sz, ybf = A["hs"], A["sig"], A["sz"], A["ybf"]
        o, z = A["o"][u % 2], A["z"][u % 2]
        ytr = A["ytr"][u % 2]
        S.dma("sp", o[:], self.VOZ[1, tok0:tok0 + 128, h * 512:(h + 1) * 512], reads=["VOZ"], writes=[o.name])
        S.dma("sp", z[:], self.VOZ[2, tok0:tok0 + 128, h * 512:(h + 1) * 512], reads=["VOZ"], writes=[z.name])
        hsb = A["HSb"][u % 2]
        hskey = "OSD%d_%d" % (A["sid"], c)
        S.dma("sp", hsb[:], self.OSD[A["sid"], c * 128:(c + 1) * 128, :], reads=[hskey], writes=[hsb.name])
        S.op("dve", lambda e: e.scalar_tensor_tensor(out=hs[:], in0=pnum[:, :], scalar=rr, in1=hsb[:],
                                                     op0=ALU.mult, op1=ALU.add),
             reads=[pnum.name, A["sc8"].name + "_rr", hsb.name], writes=[A["hs"].name])
        ss = A["sc8"][:, 0:1]
        rs = A["sc8"][:, 1:2]
        skey = A["sc8"].name
        S.op("act", lambda e: e.activation(out=sig[:], in_=hs[:], func=AF.Square, accum_out=ss),
             reads=[A["hs"].name], writes=[A["sig"].name, skey])
        self.rstd_col(ss, rs, ML_DV, key=skey)
        S.op("dve", lambda e: e.scalar_tensor_tensor(out=hs[:], in0=hs[:], scalar=rs,
                                                     in1=A["ghead"][:, :], op0=ALU.mult,
                                                     op1=ALU.mult), reads=[A["hs"].name, skey, A["ghead"].name],
             writes=[A["hs"].name])
        S.op("act", lambda e: e.activation(out=sig[:], in_=o[:], func=AF.Sigmoid), reads=[o.name], writes=[A["sig"].name])
        S.op("act", lambda e: e.activation(out=sz[:], in_=z[:], func=AF.Silu), reads=[z.name], writes=[A["sz"].name])
        S.op("dve", lambda e: e.tensor_tensor(out=hs[:], in0=hs[:], in1=sig[:], op=ALU.mult),
             reads=[A["hs"].name, A["sig"].name], writes=[A["hs"].name])
        S.op("dve", lambda e: e.tensor_tensor(out=ybf[:], in0=hs[:], in1=sz[:], op=ALU.mult),
             reads=[A["hs"].name, A["sz"].name], writes=[A["ybf"].name])

        def tr(e):
            for i in range(4):
                r = e.transpose(self.ptb[:, 512 + i * 128:512 + (i + 1) * 128], ybf[:, i * 128:(i + 1) * 128],
                                self.identb[:])
            return r
        S.op("pe", tr, reads=[A["ybf"].name, "identb"], writes=["ptb"])
        S.op("act", lambda e: e.copy(out=ytr[:], in_=self.ptb[:, 512:1024]), reads=["ptb"], writes=[ytr.name])
        S.defer_dma("sp", self.YT[h * 512:(h + 1) * 512, tok0:tok0 + 128].rearrange("(i p) t -> p i t", p=128),
                    ytr[:].rearrange("p (i t) -> p i t", i=4), reads=[ytr.name], writes=["YT"])

    def gd_proj(self, sb):
        S, ins = self.S, self.ins
        W = ins["w_gd_in"]
        NTB = self.NT // 512
        NTT = self.NT // 128
        wconv = sb("gwconv", [128, 64, 3], F32)
        accs = [sb("gacc%d" % i, [128, 512], F32) for i in range(2)]
        sacts = [sb("gsact%d" % i, [128, 512], F32) for i in range(2)]
        sqs = [sb("gsq%d" % i, [128, 512], F32) for i in range(2)]
        rns = [sb("grn%d" % i, [128, 512], F32) for i in range(2)]
        sbfs = [sb("gsbf%d" % i, [128, 512], BF16) for i in range(2)]
        gi = 0
        S.dma("sp", wconv[:], ins["w_gd_conv"][:, :, :], writes=["gwconv"])
        wi = 0
        ei = 0
        for g in range(16):
            wb = self.load_w(W, g * 512, 512, wi)
            wi += 1
            for ci in range(4):
                ct = g * 4 + ci
                for tb in range(NTB):
                    roww = self.roww_of_block(tb)
                    acc, sact, sq, rn, sbf = accs[gi % 2], sacts[gi % 2], sqs[gi % 2], rns[gi % 2], sbfs[gi % 2]
                    gi += 1
                    p = self.psum()
                    self.proj_fm(wb, ci * 128, 128, tb, p)
                    w0, w1, w2 = (wconv[:, ct, i:i + 1] for i in range(3))
                    p3 = p[:, :].rearrange("p (r w) -> p r w", w=roww)
                    acc3 = acc[:].rearrange("p (r w) -> p r w", w=roww)
                    S.op("dve", lambda e, p=p, w1=w1, acc=acc: e.tensor_scalar(out=acc[:], in0=p[:, :], scalar1=w1, scalar2=None,
                                                                     op0=ALU.mult), reads=[p.name, "gwconv"],
                         writes=[acc.name])
                    S.op("dve", lambda e, p3=p3, acc3=acc3, w0=w0: e.scalar_tensor_tensor(
                        out=acc3[:, :, 1:], in0=p3[:, :, :-1], scalar=w0, in1=acc3[:, :, 1:], op0=ALU.mult,
                        op1=ALU.add), reads=[p.name, acc.name, "gwconv"], writes=[acc.name])
                    S.op("dve", lambda e, p3=p3, acc3=acc3, w2=w2: e.scalar_tensor_tensor(
                        out=acc3[:, :, :-1], in0=p3[:, :, 1:], scalar=w2, in1=acc3[:, :, :-1], op0=ALU.mult,
                        op1=ALU.add), reads=[p.name, acc.name, "gwconv"], writes=[acc.name])
                    if ct < 32:
                        S.op("act", lambda e, sact=sact, acc=acc: e.activation(out=sact[:], in_=acc[:], func=AF.Silu), reads=[acc.name],
                             writes=[sact.name])
                        S.op("act", lambda e, sq=sq, sact=sact: e.activation(out=sq[:], in_=sact[:], func=AF.Square), reads=[sact.name],
                             writes=[sq.name])
                        p2 = self.psum()
                        S.op("pe", lambda e, p2=p2, sq=sq: e.matmul(p2[:, :], self.ones[:, :], sq[:], start=True, stop=True),
                             reads=["ones_sb", sq.name], writes=[p2.name])
                        S.op("dve", lambda e, p2=p2, rn=rn: e.tensor_scalar(out=rn[:], in0=p2[:, :], scalar1=EPS, scalar2=None,
                                                                     op0=ALU.add), reads=[p2.name], writes=[rn.name])
                        S.op("act", lambda e, rn=rn: e.activation(out=rn[:], in_=rn[:], func=AF.Ln), reads=[rn.name],
                             writes=[rn.name])
                        S.op("act", lambda e, rn=rn: e.activation(out=rn[:], in_=rn[:], func=AF.Exp, scale=-0.5),
                             reads=[rn.name], writes=[rn.name])
                        sg = self.stage()
                        scl = (128 ** -0.5) if ct < 16 else 1.0
                        S.op("dve", lambda e, sg=sg, scl=scl, sact=sact, rn=rn: e.scalar_tensor_tensor(
                            out=sg[:], in0=sact[:], scalar=scl, in1=rn[:], op0=ALU.mult, op1=ALU.mult),
                            reads=[sact.name, rn.name], writes=[sg.name])
                        dst = self.GQT if ct < 16 else self.GKT
                        r0 = (ct % 16) * 128
                        S.dma("sp", dst[r0:r0 + 128, tb * 512:(tb + 1) * 512], sg[:], reads=[sg.name],
                              writes=[dst.name])
                    else:
                        hv = ct - 32
                        S.op("act", lambda e, sbf=sbf, acc=acc: e.activation(out=sbf[:], in_=acc[:], func=AF.Silu), reads=[acc.name],
                             writes=[sbf.name])

                        def tr(e, sbf=sbf):
                            for i in range(4):
                                r = e.transpose(self.ptb[:, i * 128:(i + 1) * 128], sbf[:, i * 128:(i + 1) * 128],
                                                self.identb[:])
                            return r
                        S.op("pe", tr, reads=[sbf.name, "identb"], writes=["ptb"])
                        sg = self.stage()
                        self.evac(ei, sg[:], self.ptb[:, 0:512], ["ptb"], [sg.name])
                        ei += 1
                        S.dma("sp", self.GV[tb * 512:(tb + 1) * 512, hv * 128:(hv + 1) * 128].rearrange(
                            "(a p) v -> p a v", p=128), sg[:].rearrange("p (a v) -> p a v", a=4), reads=[sg.name],
                            writes=["GV"])
        for g in range(8):
            wb = self.load_w(W, 8192 + g * 512, 512, wi)
            wi += 1
            for tt in range(NTT):
                p = self.psum()
                self.proj_tm(wb, tt, p)
                sg = self.stage()
                self.evac(ei, sg[:], p[:, :], [p.name], [sg.name])
                ei += 1
                S.dma("sp", self.GZ[tt * 128:(tt + 1) * 128, g * 512:(g + 1) * 512], sg[:], reads=[sg.name],
                      writes=["GZ"])
        wb = self.load_w(W, 12288, 128, wi)
        for typ in range(4):
            for tb in range(NTB):
                p = self.psum()
                self.proj_fm(wb, typ * 32, 32, tb, p)
                sg = self.stgf[(typ * NTB + tb) % 2]
                S.op("act", lambda e, p=p, sg=sg: e.copy(out=sg[0:32, :], in_=p[0:32, :]), reads=[p.name],
                     writes=[sg.name])
                S.dma("sp", self.GAB[typ, :, tb * 512:(tb + 1) * 512], sg[0:32, :], reads=[sg.name],
                      writes=["GAB"])

    def gd_scan(self, sb):
        S, ins = self.S, self.ins
        A = {}
        nrt = max(1, self.Tmax // 512)
        f = lambda n, shp, dt=F32: A.__setitem__(n, sb("g_" + n, shp, dt))
        f("ga", [128, 4, 128])
        f("sp", [128, 2, 128])
        f("ng", [128, 2, 128])
        f("nG", [128, 2, 128])
        f("bt", [128, 2, 128])
        f("kd", [128, 2, 128])
        f("gc", [128, 16])
        f("par", [128, 4])
        f("toksc", [128, nrt * 2 * 5, 128])
        f("decB", [128, nrt * 2, 128])
        f("diag", [128, 128])
        f("gnB", [128, 128])
        NS = 3
        AS = []
        for k in range(NS):
            B = dict(A)
            g = lambda n, shp, dt=F32, B=B, k=k: B.__setitem__(n, sb("g_%s_s%d" % (n, k), shp, dt))
            B["OSb"] = [sb("g_OSb%d_s%d" % (i, k), [128, 512], F32) for i in range(2)]
            B["sid"] = k
            B["pp"] = self.pp[2 * k:2 * k + 2]
            B["ppi"] = 0
            g("Sst", [128, 4, 128])
            g("Sb", [128, 4, 128], BF16)
            g("diagG", [128, 4, 128])
            g("dd", [128, 4, 128])
            g("dec", [128, 4, 128])
            g("t1", [128, 4, 128])
            g("ktm", [128, 2, 128], BF16)
            for n in ("P", "PT", "N", "kb", "vb", "WTn", "U", "kdk", "y"):
                g(n, [128, 4, 128], BF16)
            for n in ("Q", "Tm", "TTm", "kT", "qT", "v", "z", "ytr", "Lb", "LTb", "Yb", "Y2b"):
                B[n] = [sb("g_%s%d_s%d" % (n, i, k), [128, 2, 128] if n in ("kT", "qT") else [128, 4, 128], BF16)
                        for i in range(2)]
            g("o", [128, 4, 128])
            g("sq", [128, 4, 128])
            g("ssq", [128, 8])
            B["uc"] = 0
            AS.append(B)
        S.dma("sp", A["gnB"][:], ins["g_gd_norm"][0:1, :].partition_broadcast(128), writes=["g_gnB"])
        S.dma("sp", A["par"][:], ins["gd_par"][:, :], writes=["g_par"])
        S.op("act", lambda e: e.activation(out=A["par"][:, 0:2], in_=A["par"][:, 0:2], func=AF.Exp),
             reads=["g_par"], writes=["g_par"])
        for s in self.seqs:
            self.gd_gates(s, A)

            def stream(k, s=s):
                for hg in range(k, 8, NS):
                    for d in (1, 0):
                        yield from self.gd_group_dir(s, hg, d, AS[k])
            run_streams([stream(k) for k in range(NS)], S)

    def gd_gates(self, s, A):
        S, ins = self.S, self.ins
        T, t0 = s["T"], s["t0"]
        nch = T // 128
        ga, sp, ng, nG, bt, kd, gc, par, toksc, decB = (A[k] for k in (
            "ga", "sp", "ng", "nG", "bt", "kd", "gc", "par", "toksc", "decB"))
        pm = self.pmisc
        for rt in range((nch + 3) // 4):
            ncl = min(4, nch - rt * 4)
            R = ncl * 32
            for cl in range(ncl):
                c = rt * 4 + cl
                for typ in range(4):
                    S.dma("sp", ga[cl * 32:(cl + 1) * 32, typ, :], self.GAB[typ, :, t0 + c * 128:t0 + (c + 1) * 128],
                          reads=["GAB"], writes=["g_ga"])
            for d in range(2):
                S.op("act", lambda e, d=d: e.activation(out=sp[0:R, d, :], in_=ga[0:R, d, :], func=AF.Exp,
                                                        bias=par[0:R, 2 + d:3 + d], scale=1.0),
                     reads=["g_ga", "g_par"], writes=["g_sp"])
            S.op("dve", lambda e: e.tensor_scalar(out=sp[0:R], in0=sp[0:R], scalar1=1.0, scalar2=None, op0=ALU.add),
                 reads=["g_sp"], writes=["g_sp"])
            S.op("act", lambda e: e.activation(out=sp[0:R], in_=sp[0:R], func=AF.Ln), reads=["g_sp"],
                 writes=["g_sp"])
            for d in range(2):
                S.op("dve", lambda e, d=d: e.tensor_scalar(out=ng[0:R, d, :], in0=sp[0:R, d, :],
                                                           scalar1=par[0:R, d:d + 1], scalar2=None, op0=ALU.mult),
                     reads=["g_sp", "g_par"], writes=["g_ng"])
            S.op("act", lambda e: e.activation(out=bt[0:R], in_=ga[0:R, 2:4, :], func=AF.Sigmoid),
                 reads=["g_ga"], writes=["g_bt"])
            for d in range(2):
                S.op("dve", lambda e, d=d: e.tensor_tensor_scan(out=nG[0:R, d, :], data0=self.ones[0:R, :],
                                                                data1=ng[0:R, d, :], initial=0.0, op0=ALU.mult,
                                                                op1=ALU.add),
                     reads=["g_ng", "ones_sb"], writes=["g_nG"])
            S.op("dve", lambda e: e.tensor_copy(gc[0:R, 8:9], nG[0:R, 1, 127:128]), reads=["g_nG"], writes=["g_gc"])
            S.op("dve", lambda e: e.tensor_tensor(out=nG[0:R, 1, :], in0=ng[0:R, 1, :], in1=nG[0:R, 1, :],
                                                  op=ALU.subtract), reads=["g_ng", "g_nG"], writes=["g_nG"])
            S.op("dve", lambda e: e.tensor_scalar(out=nG[0:R, 1, :], in0=nG[0:R, 1, :], scalar1=gc[0:R, 8:9],
                                                  scalar2=None, op0=ALU.add), reads=["g_nG", "g_gc"],
                 writes=["g_nG"])
            S.op("dve", lambda e: e.tensor_copy(gc[0:R, 0:1], nG[0:R, 0, 127:128]), reads=["g_nG"], writes=["g_gc"])
            S.op("dve", lambda e: e.tensor_copy(gc[0:R, 1:2], nG[0:R, 1, 0:1]), reads=["g_nG"], writes=["g_gc"])
            S.op("act", lambda e: e.activation(out=gc[0:R, 4:6], in_=gc[0:R, 0:2], func=AF.Exp, scale=-1.0),
                 reads=["g_gc"], writes=["g_gc"])
            S.op("dve", lambda e: e.tensor_scalar(out=gc[0:R, 2:4], in0=gc[0:R, 0:2], scalar1=-1.0, scalar2=None,
                                                  op0=ALU.mult), reads=["g_gc"], writes=["g_gc"])
            for d in range(2):
                S.op("act", lambda e, d=d: e.activation(out=kd[0:R, d, :], in_=nG[0:R, d, :], func=AF.Exp,
                                                        bias=gc[0:R, 2 + d:3 + d], scale=1.0),
                     reads=["g_nG", "g_gc"], writes=["g_kd"])
            for d in range(2):
                base = (rt * 2 + d) * 5
                for q, src in enumerate((nG, bt, kd)):
                    S.op("pe", lambda e, q=q, src=src, d=d: e.transpose(pm[:, q * 128:q * 128 + R], src[0:R, d, :],
                                                                        self.ident[0:R, 0:R]),
                         reads=[src.name, "ident_sb"], writes=["pmisc"])
                S.op("act", lambda e, base=base: e.copy(out=toksc[:, base:base + 3, 0:R],
                                                        in_=pm[:, 0:384].rearrange("p (q r) -> p q r", q=3)[:, :, 0:R]),
                     reads=["pmisc"], writes=["g_toksc"])
                S.op("act", lambda e, base=base: e.activation(out=toksc[:, base + 3, 0:R], in_=toksc[:, base, 0:R],
                                                              func=AF.Exp, scale=-1.0), reads=["g_toksc"],
                     writes=["g_toksc"])
                S.op("dve", lambda e, base=base: e.tensor_tensor(out=toksc[:, base + 4, 0:R], in0=toksc[:, base + 1, 0:R],
                                                                 in1=toksc[:, base + 3, 0:R], op=ALU.mult),
                     reads=["g_toksc"], writes=["g_toksc"])
                S.op("dve", lambda e, d=d: e.tensor_scalar(out=A["diag"][0:R, 0:R], in0=self.ident[0:R, 0:R],
                                                           scalar1=gc[0:R, 4 + d:5 + d], scalar2=None, op0=ALU.mult),
                     reads=["ident_sb", "g_gc"], writes=["g_diag"])
                p = self.psum()
                S.op("pe", lambda e, p=p: e.matmul(p[:, 0:R], self.ones[0:R, :], A["diag"][0:R, 0:R], start=True,
                                                   stop=True), reads=["ones_sb", "g_diag"], writes=[p.name])
                S.op("act", lambda e, p=p, rt=rt, d=d: e.copy(out=decB[:, rt * 2 + d, 0:R], in_=p[:, 0:R]),
                     reads=[p.name], writes=["g_decB"])

    def gd_group_dir(self, s, hg, d, A):
        S, ins = self.S, self.ins
        T, t0 = s["T"], s["t0"]
        nch = T // 128
        Sst, Sb = A["Sst"], A["Sb"]
        toksc, decB = A["toksc"], A["decB"]
        hv0 = hg * 4
        b4 = lambda ap: ap.unsqueeze(2).to_broadcast([128, 4, 128])
        m4 = lambda ap: ap.unsqueeze(1).to_broadcast([128, 4, 128])
        if s["kind"] == "s":
            S.dma("sp", Sst[:], ins["gdS0"][d, hv0:hv0 + 4].rearrange("i k v -> k i v"), writes=[A["Sst"].name])
        else:
            S.op("dve", lambda e: e.memset(Sst[:], 0.0), writes=[A["Sst"].name])
        S.op("act", lambda e: e.copy(out=Sb[:], in_=Sst[:]), reads=[A["Sst"].name], writes=[A["Sb"].name])
        order = list(range(nch)) if d == 0 else list(range(nch - 1, -1, -1))
        mincl = self.triu if d == 0 else self.tril
        mstr = self.triuS if d == 0 else self.trilS
        for c in order:
            u = A["uc"]
            A["uc"] += 1
            tok0 = t0 + c * 128
            rt, cl = c // 4, c % 4
            base = (rt * 2 + d) * 5
            r0 = cl * 32 + hv0
            col = lambda q: toksc[:, base + q, r0:r0 + 4]
            kT, qT, v = A["kT"][u % 2], A["qT"][u % 2], A["v"][u % 2]
            S.dma("sp", kT[:], self.GKT[hg * 256:(hg + 1) * 256, tok0:tok0 + 128].rearrange("(a p) t -> p a t", p=128),
                  reads=["GKT"], writes=[kT.name])
            S.dma("sp", qT[:], self.GQT[hg * 256:(hg + 1) * 256, tok0:tok0 + 128].rearrange("(a p) t -> p a t", p=128),
                  reads=["GQT"], writes=[qT.name])
            S.dma("sp", v[:], self.GV[tok0:tok0 + 128, hv0 * 128:(hv0 + 4) * 128].rearrange("t (i v) -> t i v", i=4),
                  reads=["GV"], writes=[v.name])
            ktm = A["ktm"]

            def trk(e, kT=kT):
                for a in range(2):
                    r = e.transpose(self.ptb[:, a * 128:(a + 1) * 128], kT[:, a, :], self.identb[:])
                return r
            S.op("pe", trk, reads=[kT.name, "identb"], writes=["ptb"])
            S.op("act", lambda e: e.copy(out=ktm[:], in_=self.ptb[:, 0:256].rearrange("p (a k) -> p a k", a=2)),
                 reads=["ptb"], writes=[A["ktm"].name])
            yield
            pK = self.psum_s(A)
            pQ = pK

            def mmG(e, pK=pK, kT=kT, qT=qT):
                for a in range(2):
                    e.matmul(pK[:, a * 128:(a + 1) * 128], kT[:, a, :], kT[:, a, :], start=True, stop=True)
                for a in range(2):
                    r = e.matmul(pK[:, 256 + a * 128:256 + (a + 1) * 128], qT[:, a, :], kT[:, a, :], start=True,
                                 stop=True)
                return r
            S.op("pe", mmG, reads=[kT.name, qT.name], writes=[pK.name])
            yield
            diagG, dd, dec, t1 = A["diagG"], A["dd"], A["dec"], A["t1"]
            S.op("dve", lambda e: e.tensor_tensor(out=diagG[:], in0=m4(self.ident[:, :]), in1=b4(col(0)), op=ALU.mult),
                 reads=["ident_sb", "g_toksc"], writes=[A["diagG"].name])
            pG = self.psum_s(A)
            S.op("pe", lambda e, pG=pG: e.matmul(pG[:, :], self.ones[:, :], diagG[:].rearrange("p i s -> p (i s)"),
                                                 start=True, stop=True), reads=["ones_sb", A["diagG"].name],
                 writes=[pG.name])
            pG3 = pG[:, :].rearrange("p (i s) -> p i s", i=4)
            S.op("dve", lambda e, pG3=pG3: e.tensor_tensor(out=dd[:], in0=pG3, in1=b4(col(0)), op=ALU.subtract),
                 reads=[pG.name, "g_toksc"], writes=[A["dd"].name])
            S.op("dve", lambda e: e.tensor_scalar(out=dd[:], in0=dd[:], scalar1=0.0, scalar2=None, op0=ALU.min),
                 reads=[A["dd"].name], writes=[A["dd"].name])
            S.op("act", lambda e: e.activation(out=dec[:], in_=dd[:], func=AF.Exp), reads=[A["dd"].name],
                 writes=[A["dec"].name])
            yield
            P, PT, N = A["P"], A["PT"], A["N"]
            pQ4 = pQ[:, 256:512].rearrange("p (a s) -> p a s", a=2).unsqueeze(2).to_broadcast([128, 2, 2, 128])
            pK4 = pK[:, 0:256].rearrange("p (a s) -> p a s", a=2).unsqueeze(2).to_broadcast([128, 2, 2, 128])
            v4 = lambda t: t[:].rearrange("p (a b) s -> p a b s", a=2)
            S.op("dve", lambda e, pQ4=pQ4: e.tensor_tensor(out=v4(t1), in0=pQ4, in1=v4(dec), op=ALU.mult),
                 reads=[pQ.name, A["dec"].name], writes=[A["t1"].name])
            S.op("pool", lambda e: e.tensor_tensor(out=P[:], in0=t1[:], in1=m4(mincl[:, :]), op=ALU.mult),
                 reads=[A["t1"].name, mincl.name], writes=[A["P"].name])
            S.op("dve", lambda e, pK4=pK4: e.tensor_tensor(out=v4(t1), in0=pK4, in1=v4(dec), op=ALU.mult),
                 reads=[pK.name, A["dec"].name], writes=[A["t1"].name])
            S.op("dve", lambda e: e.tensor_tensor(out=t1[:], in0=t1[:], in1=m4(mstr[:, :]), op=ALU.mult),
                 reads=[A["t1"].name, mstr.name], writes=[A["t1"].name])
            S.op("dve", lambda e: e.scalar_tensor_tensor(out=N[:], in0=t1[:], scalar=-1.0, in1=b4(col(1)),
                                                         op0=ALU.mult, op1=ALU.mult),
                 reads=[A["t1"].name, "g_toksc"], writes=[A["N"].name])
            yield
            Qs = A["Q"]

            def trN(e):
                for i in range(4):
                    e.transpose(self.ptb[:, i * 128:(i + 1) * 128], N[:, i, :], self.identb[:])
                for i in range(4):
                    r = e.transpose(self.ptb[:, 512 + i * 128:512 + (i + 1) * 128], P[:, i, :], self.identb[:])
                return r
            S.op("pe", trN, reads=[A["N"].name, A["P"].name, "identb"], writes=["ptb"])
            ptN = self.ptb[:, 0:512].rearrange("p (i s) -> p i s", i=4)
            ptP = self.ptb[:, 512:1024].rearrange("p (i s) -> p i s", i=4)
            Q = Qs[u % 2]
            S.op("act", lambda e: e.copy(out=Q[:], in_=ptN), reads=["ptb"], writes=[Q.name])
            S.op("act", lambda e: e.copy(out=PT[:], in_=ptP), reads=["ptb"], writes=[A["PT"].name])
            Tm, TTm = A["Tm"], A["TTm"]
            T, TT = Tm[0], TTm[0]
            for lev in range(7):
                mL = self.lvl[:, lev, :] if d == 0 else self.lvlT[:, lev, :]
                mLT = self.lvlT[:, lev, :] if d == 0 else self.lvl[:, lev, :]
                Tn, TTn = Tm[(lev + 1) % 2], TTm[(lev + 1) % 2]
                Lb, LTb, Yb, Y2b = A["Lb"][lev % 2], A["LTb"][lev % 2], A["Yb"][lev % 2], A["Y2b"][lev % 2]
                last = lev == 6
                if not last:
                    S.op("pool", lambda e, mL=mL: e.tensor_tensor(out=Lb[:], in0=N[:], in1=m4(mL), op=ALU.mult),
                         reads=[A["N"].name, "lvl"], writes=[Lb.name])
                S.op("pool", lambda e, mLT=mLT: e.tensor_tensor(out=LTb[:], in0=Q[:], in1=m4(mLT), op=ALU.mult),
                     reads=[Q.name, "lvl"], writes=[LTb.name])
                if lev == 0:
                    S.op("dve", lambda e, Tn=Tn: e.tensor_tensor(out=Tn[:], in0=Lb[:], in1=m4(self.ident[:, :]),
                                                                 op=ALU.add),
                         reads=[Lb.name, "ident_sb"], writes=[Tn.name])
                    S.op("dve", lambda e, TTn=TTn: e.tensor_tensor(out=TTn[:], in0=LTb[:], in1=m4(self.ident[:, :]),
                                                                   op=ALU.add),
                         reads=[LTb.name, "ident_sb"], writes=[TTn.name])
                    T, TT = Tn, TTn
                    continue
                yield
                py, py2 = self.psum_s(A), self.psum_s(A)

                def mmy(e, py=py, py2=py2, T=T, TT=TT, last=last):
                    for i in range(4):
                        r = e.matmul(py[:, i * 128:(i + 1) * 128], LTb[:, i, :], T[:, i, :], start=True, stop=True)
                    if not last:
                        for i in range(4):
                            r = e.matmul(py2[:, i * 128:(i + 1) * 128], Lb[:, i, :], TT[:, i, :], start=True,
                                         stop=True)
                    return r
                S.op("pe", mmy, reads=[Lb.name, LTb.name, T.name, TT.name], writes=[py.name, py2.name])
                S.op("act", lambda e, py=py: e.copy(out=Yb[:].rearrange("p i s -> p (i s)"), in_=py[:, :]),
                     reads=[py.name], writes=[Yb.name])
                if not last:
                    S.op("act", lambda e, py2=py2: e.copy(out=Y2b[:].rearrange("p i s -> p (i s)"), in_=py2[:, :]),
                         reads=[py2.name], writes=[Y2b.name])
                yield
                px, pxt = self.psum_s(A), self.psum_s(A)

                def mmx(e, px=px, pxt=pxt, T=T, TT=TT, last=last):
                    if not last:
                        for i in range(4):
                            e.matmul(px[:, i * 128:(i + 1) * 128], Y2b[:, i, :], T[:, i, :], start=True, stop=True)
                    for i in range(4):
                        r = e.matmul(pxt[:, i * 128:(i + 1) * 128], Yb[:, i, :], TT[:, i, :], start=True, stop=True)
                    return r
                S.op("pe", mmx, reads=[Yb.name, Y2b.name, T.name, TT.name], writes=[px.name, pxt.name])
                if not last:
                    S.op("dve", lambda e, px=px, T=T, Tn=Tn: e.tensor_tensor(
                        out=Tn[:].rearrange("p i s -> p (i s)"), in0=px[:, :], in1=T[:].rearrange("p i s -> p (i s)"),
                        op=ALU.add), reads=[px.name, T.name], writes=[Tn.name])
                S.op("dve", lambda e, pxt=pxt, TT=TT, TTn=TTn: e.tensor_tensor(
                    out=TTn[:].rearrange("p i s -> p (i s)"), in0=pxt[:, :], in1=TT[:].rearrange("p i s -> p (i s)"),
                    op=ALU.add), reads=[pxt.name, TT.name], writes=[TTn.name])
                T, TT = Tn, TTn
            R = TT
            yield
            kb, vb, WTn, U, kdk = A["kb"], A["vb"], A["WTn"], A["U"], A["kdk"]
            k4 = ktm[:].unsqueeze(2).to_broadcast([128, 2, 2, 128])
            S.op("pool", lambda e: e.tensor_tensor(out=v4(kb), in0=k4,
                                                  in1=col(4).rearrange("p (a b) -> p a b", a=2).unsqueeze(3).to_broadcast(
                                                      [128, 2, 2, 128]), op=ALU.mult),
                 reads=[A["ktm"].name, "g_toksc"], writes=[A["kb"].name])
            S.op("pool", lambda e, v=v: e.tensor_tensor(out=vb[:], in0=v[:], in1=b4(col(1)), op=ALU.mult),
                 reads=[v.name, "g_toksc"], writes=[A["vb"].name])
            S.op("pool", lambda e: e.tensor_tensor(out=v4(kdk), in0=k4,
                                                  in1=col(2).rearrange("p (a b) -> p a b", a=2).unsqueeze(3).to_broadcast(
                                                      [128, 2, 2, 128]), op=ALU.mult),
                 reads=[A["ktm"].name, "g_toksc"], writes=[A["kdk"].name])
            pW = self.psum_s(A)

            def mmW(e, pW=pW, R=R):
                for i in range(4):
                    r = e.matmul(pW[:, i * 128:(i + 1) * 128], kb[:, i, :], R[:, i, :], start=True, stop=True)
                return r
            S.op("pe", mmW, reads=[A["kb"].name, R.name], writes=[pW.name])
            S.op("act", lambda e, pW=pW: e.activation(out=WTn[:].rearrange("p i s -> p (i s)"), in_=pW[:, :],
                                                      func=AF.Copy, scale=-1.0), reads=[pW.name], writes=[A["WTn"].name])
            yield
            pU = self.psum_s(A)

            def mmU(e, pU=pU, R=R):
                for i in range(4):
                    e.matmul(pU[:, i * 128:(i + 1) * 128], R[:, i, :], vb[:, i, :], start=True, stop=False)
                    r = e.matmul(pU[:, i * 128:(i + 1) * 128], WTn[:, i, :], Sb[:, i, :], start=False, stop=True)
                return r
            S.op("pe", mmU, reads=[R.name, A["vb"].name, A["WTn"].name, A["Sb"].name], writes=[pU.name])
            S.op("act", lambda e, pU=pU: e.copy(out=U[:].rearrange("p i s -> p (i s)"), in_=pU[:, :]),
                 reads=[pU.name], writes=[A["U"].name])
            yield
            pO1, pO2 = self.psum_s(A), self.psum_s(A)

            def mmO(e, pO1=pO1, pO2=pO2, qT=qT):
                for i in range(4):
                    e.matmul(pO1[:, i * 128:(i + 1) * 128], qT[:, i // 2, :], Sb[:, i, :], start=True, stop=True)
                for i in range(4):
                    r = e.matmul(pO2[:, i * 128:(i + 1) * 128], PT[:, i, :], U[:, i, :], start=True, stop=True)
                return r
            S.op("pe", mmO, reads=[qT.name, A["Sb"].name, A["PT"].name, A["U"].name], writes=[pO1.name, pO2.name])
            o = A["o"]
            o3 = lambda p: p[:, :].rearrange("p (i s) -> p i s", i=4)
            S.op("dve", lambda e, pO1=pO1: e.tensor_tensor(out=o[:], in0=o3(pO1), in1=b4(col(3)), op=ALU.mult),
                 reads=[pO1.name, "g_toksc"], writes=[A["o"].name])
            osb = A["OSb"][u % 2]
            OSc = osb[:].rearrange("p (i s) -> p i s", i=4)
            osd = self.OSD[A["sid"], c * 128:(c + 1) * 128, :]
            oskey = "OSD%d_%d" % (A["sid"], c)
            if d == 1:
                S.op("dve", lambda e, pO2=pO2, OSc=OSc: e.tensor_tensor(out=OSc, in0=o3(pO2), in1=o[:], op=ALU.add),
                     reads=[pO2.name, A["o"].name], writes=[osb.name])
                S.defer_dma("sp", osd, osb[:], reads=[osb.name], writes=[oskey])
            else:
                S.dma("sp", osb[:], osd, reads=[oskey], writes=[osb.name])
                S.op("dve", lambda e, pO2=pO2: e.tensor_tensor(out=o[:], in0=o3(pO2), in1=o[:], op=ALU.add),
                     reads=[pO2.name, A["o"].name], writes=[A["o"].name])
                S.op("dve", lambda e, OSc=OSc: e.tensor_tensor(out=o[:], in0=o[:], in1=OSc, op=ALU.add),
                     reads=[A["o"].name, osb.name], writes=[A["o"].name])
                self.gd_finalize(hv0, tok0, A, u)
            yield
            pS = self.psum_s(A)

            def mmS(e, pS=pS):
                for i in range(4):
                    r = e.matmul(pS[:, i * 128:(i + 1) * 128], kdk[:, i, :], U[:, i, :], start=True, stop=True)
                return r
            S.op("pe", mmS, reads=[A["kdk"].name, A["U"].name], writes=[pS.name])
            S.op("dve", lambda e, rt=rt, r0=r0: e.tensor_tensor(out=Sst[:], in0=Sst[:],
                                                                in1=b4(decB[:, rt * 2 + d, r0:r0 + 4]), op=ALU.mult),
                 reads=[A["Sst"].name, "g_decB"], writes=[A["Sst"].name])
            S.op("dve", lambda e, pS=pS: e.tensor_tensor(out=Sst[:], in0=Sst[:], in1=o3(pS), op=ALU.add),
                 reads=[A["Sst"].name, pS.name], writes=[A["Sst"].name])
            S.op("act", lambda e: e.copy(out=Sb[:], in_=Sst[:]), reads=[A["Sst"].name], writes=[A["Sb"].name])
        if s["kind"] == "p":
            S.dma("sp", self.st_S[s["pi"], 0, d, hv0:hv0 + 4].rearrange("i k v -> k i v"), Sst[:],
                  reads=[A["Sst"].name], writes=["st_S"])

    def gd_finalize(self, hv0, tok0, A, u):
        S = self.S
        o, sq, ssq, y = A["o"], A["sq"], A["ssq"], A["y"]
        z, ytr = A["z"][u % 2], A["ytr"][u % 2]
        b4 = lambda ap: ap.unsqueeze(2).to_broadcast([128, 4, 128])
        m4 = lambda ap: ap.unsqueeze(1).to_broadcast([128, 4, 128])
        S.dma("sp", z[:], self.GZ[tok0:tok0 + 128, hv0 * 128:(hv0 + 4) * 128].rearrange("t (i v) -> t i v", i=4),
              reads=["GZ"], writes=[z.name])
        S.op("act", lambda e: e.activation(out=sq[:], in_=o[:], func=AF.Square), reads=[A["o"].name], writes=[A["sq"].name])
        S.op("dve", lambda e: e.tensor_reduce(out=ssq[:, 0:4], in_=sq[:], axis=AX.X, op=ALU.add), reads=[A["sq"].name],
             writes=[A["ssq"].name])
        S.op("dve", lambda e: e.tensor_scalar(out=ssq[:, 0:4], in0=ssq[:, 0:4], scalar1=1.0 / 128, scalar2=EPS,
                                              op0=ALU.mult, op1=ALU.add), reads=[A["ssq"].name], writes=[A["ssq"].name])
        S.op("act", lambda e: e.activation(out=ssq[:, 0:4], in_=ssq[:, 0:4], func=AF.Sqrt), reads=[A["ssq"].name],
             writes=[A["ssq"].name])
        S.op("dve", lambda e: e.reciprocal(out=ssq[:, 0:4], in_=ssq[:, 0:4]), reads=[A["ssq"].name], writes=[A["ssq"].name])
        S.op("dve", lambda e: e.tensor_tensor(out=o[:], in0=o[:], in1=b4(ssq[:, 0:4]), op=ALU.mult),
             reads=[A["o"].name, A["ssq"].name], writes=[A["o"].name])
        S.op("dve", lambda e: e.tensor_tensor(out=o[:], in0=o[:], in1=m4(A["gnB"][:, :]), op=ALU.mult),
             reads=[A["o"].name, "g_gnB"], writes=[A["o"].name])
        S.op("act", lambda e, z=z: e.activation(out=sq[:], in_=z[:], func=AF.Silu), reads=[z.name], writes=[A["sq"].name])
        S.op("dve", lambda e: e.tensor_tensor(out=y[:], in0=o[:], in1=sq[:], op=ALU.mult), reads=[A["o"].name, A["sq"].name],
             writes=[A["y"].name])

        def tr(e):
            for i in range(4):
                r = e.transpose(self.ptb[:, i * 128:(i + 1) * 128], y[:, i, :], self.identb[:])
            return r
        S.op("pe", tr, reads=[A["y"].name, "identb"], writes=["ptb"])
        S.op("act", lambda e: e.copy(out=ytr[:].rearrange("p i s -> p (i s)"), in_=self.ptb[:, 0:512]),
             reads=["ptb"], writes=[ytr.name])
        S.defer_dma("sp", self.YT[hv0 * 128:(hv0 + 4) * 128, tok0:tok0 + 128].rearrange("(i p) t -> p i t", p=128),
                    ytr[:], reads=[ytr.name], writes=["YT"])

    def phase_d(self, l, wout, sb):
        S = self.S
        NTT = self.NT // 128
        gateb = sb("gateb", [128, D], F32)
        wd = sb("wd", [128, 32, D], BF16)
        ytile = [sb("ytile%d" % i, [128, 32, 128], BF16) for i in range(3)]
        xbs = [sb("xb%d" % i, [128, D], F32) for i in range(2)]
        tmp = [sb("tmpd%d" % i, [128, 512], F32) for i in range(2)]
        src = wout.rearrange("(kt p) c -> p kt c", p=128)
        for q in range(8):
            S.dma("pool", wd[:, q * 4:(q + 1) * 4, :], src[:, q * 4:(q + 1) * 4, :], writes=["wd_%d" % q])
        wkeys = ["wd_%d" % q for q in range(8)]
        cur = None
        k = 0
        for tt in range(NTT):
            c = self.tile_cond(tt)
            if c != cur:
                cur = c
                S.dma("sp", gateb[:], self.MOD[c:c + 1, 2 * D:3 * D].partition_broadcast(128), reads=["MOD"],
                      writes=["gateb"])
            yt = ytile[tt % 3]
            xb = xbs[tt % 2]
            S.dma("sp", yt[:], self.YT[:, tt * 128:(tt + 1) * 128].rearrange("(kt p) t -> p kt t", p=128),
                  reads=["YT"], writes=[yt.name])
            S.dma("sp", xb[:], self.X[tt * 128:(tt + 1) * 128, :], reads=["X"], writes=[xb.name])
            for cg in range(4):
                p = self.psum()

                def mm(e, p=p, yt=yt, cg=cg):
                    for kt in range(32):
                        r = e.matmul(p[:, :], yt[:, kt, :], wd[:, kt, cg * 512:(cg + 1) * 512], start=(kt == 0),
                                     stop=(kt == 31))
                    return r
                S.op("pe", mm, reads=[yt.name] + wkeys, writes=[p.name])
                tm = tmp[k % 2]
                k += 1
                S.op("dve", lambda e, p=p, cg=cg, tm=tm: e.tensor_tensor(
                    out=tm[:], in0=p[:, :], in1=gateb[:, cg * 512:(cg + 1) * 512], op=ALU.mult),
                    reads=[p.name, "gateb"], writes=[tm.name])
                S.op("pool", lambda e, xb=xb, cg=cg, tm=tm: e.tensor_tensor(
                    out=xb[:, cg * 512:(cg + 1) * 512], in0=xb[:, cg * 512:(cg + 1) * 512], in1=tm[:], op=ALU.add),
                    reads=[xb.name, tm.name], writes=[xb.name])
            S.dma("sp", self.X[tt * 128:(tt + 1) * 128, :], xb[:], reads=[xb.name], writes=["X"])

    def final_norm(self):
        S, ins = self.S, self.ins
        NTT = self.NT // 128
        with ExitStack() as stk:
            sb = self.sbp(stk)
            big = [sb("fbig%d" % i, [128, D], F32) for i in range(3)]
            gB = sb("gfin", [128, D], F32)
            S.dma("sp", gB[:], ins["g_final"][0:1, :].partition_broadcast(128), writes=["gfin"])
            for tt in range(NTT):
                xt = big[tt % 2]
                S.dma("sp", xt[:], self.X[tt * 128:(tt + 1) * 128, :], reads=["X"], writes=[xt.name])
                ss = self.small[:, 0:1]
                rs = self.small[:, 1:2]
                junk = big[2]
                S.op("act", lambda e, xt=xt: e.activation(out=junk[:], in_=xt[:], func=AF.Square, accum_out=ss),
                     reads=[xt.name], writes=["fbig2", "small"])
                self.rstd_col(ss, rs, D)
                S.op("dve", lambda e, xt=xt: e.scalar_tensor_tensor(out=xt[:], in0=xt[:], scalar=rs, in1=gB[:],
                                                                    op0=ALU.mult, op1=ALU.mult),
                     reads=[xt.name, "small", "gfin"], writes=[xt.name])
                S.dma("sp", self.y_out[tt * 128:(tt + 1) * 128, :], xt[:], reads=[xt.name], writes=["y_out"])
            S.barrier()


def host_consts():
    s = np.arange(128)
    tril = (s[:, None] <= s[None, :]).astype(np.float32)
    triu = (s[:, None] >= s[None, :]).astype(np.float32)
    trilS = (s[:, None] < s[None, :]).astype(np.float32)
    triuS = (s[:, None] > s[None, :]).astype(np.float32)
    lvl = np.zeros((128, 7, 128), np.float32)
    for li in range(7):
        b = 1 << li
        for i in range(0, 128, 2 * b):
            lvl[i + b:i + 2 * b, li, i:i + b] = 1.0
    lvlT = np.ascontiguousarray(lvl.transpose(2, 1, 0))
    return {"ident": np.eye(128, dtype=np.float32), "tril": tril, "triu": triu, "trilS": trilS, "triuS": triuS,
            "lvl": lvl, "lvlT": lvlT,
            "ones": np.ones((128, 128), np.float32)}


def make_inputs(inp, core, x_in=None, b=None):
    if b is None:
        b = core // 4
    cond = np.stack([inp["c_ctx"], inp["c"][b]], 0)
    condT = np.ascontiguousarray(cond.reshape(2, 16, 128).transpose(2, 1, 0))
    im = dict(host_consts())
    if x_in is None:
        xp = inp["x_prompt"][2 * core:2 * core + 2].reshape(512, D)
        xs = inp["x_sample"][b]
        x_in = np.concatenate([xp, xs], 0)
    im.update(
        x_in=np.ascontiguousarray(x_in), condT=condT, w_ada=inp["w_ada"], b_ada=inp["b_ada"],
        g_norm=inp["g_norm"], g_final=inp["g_final"][None, :],
        w_sc_in=inp["w_sc_in"][0],
        w_sc_conv=np.ascontiguousarray(inp["w_sc_conv"][0].reshape(3, 32, 128).transpose(2, 1, 0)),
        w_sc_out=inp["w_sc_out"][0],
        w_ml_in=inp["w_ml_in"],
        b_mlg=np.ascontiguousarray(inp["b_ml_gate"].reshape(2, 4, 8).transpose(0, 2, 1)),
        g_ml_head=inp["g_ml_head"], w_ml_out=inp["w_ml_out"],
        mlC0=inp["cache_ml_C"][b], mln0=inp["cache_ml_n"][b], mlm0=inp["cache_ml_m"][b],
        w_gd_in=inp["w_gd_in"][0],
        w_gd_conv=np.ascontiguousarray(inp["w_gd_conv"][0].reshape(3, 64, 128).transpose(2, 1, 0)),
        gd_par=np.ascontiguousarray(np.tile(np.concatenate([inp["gd_A_log"][0].T, inp["gd_dt_bias"][0].T], 1), (4, 1))),
        g_gd_norm=inp["g_gd_norm"], w_gd_out=inp["w_gd_out"][0], gdS0=inp["cache_gd_S"][b, 0],
    )
    return im


SEQS = [dict(T=256, cond=0, roww=256, kind="p"), dict(T=256, cond=0, roww=256, kind="p"),
        dict(T=2048, cond=1, roww=64, kind="s")]


def kernel(**inputs):
    inp = {k: np.ascontiguousarray(np.asarray(v)) for k, v in inputs.items()}
    kb = K(SEQS, [0, 1, 2, 3], do_final=True)
    nc = kb.build()
    in_maps = []
    for core in range(8):
        im = make_inputs(inp, core)
        in_maps.append({k: np.ascontiguousarray(v, dtype=np.float32) for k, v in im.items() if k in kb.ins})
    res = run_bass_kernel_spmd(nc, in_maps, core_ids=list(range(8)))
    r = res.results
    y_prompt = np.concatenate([r[c]["y_out"][:512].reshape(2, 256, D) for c in range(8)], 0)
    y_sample = np.stack([r[0]["y_out"][512:], r[4]["y_out"][512:]], 0)
    st_C = np.concatenate([r[c]["st_C"] for c in range(8)], 0)
    st_n = np.concatenate([r[c]["st_n"] for c in range(8)], 0)
    st_m = np.concatenate([r[c]["st_m"] for c in range(8)], 0)
    st_S = np.concatenate([r[c]["st_S"] for c in range(8)], 0)
    return (y_prompt.astype(np.float32), y_sample.astype(np.float32), st_C.astype(np.float32),
            st_n.astype(np.float32), st_m.astype(np.float32), st_S.astype(np.float32))
```

```python
import math
from contextlib import ExitStack
import numpy as np
import concourse.bass as bass
import concourse.mybir as mybir
from concourse.bass_utils import run_bass_kernel_spmd

F32 = mybir.dt.float32
BF16 = mybir.dt.bfloat16
AF = mybir.ActivationFunctionType
ALU = mybir.AluOpType
AX = mybir.AxisListType

D = 2048
KT = D // 128
EPS = 1e-6
INNER = 4096
ML_H, ML_DK, ML_DV = 8, 256, 512
ML_COLS = 16416
SC_COLS = 16384
GD_COLS = 12416
GD_HV, GD_HQ = 32, 16
CH = 128


class Sched:
    def __init__(self, nc, stack, ndma=30):
        self.nc = nc
        self.E = {"pe": nc.tensor, "dve": nc.vector, "act": nc.scalar, "pool": nc.gpsimd, "sp": nc.sync}
        self.sem = {e: stack.enter_context(nc.semaphore("s_" + e)) for e in ("pe", "dve", "act", "pool")}
        self.cnt = {e: 0 for e in self.sem}
        self.dsem = [stack.enter_context(nc.semaphore("d%d" % i)) for i in range(ndma)]
        self.dcnt = [0] * ndma
        self.drr = 0
        self.drr_pool = 0
        self.known = {e: {} for e in self.E}
        self.last_w = {}
        self.readers = {}
        self.nwaits = 0
        self.ninst = 0
        self.deferred = []
        self.round = 0

    def _semobj(self, key):
        return self.sem[key] if isinstance(key, str) else self.dsem[key]

    def _wait(self, eng, tok):
        key, val = tok
        if eng == "pe" and key == "pe":
            return
        if self.known[eng].get(key, 0) >= val:
            return
        self.E[eng].wait_ge(self._semobj(key), val)
        self.known[eng][key] = val
        self.nwaits += 1

    def _deps(self, eng, reads, writes):
        for r in reads:
            lw = self.last_w.get(r)
            if lw:
                self._wait(eng, lw)
        for w in writes:
            lw = self.last_w.get(w)
            if lw:
                self._wait(eng, lw)
            for key, val in self.readers.get(w, {}).items():
                self._wait(eng, (key, val))

    def _record(self, tok, reads, writes):
        key, val = tok
        for r in reads:
            d = self.readers.setdefault(r, {})
            d[key] = max(d.get(key, 0), val)
        for w in writes:
            self.last_w[w] = tok
            self.readers[w] = {}

    def op(self, eng, fn, reads=(), writes=()):
        self._deps(eng, reads, writes)
        inst = fn(self.E[eng])
        self.cnt[eng] += 1
        inst.then_inc(self.sem[eng], 1)
        self.ninst += 1
        self._record((eng, self.cnt[eng]), reads, writes)

    def dma(self, q, out, in_, reads=(), writes=(), slow=False):
        if self.deferred and reads:
            rs = set(reads)
            hit = [it for it in self.deferred if rs & set(it[2].get("writes", ()))]
            if hit:
                self.deferred = [it for it in self.deferred if not (rs & set(it[2].get("writes", ())))]
                for it in hit:
                    self.dma(*it[1], **it[2])
        self._deps(q, reads, writes)
        npool = 6
        if q == "pool":
            i = self.drr_pool
            self.drr_pool = (self.drr_pool + 1) % npool
        else:
            i = npool + self.drr
            self.drr = (self.drr + 1) % (len(self.dsem) - npool)
        if self.dcnt[i]:
            self._wait(q, (i, self.dcnt[i]))
        self.dcnt[i] += 16
        if slow:
            self.E[q].dma_start(out=out, in_=in_, allow_slow_non_contiguous=True).then_inc(self.dsem[i], 16)
        else:
            self.E[q].dma_start(out=out, in_=in_).then_inc(self.dsem[i], 16)
        self.ninst += 1
        self._record((i, self.dcnt[i]), reads, writes)

    def barrier(self):
        self.flush()
        for e in self.E:
            for i, c in enumerate(self.dcnt):
                if c:
                    self._wait(e, (i, c))
            for k, c in self.cnt.items():
                if c and k != e:
                    self._wait(e, (k, c))
        for e in self.sem:
            if self.cnt[e]:
                self._wait(e, (e, self.cnt[e])) if e != "pe" else None
        self.last_w = {}
        self.readers = {}

    def defer_dma(self, *a, **kw):
        self.deferred.append([self.round + 3, a, kw])

    def tick(self):
        self.round += 1
        keep = []
        for it in self.deferred:
            if it[0] <= self.round:
                self.dma(*it[1], **it[2])
            else:
                keep.append(it)
        self.deferred = keep

    def flush(self):
        for it in self.deferred:
            self.dma(*it[1], **it[2])
        self.deferred = []

    def finish(self, eng="sp"):
        for i, c in enumerate(self.dcnt):
            if c:
                self._wait(eng, (i, c))
        for e, c in self.cnt.items():
            if c:
                self._wait(eng, (e, c))


def run_streams(gens, S=None):
    gens = list(gens)
    while gens:
        for g in list(gens):
            try:
                next(g)
            except StopIteration:
                gens.remove(g)
        if S is not None:
            S.tick()
    if S is not None:
        S.flush()


class TN:
    def __init__(self, name, t):
        self.name = name
        self.t = t

    def __getitem__(self, k):
        return self.t[k]


class K:
    def __init__(self, seqs, layers, do_final=True):
        self.seqs = []
        t0 = 0
        npi = 0
        for s in seqs:
            s = dict(s)
            s["t0"] = t0
            t0 += s["T"]
            if s["kind"] == "p":
                s["pi"] = npi
                npi += 1
            self.seqs.append(s)
        self.NT = t0
        assert self.NT % 512 == 0
        self.layers = layers
        self.do_final = do_final
        self.n_p = npi
        self.Tmax = max(s["T"] for s in self.seqs)

    def build(self):
        nc = bass.Bass("TRN2", target_bir_lowering=False)
        self.nc = nc
        NT = self.NT
        dt = nc.dram_tensor
        ins = {}

        def inp(name, shape, dtype=F32):
            ins[name] = dt(name, list(shape), dtype, kind="ExternalInput").ap()
            return ins[name]

        def scratch(name, shape, dtype):
            return dt(name, list(shape), dtype, kind="Internal").ap()

        def outp(name, shape):
            return dt(name, list(shape), F32, kind="ExternalOutput").ap()

        self.ins = ins
        inp("x_in", [NT, D])
        inp("condT", [128, KT, 2])
        inp("ident", [128, 128])
        inp("tril", [128, 128])
        inp("triu", [128, 128])
        inp("ones", [128, 128])
        inp("w_ada", [4, D, 3 * D])
        inp("b_ada", [4, 3 * D])
        inp("g_norm", [4, D])
        inp("g_final", [1, D])
        kinds = set(l % 3 for l in self.layers)
        self.kinds = kinds
        if 1 in kinds:
            inp("w_sc_in", [D, SC_COLS])
            inp("w_sc_conv", [128, 32, 3])
            inp("w_sc_out", [INNER, D])
        if 0 in kinds:
            inp("w_ml_in", [2, D, ML_COLS])
            inp("b_mlg", [2, 8, 4])
            inp("g_ml_head", [2, INNER])
            inp("w_ml_out", [2, INNER, D])
            inp("mlC0", [2, 2, 8, ML_DK, ML_DV])
            inp("mln0", [2, 2, 8, ML_DK])
            inp("mlm0", [2, 2, 8])
            self.QT = scratch("QT", [2048, NT], BF16)
            self.KTd = scratch("KTd", [2048, NT], BF16)
            self.VOZ = scratch("VOZ", [3, NT, INNER], BF16)
            self.GATES = scratch("GATES", [4, 8, NT], F32)
            self.st_C = outp("st_C", [self.n_p, 2, 2, 8, ML_DK, ML_DV])
            self.st_n = outp("st_n", [self.n_p, 2, 2, 8, ML_DK])
            self.st_m = outp("st_m", [self.n_p, 2, 2, 8])
        if 2 in kinds:
            inp("w_gd_in", [D, GD_COLS])
            inp("w_gd_conv", [128, 64, 3])
            inp("gd_par", [128, 4])
            inp("g_gd_norm", [1, 128])
            inp("w_gd_out", [INNER, D])
            inp("gdS0", [2, 32, 128, 128])
            inp("trilS", [128, 128])
            inp("lvl", [128, 7, 128])
            inp("lvlT", [128, 7, 128])
            inp("triuS", [128, 128])
            self.GQT = scratch("GQT", [2048, NT], BF16)
            self.GKT = scratch("GKT", [2048, NT], BF16)
            self.GV = scratch("GV", [NT, INNER], BF16)
            self.GZ = scratch("GZ", [NT, INNER], BF16)
            self.GAB = scratch("GAB", [4, 32, NT], F32)
            self.st_S = outp("st_S", [self.n_p, 1, 2, 32, 128, 128])
        self.X = scratch("X", [NT, D], F32)
        self.YT = scratch("YT", [INNER, NT], BF16)
        self.MOD = scratch("MOD", [2, 3 * D], F32)
        self.OSD = scratch("OSD", [4, self.Tmax, 512], F32)
        self.y_out = outp("y_out", [NT, D])

        with ExitStack() as st:
            self.S = Sched(nc, st)
            self._uid = [0]

            def _mk(stk, name, shape, dtype):
                self._uid[0] += 1
                return TN(name, stk.enter_context(nc.sbuf_tensor("%s_%d" % (name, self._uid[0]), list(shape), dtype)))
            self.sbp = lambda stk: (lambda name, shape, dtype: _mk(stk, name, shape, dtype))
            sb = self.sbp(st)
            ps = lambda name, shape, dtype: TN(name, st.enter_context(nc.psum_tensor(name, list(shape), dtype)))
            self.ident = sb("ident_sb", [128, 128], F32)
            self.identb = sb("identb", [128, 128], BF16)
            self.tril = sb("tril_sb", [128, 128], F32)
            self.triu = sb("triu_sb", [128, 128], F32)
            self.ones = sb("ones_sb", [128, 128], F32)
            if 2 in self.kinds:
                self.trilS = sb("trilS_sb", [128, 128], F32)
                self.triuS = sb("triuS_sb", [128, 128], F32)
                self.lvl = sb("lvl_sb", [128, 7, 128], BF16)
                self.lvlT = sb("lvlT_sb", [128, 7, 128], BF16)
            self.condT = sb("condT_sb", [128, KT, 2], F32)
            self.scT = sb("scT", [128, KT, 2], BF16)
            self.small = sb("small", [128, 64], F32)
            self.stg = [sb("stg%d" % i, [128, 512], BF16) for i in range(4)]
            self.stgf = [sb("stgf%d" % i, [128, 512], F32) for i in range(2)]
            self.modst = [sb("modst%d" % i, [1, 2, 512], F32) for i in range(2)]
            self.pp = [ps("pp%d" % i, [128, 512], F32) for i in range(6)]
            self.ptb = ps("ptb", [128, 1024], BF16)
            self.pmisc = ps("pmisc", [128, 512], F32)
            self.ppi = 0
            self.stgi = 0
            self.emit()
            self.S.finish("sp")
        return nc

    def psum(self):
        p = self.pp[self.ppi % len(self.pp)]
        self.ppi += 1
        return p

    def psum_s(self, A):
        p = A["pp"][A["ppi"] % len(A["pp"])]
        A["ppi"] += 1
        return p

    def stage(self):
        p = self.stg[self.stgi % len(self.stg)]
        self.stgi += 1
        return p

    def emit(self):
        S, nc, ins = self.S, self.nc, self.ins
        S.dma("sp", self.ident[:], ins["ident"][:, :], writes=["ident_sb"])
        S.dma("sp", self.tril[:], ins["tril"][:, :], writes=["tril_sb"])
        S.dma("sp", self.triu[:], ins["triu"][:, :], writes=["triu_sb"])
        S.dma("sp", self.ones[:], ins["ones"][:, :], writes=["ones_sb"])
        S.dma("sp", self.condT[:], ins["condT"][:, :, :], writes=["condT_sb"])
        if "trilS" in ins:
            S.dma("sp", self.trilS[:], ins["trilS"][:, :], writes=["trilS_sb"])
            S.dma("sp", self.triuS[:], ins["triuS"][:, :], writes=["triuS_sb"])
            S.dma("pool", self.lvl[:], ins["lvl"][:, :, :], writes=["lvl"])
            S.dma("pool", self.lvlT[:], ins["lvlT"][:, :, :], writes=["lvl"])
        S.op("dve", lambda e: e.tensor_copy(self.identb[:], self.ident[:]), reads=["ident_sb"], writes=["identb"])
        S.op("act", lambda e: e.activation(out=self.scT[:], in_=self.condT[:], func=AF.Silu),
             reads=["condT_sb"], writes=["scT"])
        S.dma("sp", self.X[:, :], ins["x_in"][:, :], writes=["X"])
        for l in self.layers:
            self.layer(l)
        if self.do_final:
            self.final_norm()
        else:
            S.dma("sp", self.y_out[:, :], self.X[:, :], reads=["X"], writes=["y_out"])

    def layer(self, l):
        S, ins = self.S, self.ins
        kind = l % 3
        j = l // 3
        with ExitStack() as stk:
            sb = self.sbp(stk)
            self.wbuf = [sb("wbuf%d" % i, [128, KT, 512], BF16) for i in range(2)]
            self.compute_mod(l)
            self.hT = sb("hT", [128, KT, self.NT], BF16)
            self.phase_a(l, sb)
            if kind == 1:
                with ExitStack() as stk2:
                    self.sc_mixer(self.sbp(stk2))
                    S.barrier()
            elif kind == 0:
                self.ml_proj(j)
                S.barrier()
            else:
                with ExitStack() as stk2:
                    self.gd_proj(self.sbp(stk2))
                    S.barrier()
        if kind == 0:
            with ExitStack() as stk:
                self.ml_scan(j, self.sbp(stk))
                S.barrier()
        if kind == 2:
            with ExitStack() as stk:
                self.gd_scan(self.sbp(stk))
                S.barrier()
        wout = {0: lambda: ins["w_ml_out"][j], 1: lambda: ins["w_sc_out"], 2: lambda: ins["w_gd_out"]}[kind]()
        with ExitStack() as stk:
            self.phase_d(l, wout, self.sbp(stk))
            S.barrier()

    def compute_mod(self, l):
        S, ins = self.S, self.ins
        for cb in range(12):
            wb = self.wbuf[cb % 2]
            src = ins["w_ada"][l, :, cb * 512:(cb + 1) * 512].rearrange("(kt p) c -> p kt c", p=128)
            S.dma("pool", wb[:], src, writes=[wb.name])
            bb = self.stgf[cb % 2]
            S.dma("sp", bb[0:1, :], ins["b_ada"][l:l + 1, cb * 512:(cb + 1) * 512], writes=[bb.name])
            ms = self.modst[cb % 2]
            for c in range(2):
                p = self.psum()

                def mm(e, p=p, wb=wb, c=c):
                    for kt in range(KT):
                        r = e.matmul(p[0:1, :], self.scT[:, kt, c:c + 1], wb[:, kt, :], start=(kt == 0),
                                     stop=(kt == KT - 1))
                    return r
                S.op("pe", mm, reads=["scT", wb.name], writes=[p.name])
                S.op("dve", lambda e, p=p, bb=bb, ms=ms, c=c: e.tensor_tensor(
                    out=ms[0:1, c, :], in0=p[0:1, :], in1=bb[0:1, :], op=ALU.add),
                    reads=[p.name, bb.name], writes=[ms.name])
                S.dma("sp", self.MOD[c:c + 1, cb * 512:(cb + 1) * 512], ms[0:1, c, :],
                      reads=[ms.name], writes=["MOD"])

    def tile_cond(self, tt):
        tok = tt * 128
        for s in self.seqs:
            if s["t0"] <= tok < s["t0"] + s["T"]:
                return s["cond"]
        raise AssertionError

    def rstd_col(self, ss, rs, n, key="small"):
        S = self.S
        S.op("dve", lambda e: e.tensor_scalar(out=rs, in0=ss, scalar1=1.0 / n, scalar2=EPS,
                                              op0=ALU.mult, op1=ALU.add), reads=[key], writes=[key])
        S.op("act", lambda e: e.activation(out=rs, in_=rs, func=AF.Sqrt), reads=[key], writes=[key])
        S.op("dve", lambda e: e.reciprocal(out=rs, in_=rs), reads=[key], writes=[key])

    def phase_a(self, l, sb):
        S, ins = self.S, self.ins
        big = [sb("big%d" % i, [128, D], F32) for i in range(3)]
        hbf = sb("hbf", [128, D], BF16)
        amod = sb("amod", [128, D], F32)
        shiftb = sb("shiftb", [128, D], F32)
        NTT = self.NT // 128
        cur = None
        for tt in range(NTT):
            c = self.tile_cond(tt)
            if c != cur:
                cur = c
                S.dma("sp", shiftb[:], self.MOD[c:c + 1, 0:D].partition_broadcast(128), reads=["MOD"],
                      writes=["shiftb"])
                S.dma("sp", amod[:], self.MOD[c:c + 1, D:2 * D].partition_broadcast(128), reads=["MOD"],
                      writes=["amod"])
                S.dma("sp", big[2][:], ins["g_norm"][l:l + 1, :].partition_broadcast(128), writes=["big2"])
                S.op("dve", lambda e: e.scalar_tensor_tensor(out=amod[:], in0=amod[:], scalar=1.0, in1=big[2][:],
                                                             op0=ALU.add, op1=ALU.mult),
                     reads=["amod", "big2"], writes=["amod"])
            xt = big[tt % 2]
            S.dma("sp", xt[:], self.X[tt * 128:(tt + 1) * 128, :], reads=["X"], writes=[xt.name])
            ss = self.small[:, 0:1]
            rs = self.small[:, 1:2]
            junk = big[2]
            S.op("act", lambda e, xt=xt: e.activation(out=junk[:], in_=xt[:], func=AF.Square, accum_out=ss),
                 reads=[xt.name], writes=["big2", "small"])
            self.rstd_col(ss, rs, D)
            S.op("dve", lambda e, xt=xt: e.scalar_tensor_tensor(out=junk[:], in0=xt[:], scalar=rs, in1=amod[:],
                                                                op0=ALU.mult, op1=ALU.mult),
                 reads=[xt.name, "small", "amod"], writes=["big2"])
            S.op("dve", lambda e: e.tensor_tensor(out=hbf[:], in0=junk[:], in1=shiftb[:], op=ALU.add),
                 reads=["big2", "shiftb"], writes=["hbf"])
            for g in range(KT // 8):
                def tr(e, g=g):
                    for jj in range(8):
                        kt = g * 8 + jj
                        r = e.transpose(self.ptb[:, jj * 128:(jj + 1) * 128], hbf[:, kt * 128:(kt + 1) * 128],
                                        self.identb[:])
                    return r
                S.op("pe", tr, reads=["hbf", "identb"], writes=["ptb"])
                S.op("act", lambda e, g=g, tt=tt: e.copy(
                    out=self.hT[:, g * 8:(g + 1) * 8, tt * 128:(tt + 1) * 128],
                    in_=self.ptb[:, :].rearrange("p (j t) -> p j t", j=8)), reads=["ptb"], writes=["hT"])

    def load_w(self, W, col0, ncols, i):
        wb = self.wbuf[i % 2]
        src = W[:, col0:col0 + ncols].rearrange("(kt p) c -> p kt c", p=128)
        self.S.dma("pool", wb[:, :, 0:ncols], src, writes=[wb.name])
        return wb

    def proj_fm(self, wb, c0, m, tb, p):
        def mm(e):
            for kt in range(KT):
                r = e.matmul(p[0:m, :], wb[:, kt, c0:c0 + m], self.hT[:, kt, tb * 512:(tb + 1) * 512],
                             start=(kt == 0), stop=(kt == KT - 1))
            return r
        self.S.op("pe", mm, reads=[wb.name, "hT"], writes=[p.name])

    def proj_tm(self, wb, tt, p):
        def mm(e):
            for kt in range(KT):
                r = e.matmul(p[:, :], self.hT[:, kt, tt * 128:(tt + 1) * 128], wb[:, kt, :],
                             start=(kt == 0), stop=(kt == KT - 1))
            return r
        self.S.op("pe", mm, reads=[wb.name, "hT"], writes=[p.name])

    def evac(self, i, out, in_, reads, writes, scale=None):
        S = self.S
        if i % 2 == 0:
            if scale is None:
                S.op("act", lambda e: e.copy(out=out, in_=in_), reads=reads, writes=writes)
            else:
                S.op("act", lambda e: e.activation(out=out, in_=in_, func=AF.Copy, scale=scale), reads=reads,
                     writes=writes)
        else:
            if scale is None:
                S.op("dve", lambda e: e.tensor_copy(out, in_), reads=reads, writes=writes)
            else:
                S.op("dve", lambda e: e.tensor_scalar(out=out, in0=in_, scalar1=scale, scalar2=None, op0=ALU.mult),
                     reads=reads, writes=writes)

    def sc_mixer(self, sb):
        S, ins = self.S, self.ins
        NTB = self.NT // 512
        W = ins["w_sc_in"]
        wconv = sb("wconv", [128, 32, 3], F32)
        t2k = [sb("t2k%d" % i, [128, 512], F32) for i in range(4)]
        S.dma("sp", wconv[:], ins["w_sc_conv"][:, :, :], writes=["wconv"])
        for ct in range(32):
            wb = self.wbuf[ct % 2]
            wk = wb.name
            for jj in range(4):
                src = W[:, jj * INNER + ct * 128: jj * INNER + (ct + 1) * 128].rearrange("(kt p) c -> p kt c", p=128)
                S.dma("pool", wb[:, :, jj * 128:(jj + 1) * 128], src, writes=[wk])
            for tb in range(NTB):
                roww = self.roww_of_block(tb)
                pl = [self.psum() for _ in range(4)]
                pu, pB, pC, pz = pl
                for jj, p in enumerate(pl):
                    self.proj_fm(wb, jj * 128, 128, tb, p)
                u_sb, cu, acc, sz = t2k
                S.op("act", lambda e, pu=pu: e.copy(out=u_sb[:], in_=pu[:, :]), reads=[pu.name], writes=["t2k0"])
                S.op("act", lambda e, pz=pz: e.activation(out=sz[:], in_=pz[:, :], func=AF.Silu),
                     reads=[pz.name], writes=["t2k3"])
                S.op("dve", lambda e, pC=pC: e.tensor_tensor(out=cu[:], in0=pC[:, :], in1=u_sb[:], op=ALU.mult),
                     reads=[pC.name, "t2k0"], writes=["t2k1"])
                w0 = wconv[:, ct, 0:1]
                w1 = wconv[:, ct, 1:2]
                w2 = wconv[:, ct, 2:3]
                S.op("dve", lambda e, w1=w1: e.tensor_scalar(out=acc[:], in0=cu[:], scalar1=w1, scalar2=None,
                                                             op0=ALU.mult), reads=["t2k1", "wconv"], writes=["t2k2"])
                cu3 = cu[:].rearrange("p (r w) -> p r w", w=roww)
                acc3 = acc[:].rearrange("p (r w) -> p r w", w=roww)
                S.op("dve", lambda e, w0=w0, cu3=cu3, acc3=acc3: e.scalar_tensor_tensor(
                    out=acc3[:, :, 1:], in0=cu3[:, :, :-1], scalar=w0, in1=acc3[:, :, 1:], op0=ALU.mult, op1=ALU.add),
                    reads=["t2k1", "t2k2", "wconv"], writes=["t2k2"])
                S.op("dve", lambda e, w2=w2, cu3=cu3, acc3=acc3: e.scalar_tensor_tensor(
                    out=acc3[:, :, :-1], in0=cu3[:, :, 1:], scalar=w2, in1=acc3[:, :, :-1], op0=ALU.mult, op1=ALU.add),
                    reads=["t2k1", "t2k2", "wconv"], writes=["t2k2"])
                S.op("dve", lambda e, pB=pB: e.tensor_tensor(out=acc[:], in0=pB[:, :], in1=acc[:], op=ALU.mult),
                     reads=[pB.name, "t2k2"], writes=["t2k2"])
                yb = self.stage()
                S.op("dve", lambda e, yb=yb: e.tensor_tensor(out=yb[:], in0=acc[:], in1=sz[:], op=ALU.mult),
                     reads=["t2k2", "t2k3"], writes=[yb.name])
                S.dma("sp", self.YT[ct * 128:(ct + 1) * 128, tb * 512:(tb + 1) * 512], yb[:],
                      reads=[yb.name], writes=["YT"])

    def roww_of_block(self, tb):
        tok = tb * 512
        for s in self.seqs:
            if s["t0"] <= tok < s["t0"] + s["T"]:
                assert 512 % s["roww"] == 0 and (tok - s["t0"]) % s["roww"] == 0
                return s["roww"]
        raise AssertionError

    def ml_proj(self, j):
        S, ins = self.S, self.ins
        W = ins["w_ml_in"][j]
        NTB = self.NT // 512
        NTT = self.NT // 128
        wi = 0
        ei = 0
        for g in range(8):
            wb = self.load_w(W, g * 512, 512, wi)
            wi += 1
            for ci in range(4):
                col = g * 512 + ci * 128
                dst = self.QT if col < 2048 else self.KTd
                r0 = col % 2048
                for tb in range(NTB):
                    p = self.psum()
                    self.proj_fm(wb, ci * 128, 128, tb, p)
                    sg = self.stage()
                    self.evac(ei, sg[:], p[:, :], [p.name], [sg.name], scale=(ML_DK ** -0.5) if col < 2048 else None)
                    ei += 1
                    S.dma("sp", dst[r0:r0 + 128, tb * 512:(tb + 1) * 512], sg[:], reads=[sg.name],
                          writes=[dst.name])
        for g in range(24):
            wb = self.load_w(W, 4096 + g * 512, 512, wi)
            wi += 1
            which = g // 8
            c0 = (g % 8) * 512
            for tt in range(NTT):
                p = self.psum()
                self.proj_tm(wb, tt, p)
                sg = self.stage()
                self.evac(ei, sg[:], p[:, :], [p.name], [sg.name])
                ei += 1
                S.dma("sp", self.VOZ[which, tt * 128:(tt + 1) * 128, c0:c0 + 512], sg[:], reads=[sg.name],
                      writes=["VOZ"])
        wb = self.load_w(W, 16384, 32, wi)
        S.dma("sp", self.small[0:8, 8:12], ins["b_mlg"][j], writes=["small_bg"])
        for typ in range(4):
            for tb in range(NTB):
                p = self.psum()
                self.proj_fm(wb, typ * 8, 8, tb, p)
                sg = self.stgf[(typ * NTB + tb) % 2]
                S.op("dve", lambda e, p=p, sg=sg, typ=typ: e.tensor_scalar(
                    out=sg[0:8, :], in0=p[0:8, :], scalar1=self.small[0:8, 8 + typ:9 + typ], scalar2=None,
                    op0=ALU.add), reads=[p.name, "small_bg"], writes=[sg.name])
                S.dma("sp", self.GATES[typ, :, tb * 512:(tb + 1) * 512], sg[0:8, :], reads=[sg.name],
                      writes=["GATES"])

    def ml_scan(self, j, sb):
        S, ins = self.S, self.ins
        A = {}
        A["gt"] = sb("gt", [128, 4, 128], F32)
        A["ee"] = sb("ee", [128, 2, 128], F32)
        A["nb"] = sb("nb", [128, 2, 128], F32)
        A["uu"] = sb("uu", [128, 2, 128], F32)
        A["ww"] = sb("ww", [128, 2, 128], F32)
        A["fl"] = sb("fl", [128, 2, 128], F32)
        A["cols"] = sb("cols", [128, 16], F32)
        A["rows"] = sb("rows", [1, 12, 128], F32)
        A["mh"] = sb("mh", [1, 2, 17, 8], F32)
        A["toksc"] = sb("toksc", [128, 4, 128], F32)
        A["decB"] = sb("decB", [128, 2, 128], F32)
        A["diag"] = sb("diag", [128, 128], F32)
        NS = 4
        AS = []
        for k in range(NS):
            B = dict(A)
            g = lambda n, shp, dt=F32, B=B, k=k: B.__setitem__(n, sb("%s_s%d" % (n, k), shp, dt))
            B["HSb"] = [sb("HSb%d_s%d" % (i, k), [128, 512], F32) for i in range(2)]
            B["sid"] = k
            B["Cst"] = [sb("Cst%d_s%d" % (d, k), [128, 2, 513], F32) for d in range(2)]
            g("Cb", [128, 2, 513], BF16)
            for n in ("qT", "kT"):
                B[n] = [sb("%s%d_s%d" % (n, i, k), [128, 2, 128], BF16) for i in range(2)]
            g("ktm", [128, 256], BF16)
            for n in ("v", "o", "z", "ytr"):
                B[n] = [sb("%s%d_s%d" % (n, i, k), [128, 512], BF16) for i in range(2)]
            g("vx", [128, 520], BF16)
            g("sm", [128, 128], BF16)
            g("hs", [128, 512])
            g("sig", [128, 512])
            g("sz", [128, 512])
            g("ybf", [128, 512], BF16)
            g("sc8", [128, 8])
            g("ghead", [128, 512])
            B["uc"] = 0
            AS.append(B)
        for s in self.seqs:
            self.ml_gates(j, s, A)

            def stream(k, s=s):
                for h in range(k, ML_H, NS):
                    S.dma("sp", AS[k]["ghead"][:], ins["g_ml_head"][j:j + 1, h * 512:(h + 1) * 512].partition_broadcast(128),
                          writes=[AS[k]["ghead"].name])
                    for d in (1, 0):
                        yield from self.ml_head_dir(j, s, h, d, AS[k])
            run_streams([stream(k) for k in range(NS)], S)

    def ml_gates(self, j, s, A):
        S, ins = self.S, self.ins
        T, t0 = s["T"], s["t0"]
        nch = T // 128
        R = nch * 8
        gt, ee, nb, uu, ww, fl, cols, rows, mh = (A[k] for k in ("gt", "ee", "nb", "uu", "ww", "fl", "cols", "rows", "mh"))
        for c in range(nch):
            for typ in range(4):
                S.dma("sp", gt[c * 8:(c + 1) * 8, typ, :], self.GATES[typ, :, t0 + c * 128: t0 + (c + 1) * 128],
                      reads=["GATES"], writes=["gt"])
        S.op("act", lambda e: e.activation(out=ee[0:R], in_=gt[0:R, 2:4, :], func=AF.Exp, scale=-1.0),
             reads=["gt"], writes=["ee"])
        S.op("dve", lambda e: e.tensor_scalar(out=ee[0:R], in0=ee[0:R], scalar1=1.0, scalar2=None, op0=ALU.add),
             reads=["ee"], writes=["ee"])
        S.op("act", lambda e: e.activation(out=ee[0:R], in_=ee[0:R], func=AF.Ln), reads=["ee"], writes=["ee"])
        S.op("dve", lambda e: e.tensor_tensor_scan(out=nb[0:R, 0, :], data0=self.ones[0:R, :], data1=ee[0:R, 0, :],
                                                   initial=0.0, op0=ALU.mult, op1=ALU.add),
             reads=["ee", "ones_sb"], writes=["nb"])
        S.op("dve", lambda e: e.tensor_tensor_scan(out=nb[0:R, 1, :], data0=self.ones[0:R, :], data1=ee[0:R, 1, :],
                                                   initial=0.0, op0=ALU.mult, op1=ALU.add),
             reads=["ee", "ones_sb"], writes=["nb"])
        S.op("dve", lambda e: e.tensor_copy(cols[0:R, 8:9], nb[0:R, 1, 127:128]), reads=["nb"], writes=["cols"])
        S.op("dve", lambda e: e.tensor_tensor(out=nb[0:R, 1, :], in0=ee[0:R, 1, :], in1=nb[0:R, 1, :],
                                              op=ALU.subtract), reads=["ee", "nb"], writes=["nb"])
        S.op("dve", lambda e: e.tensor_scalar(out=nb[0:R, 1, :], in0=nb[0:R, 1, :], scalar1=cols[0:R, 8:9],
                                              scalar2=None, op0=ALU.add), reads=["nb", "cols"], writes=["nb"])
        S.op("dve", lambda e: e.tensor_tensor(out=uu[0:R], in0=gt[0:R, 0:2, :], in1=nb[0:R], op=ALU.add),
             reads=["gt", "nb"], writes=["uu"])
        S.op("dve", lambda e: e.tensor_reduce(out=cols[0:R, 0:2], in_=uu[0:R], axis=AX.X, op=ALU.max),
             reads=["uu"], writes=["cols"])
        S.op("dve", lambda e: e.tensor_copy(cols[0:R, 2:3], nb[0:R, 0, 127:128]), reads=["nb"], writes=["cols"])
        S.op("dve", lambda e: e.tensor_copy(cols[0:R, 3:4], nb[0:R, 1, 0:1]), reads=["nb"], writes=["cols"])
        pm = self.pmisc
        for q in range(4):
            S.op("pe", lambda e, q=q: e.transpose(pm[0:1, q * 128:q * 128 + R], cols[0:R, q:q + 1],
                                                  self.ident[0:R, 0:R]), reads=["cols", "ident_sb"],
                 writes=["pmisc"])
        S.op("dve", lambda e: e.tensor_copy(rows[0:1, 0:4, 0:R], pm[0:1, :].rearrange("p (q r) -> p q r", q=4)[:, :, 0:R]),
             reads=["pmisc"], writes=["rows"])
        for d in range(2):
            if s["kind"] == "s":
                S.dma("sp", mh[0:1, d, 0, :], ins["mlm0"][j:j + 1, d, :], writes=["mh"])
            else:
                S.op("dve", lambda e, d=d: e.memset(mh[0:1, d, 0, :], 0.0), writes=["mh"])
            order = list(range(nch)) if d == 0 else list(range(nch - 1, -1, -1))
            for idx, c in enumerate(order):
                cs = slice(c * 8, (c + 1) * 8)
                S.op("dve", lambda e, d=d, idx=idx, cs=cs: e.tensor_tensor(
                    out=rows[0:1, 4 + d, cs], in0=mh[0:1, d, idx, :], in1=rows[0:1, d, cs], op=ALU.max),
                    reads=["mh", "rows"], writes=["rows"])
                S.op("dve", lambda e, d=d, idx=idx, cs=cs: e.tensor_tensor(
                    out=rows[0:1, 6 + d, cs], in0=mh[0:1, d, idx, :], in1=rows[0:1, 4 + d, cs], op=ALU.subtract),
                    reads=["mh", "rows"], writes=["rows"])
                S.op("dve", lambda e, d=d, idx=idx, cs=cs: e.tensor_tensor(
                    out=mh[0:1, d, idx + 1, :], in0=rows[0:1, 4 + d, cs], in1=rows[0:1, 2 + d, cs], op=ALU.subtract),
                    reads=["mh", "rows"], writes=["mh"])
            if s["kind"] == "p":
                S.dma("sp", self.st_m[s["pi"]:s["pi"] + 1, j, d, :], mh[0:1, d, nch, :], reads=["mh"],
                      writes=["st_m"])
        S.op("act", lambda e: e.activation(out=rows[0:1, 6:8, 0:R], in_=rows[0:1, 6:8, 0:R], func=AF.Exp),
             reads=["rows"], writes=["rows"])
        S.op("dve", lambda e: e.tensor_scalar(out=rows[0:1, 4:6, 0:R], in0=rows[0:1, 4:6, 0:R], scalar1=-1.0,
                                              scalar2=None, op0=ALU.mult), reads=["rows"], writes=["rows"])
        for q in range(4):
            S.op("pe", lambda e, q=q: e.transpose(pm[0:R, 256 + q:257 + q], rows[0:1, 4 + q, 0:R],
                                                  self.ident[0:1, 0:1]), reads=["rows", "ident_sb"],
                 writes=["pmisc"])
        S.op("dve", lambda e: e.tensor_copy(cols[0:R, 4:8], pm[0:R, 256:260]), reads=["pmisc"], writes=["cols"])
        for d in range(2):
            S.op("act", lambda e, d=d: e.activation(out=ww[0:R, d, :], in_=uu[0:R, d, :], func=AF.Exp,
                                                    bias=cols[0:R, 4 + d:5 + d], scale=1.0),
                 reads=["uu", "cols"], writes=["ww"])
            S.op("act", lambda e, d=d: e.activation(out=fl[0:R, d, :], in_=nb[0:R, d, :], func=AF.Exp,
                                                    bias=cols[0:R, 4 + d:5 + d], scale=1.0),
                 reads=["nb", "cols"], writes=["fl"])
        for q in range(4):
            src = ww if q < 2 else fl
            S.op("pe", lambda e, q=q, src=src: e.transpose(pm[:, q * 128:q * 128 + R], src[0:R, q % 2, :],
                                                           self.ident[0:R, 0:R]),
                 reads=[src.name, "ident_sb"], writes=["pmisc"])
        S.op("act", lambda e: e.copy(out=A["toksc"][:, :, 0:R],
                                     in_=pm[:, :].rearrange("p (q r) -> p q r", q=4)[:, :, 0:R]),
             reads=["pmisc"], writes=["toksc"])
        for d in range(2):
            S.op("dve", lambda e, d=d: e.tensor_scalar(out=A["diag"][0:R, 0:R], in0=self.ident[0:R, 0:R],
                                                       scalar1=cols[0:R, 6 + d:7 + d], scalar2=None, op0=ALU.mult),
                 reads=["ident_sb", "cols"], writes=["diag"])
            p = self.psum()
            S.op("pe", lambda e, p=p: e.matmul(p[:, 0:R], self.ones[0:R, :], A["diag"][0:R, 0:R], start=True,
                                               stop=True), reads=["ones_sb", "diag"], writes=[p.name])
            S.op("act", lambda e, p=p, d=d: e.copy(out=A["decB"][:, d, 0:R], in_=p[:, 0:R]), reads=[p.name],
                 writes=["decB"])

    def ml_head_dir(self, j, s, h, d, A):
        S, ins = self.S, self.ins
        T, t0 = s["T"], s["t0"]
        nch = T // 128
        Cst = A["Cst"][d]
        Cb = A["Cb"]
        if s["kind"] == "s":
            S.dma("sp", Cst[:, :, 0:512], ins["mlC0"][j, d, h].rearrange("(dt p) v -> p dt v", p=128),
                  writes=[Cst.name])
            S.dma("sp", Cst[:, :, 512], ins["mln0"][j, d, h].rearrange("(dt p) -> p dt", p=128),
                  writes=[Cst.name], slow=True)
        else:
            S.op("dve", lambda e: e.memset(Cst[:], 0.0), writes=[Cst.name])
        order = list(range(nch)) if d == 0 else list(range(nch - 1, -1, -1))
        mask = self.tril if d == 0 else self.triu
        for c in order:
            u = A["uc"]
            A["uc"] += 1
            tok0 = t0 + c * 128
            col = c * 8 + h
            qT, kT, v = A["qT"][u % 2], A["kT"][u % 2], A["v"][u % 2]
            ktm, vx, sm = A["ktm"], A["vx"], A["sm"]
            toksc, decB = A["toksc"], A["decB"]
            S.dma("sp", qT[:], self.QT[h * 256:(h + 1) * 256, tok0:tok0 + 128].rearrange("(dt p) t -> p dt t", p=128),
                  reads=["QT"], writes=[qT.name])
            S.dma("sp", kT[:], self.KTd[h * 256:(h + 1) * 256, tok0:tok0 + 128].rearrange("(dt p) t -> p dt t", p=128),
                  reads=["KTd"], writes=[kT.name])
            S.dma("sp", v[:], self.VOZ[0, tok0:tok0 + 128, h * 512:(h + 1) * 512], reads=["VOZ"], writes=[v.name])
            yield
            def trk(e, kT=kT):
                for dt_ in range(2):
                    r = e.transpose(self.ptb[:, dt_ * 128:(dt_ + 1) * 128], kT[:, dt_, :], self.identb[:])
                return r
            S.op("pe", trk, reads=[kT.name, "identb"], writes=["ptb"])
            S.op("act", lambda e: e.copy(out=ktm[:], in_=self.ptb[:, 0:256]), reads=["ptb"], writes=[A["ktm"].name])
            wcol = toksc[:, d, col:col + 1]
            S.op("dve", lambda e, v=v, wcol=wcol: e.tensor_scalar(out=vx[:, 0:512], in0=v[:], scalar1=wcol,
                                                                  scalar2=None, op0=ALU.mult),
                 reads=[v.name, "toksc"], writes=[A["vx"].name])
            S.op("dve", lambda e, wcol=wcol: e.tensor_copy(vx[:, 512:513], wcol), reads=["toksc"], writes=[A["vx"].name])
            yield
            S.op("dve", lambda e, col=col: e.tensor_scalar(out=Cst[:], in0=Cst[:], scalar1=decB[:, d, col:col + 1],
                                                           scalar2=None, op0=ALU.mult),
                 reads=[Cst.name, "decB"], writes=[Cst.name])
            S.op("act", lambda e: e.copy(out=Cb[:], in_=Cst[:]), reads=[Cst.name], writes=[A["Cb"].name])
            yield
            pS = self.psum()

            def mmS(e, pS=pS, kT=kT, qT=qT):
                for dt_ in range(2):
                    r = e.matmul(pS[:, 0:128], kT[:, dt_, :], qT[:, dt_, :], start=(dt_ == 0), stop=(dt_ == 1))
                return r
            S.op("pe", mmS, reads=[kT.name, qT.name], writes=[pS.name])
            S.op("dve", lambda e, pS=pS: e.tensor_tensor(out=sm[:], in0=pS[:, 0:128], in1=mask[:], op=ALU.mult),
                 reads=[pS.name, mask.name], writes=[A["sm"].name])
            yield
            pnum = self.psum()
            pden = self.psum()

            def mmN(e, pnum=pnum, pden=pden, qT=qT):
                e.matmul(pnum[:, :], sm[:], vx[:, 0:512], start=True, stop=False)
                e.matmul(pnum[:, :], qT[:, 0, :], Cb[:, 0, 0:512], start=False, stop=False)
                e.matmul(pnum[:, :], qT[:, 1, :], Cb[:, 1, 0:512], start=False, stop=True)
                e.matmul(pden[:, 0:1], sm[:], vx[:, 512:513], start=True, stop=False)
                e.matmul(pden[:, 0:1], qT[:, 0, :], Cb[:, 0, 512:513], start=False, stop=False)
                return e.matmul(pden[:, 0:1], qT[:, 1, :], Cb[:, 1, 512:513], start=False, stop=True)
            S.op("pe", mmN, reads=[A["sm"].name, A["vx"].name, qT.name, A["Cb"].name], writes=[pnum.name, pden.name])
            rr = A["sc8"][:, 4:5]
            S.op("act", lambda e, pden=pden: e.activation(out=rr, in_=pden[:, 0:1], func=AF.Abs),
                 reads=[pden.name], writes=[A["sc8"].name + "_rr"])
            S.op("dve", lambda e, col=col: e.tensor_tensor(out=rr, in0=rr, in1=toksc[:, 2 + d, col:col + 1],
                                                           op=ALU.max), reads=[A["sc8"].name + "_rr", "toksc"],
                 writes=[A["sc8"].name + "_rr"])
            S.op("dve", lambda e: e.reciprocal(out=rr, in_=rr), reads=[A["sc8"].name + "_rr"], writes=[A["sc8"].name + "_rr"])
            hsb = A["HSb"][u % 2]
            hsd = self.OSD[A["sid"], c * 128:(c + 1) * 128, :]
            hskey = "OSD%d_%d" % (A["sid"], c)
            if d == 1:
                S.op("dve", lambda e, pnum=pnum, hsb=hsb: e.tensor_scalar(out=hsb[:], in0=pnum[:, :], scalar1=rr,
                                                                          scalar2=None, op0=ALU.mult),
                     reads=[pnum.name, A["sc8"].name + "_rr"], writes=[hsb.name])
                S.defer_dma("pool", hsd, hsb[:], reads=[hsb.name], writes=[hskey])
            else:
                self.ml_finalize(j, h, c, tok0, pnum, rr, A, u)
            yield
            pD0, pD1 = self.psum(), self.psum()

            def mmD(e, pD0=pD0, pD1=pD1):
                e.matmul(pD0[:, :], ktm[:, 0:128], vx[:, 0:512], start=True, stop=True)
                e.matmul(pD1[:, :], ktm[:, 128:256], vx[:, 0:512], start=True, stop=True)
                e.matmul(self.pmisc[:, 300:301], ktm[:, 0:128], vx[:, 512:513], start=True, stop=True)
                return e.matmul(self.pmisc[:, 301:302], ktm[:, 128:256], vx[:, 512:513], start=True, stop=True)
            S.op("pe", mmD, reads=[A["ktm"].name, A["vx"].name], writes=[pD0.name, pD1.name, "pmisc"])
            S.op("dve", lambda e, pD0=pD0: e.tensor_tensor(out=Cst[:, 0, 0:512], in0=Cst[:, 0, 0:512], in1=pD0[:, :],
                                                           op=ALU.add), reads=[pD0.name, Cst.name], writes=[Cst.name])
            S.op("dve", lambda e, pD1=pD1: e.tensor_tensor(out=Cst[:, 1, 0:512], in0=Cst[:, 1, 0:512], in1=pD1[:, :],
                                                           op=ALU.add), reads=[pD1.name, Cst.name], writes=[Cst.name])
            S.op("dve", lambda e: e.tensor_tensor(out=Cst[:, :, 512], in0=Cst[:, :, 512], in1=self.pmisc[:, 300:302],
                                                  op=ALU.add), reads=["pmisc", Cst.name], writes=[Cst.name])
        if s["kind"] == "p":
            pi = s["pi"]
            S.dma("sp", self.st_C[pi, j, d, h].rearrange("(dt p) v -> p dt v", p=128), Cst[:, :, 0:512],
                  reads=[Cst.name], writes=["st_C"])
            S.dma("sp", self.st_n[pi, j, d, h].rearrange("(dt p) -> p dt", p=128), Cst[:, :, 512],
                  reads=[Cst.name], writes=["st_n"], slow=True)

    def ml_finalize(self, j, h, c, tok0, pnum, rr, A, u):
        S = self.S
        hs, sig, sz, ybf = A["hs"], A["sig"], A["sz"], A["ybf"]
        o, z = A["o"][u % 2], A["z"][u % 2]
        ytr = A["ytr"][u % 2]
        S.dma("sp", o[:], self.VOZ[1, tok0:tok0 + 128, h * 512:(h + 1) * 512], reads=["VOZ"], writes=[o.name])
        S.dma("sp", z[:], self.VOZ[2, tok0:tok0 + 128, h * 512:(h + 1) * 512], reads=["VOZ"], writes=[z.name])
        hsb = A["HSb"][u % 2]
        hskey = "OSD%d_%d" % (A["sid"], c)
        S.dma("sp", hsb[:], self.OSD[A["sid"], c * 128:(c + 1) * 128, :], reads=[hskey], writes=[hsb.name])
        S.op("dve", lambda e: e.scalar_tensor_tensor(out=hs[:], in0=pnum[:, :], scalar=rr, in1=hsb[:],
                                                     op0=ALU.mult, op1=ALU.add),
             reads=[pnum.name, A["sc8"].name + "_rr", hsb.name], writes=[A["hs"].name])
        ss = A["sc8"][:, 0:1]
        rs = A["sc8"][:, 1:2]
        skey = A["sc8"].name
        S.op("act", lambda e: e.activation(out=sig[:], in_=hs[:], func=AF.Square, accum_out=ss),
             reads=[A["hs"].name], writes=[A["sig"].name, skey])
        self.rstd_col(ss, rs, ML_DV, key=skey)
        S.op("dve", lambda e: e.scalar_tensor_tensor(out=hs[:], in0=hs[:], scalar=rs,
                                                     in1=A["ghead"][:, :], op0=ALU.mult,
                                                     op1=ALU.mult), reads=[A["hs"].name, skey, A["ghead"].name],
             writes=[A["hs"].name])
        S.op("act", lambda e: e.activation(out=sig[:], in_=o[:], func=AF.Sigmoid), reads=[o.name], writes=[A["sig"].name])
        S.op("act", lambda e: e.activation(out=sz[:], in_=z[:], func=AF.Silu), reads=[z.name], writes=[A["sz"].name])
        S.op("dve", lambda e: e.tensor_tensor(out=hs[:], in0=hs[:], in1=sig[:], op=ALU.mult),
             reads=[A["hs"].name, A["sig"].name], writes=[A["hs"].name])
        S.op("dve", lambda e: e.tensor_tensor(out=ybf[:], in0=hs[:], in1=sz[:], op=ALU.mult),
             reads=[A["hs"].name, A["sz"].name], writes=[A["ybf"].name])

        def tr(e):
            for i in range(4):
                r = e.transpose(self.ptb[:, 512 + i * 128:512 + (i + 1) * 128], ybf[:, i * 128:(i + 1) * 128],
                                self.identb[:])
            return r
        S.op("pe", tr, reads=[A["ybf"].name, "identb"], writes=["ptb"])
        S.op("act", lambda e: e.copy(out=ytr[:], in_=self.ptb[:, 512:1024]), reads=["ptb"], writes=[ytr.name])
        S.defer_dma("pool", self.YT[h * 512:(h + 1) * 512, tok0:tok0 + 128].rearrange("(i p) t -> p i t", p=128),
                    ytr[:].rearrange("p (i t) -> p i t", i=4), reads=[ytr.name], writes=["YT"])

    def gd_proj(self, sb):
        S, ins = self.S, self.ins
        W = ins["w_gd_in"]
        NTB = self.NT // 512
        NTT = self.NT // 128
        wconv = sb("gwconv", [128, 64, 3], F32)
        accs = [sb("gacc%d" % i, [128, 512], F32) for i in range(2)]
        sacts = [sb("gsact%d" % i, [128, 512], F32) for i in range(2)]
        sqs = [sb("gsq%d" % i, [128, 512], F32) for i in range(2)]
        rns = [sb("grn%d" % i, [128, 512], F32) for i in range(2)]
        sbfs = [sb("gsbf%d" % i, [128, 512], BF16) for i in range(2)]
        gi = 0
        S.dma("sp", wconv[:], ins["w_gd_conv"][:, :, :], writes=["gwconv"])
        wi = 0
        ei = 0
        for g in range(16):
            wb = self.load_w(W, g * 512, 512, wi)
            wi += 1
            for ci in range(4):
                ct = g * 4 + ci
                for tb in range(NTB):
                    roww = self.roww_of_block(tb)
                    acc, sact, sq, rn, sbf = accs[gi % 2], sacts[gi % 2], sqs[gi % 2], rns[gi % 2], sbfs[gi % 2]
                    gi += 1
                    p = self.psum()
                    self.proj_fm(wb, ci * 128, 128, tb, p)
                    w0, w1, w2 = (wconv[:, ct, i:i + 1] for i in range(3))
                    p3 = p[:, :].rearrange("p (r w) -> p r w", w=roww)
                    acc3 = acc[:].rearrange("p (r w) -> p r w", w=roww)
                    S.op("dve", lambda e, p=p, w1=w1, acc=acc: e.tensor_scalar(out=acc[:], in0=p[:, :], scalar1=w1, scalar2=None,
                                                                     op0=ALU.mult), reads=[p.name, "gwconv"],
                         writes=[acc.name])
                    S.op("dve", lambda e, p3=p3, acc3=acc3, w0=w0: e.scalar_tensor_tensor(
                        out=acc3[:, :, 1:], in0=p3[:, :, :-1], scalar=w0, in1=acc3[:, :, 1:], op0=ALU.mult,
                        op1=ALU.add), reads=[p.name, acc.name, "gwconv"], writes=[acc.name])
                    S.op("dve", lambda e, p3=p3, acc3=acc3, w2=w2: e.scalar_tensor_tensor(
                        out=acc3[:, :, :-1], in0=p3[:, :, 1:], scalar=w2, in1=acc3[:, :, :-1], op0=ALU.mult,
                        op1=ALU.add), reads=[p.name, acc.name, "gwconv"], writes=[acc.name])
                    if ct < 32:
                        S.op("act", lambda e, sact=sact, acc=acc: e.activation(out=sact[:], in_=acc[:], func=AF.Silu), reads=[acc.name],
                             writes=[sact.name])
                        S.op("act", lambda e, sq=sq, sact=sact: e.activation(out=sq[:], in_=sact[:], func=AF.Square), reads=[sact.name],
                             writes=[sq.name])
                        p2 = self.psum()
                        S.op("pe", lambda e, p2=p2, sq=sq: e.matmul(p2[:, :], self.ones[:, :], sq[:], start=True, stop=True),
                             reads=["ones_sb", sq.name], writes=[p2.name])
                        S.op("dve", lambda e, p2=p2, rn=rn: e.tensor_scalar(out=rn[:], in0=p2[:, :], scalar1=EPS, scalar2=None,
                                                                     op0=ALU.add), reads=[p2.name], writes=[rn.name])
                        S.op("act", lambda e, rn=rn: e.activation(out=rn[:], in_=rn[:], func=AF.Ln), reads=[rn.name],
                             writes=[rn.name])
                        S.op("act", lambda e, rn=rn: e.activation(out=rn[:], in_=rn[:], func=AF.Exp, scale=-0.5),
                             reads=[rn.name], writes=[rn.name])
                        sg = self.stage()
                        scl = (128 ** -0.5) if ct < 16 else 1.0
                        S.op("dve", lambda e, sg=sg, scl=scl, sact=sact, rn=rn: e.scalar_tensor_tensor(
                            out=sg[:], in0=sact[:], scalar=scl, in1=rn[:], op0=ALU.mult, op1=ALU.mult),
                            reads=[sact.name, rn.name], writes=[sg.name])
                        dst = self.GQT if ct < 16 else self.GKT
                        r0 = (ct % 16) * 128
                        S.dma("sp", dst[r0:r0 + 128, tb * 512:(tb + 1) * 512], sg[:], reads=[sg.name],
                              writes=[dst.name])
                    else:
                        hv = ct - 32
                        S.op("act", lambda e, sbf=sbf, acc=acc: e.activation(out=sbf[:], in_=acc[:], func=AF.Silu), reads=[acc.name],
                             writes=[sbf.name])

                        def tr(e, sbf=sbf):
                            for i in range(4):
                                r = e.transpose(self.ptb[:, i * 128:(i + 1) * 128], sbf[:, i * 128:(i + 1) * 128],
                                                self.identb[:])
                            return r
                        S.op("pe", tr, reads=[sbf.name, "identb"], writes=["ptb"])
                        sg = self.stage()
                        self.evac(ei, sg[:], self.ptb[:, 0:512], ["ptb"], [sg.name])
                        ei += 1
                        S.dma("sp", self.GV[tb * 512:(tb + 1) * 512, hv * 128:(hv + 1) * 128].rearrange(
                            "(a p) v -> p a v", p=128), sg[:].rearrange("p (a v) -> p a v", a=4), reads=[sg.name],
                            writes=["GV"])
        for g in range(8):
            wb = self.load_w(W, 8192 + g * 512, 512, wi)
            wi += 1
            for tt in range(NTT):
                p = self.psum()
                self.proj_tm(wb, tt, p)
                sg = self.stage()
                self.evac(ei, sg[:], p[:, :], [p.name], [sg.name])
                ei += 1
                S.dma("sp", self.GZ[tt * 128:(tt + 1) * 128, g * 512:(g + 1) * 512], sg[:], reads=[sg.name],
                      writes=["GZ"])
        wb = self.load_w(W, 12288, 128, wi)
        for typ in range(4):
            for tb in range(NTB):
                p = self.psum()
                self.proj_fm(wb, typ * 32, 32, tb, p)
                sg = self.stgf[(typ * NTB + tb) % 2]
                S.op("act", lambda e, p=p, sg=sg: e.copy(out=sg[0:32, :], in_=p[0:32, :]), reads=[p.name],
                     writes=[sg.name])
                S.dma("sp", self.GAB[typ, :, tb * 512:(tb + 1) * 512], sg[0:32, :], reads=[sg.name],
                      writes=["GAB"])

    def gd_scan(self, sb):
        S, ins = self.S, self.ins
        A = {}
        nrt = max(1, self.Tmax // 512)
        f = lambda n, shp, dt=F32: A.__setitem__(n, sb("g_" + n, shp, dt))
        f("ga", [128, 4, 128])
        f("sp", [128, 2, 128])
        f("ng", [128, 2, 128])
        f("nG", [128, 2, 128])
        f("bt", [128, 2, 128])
        f("kd", [128, 2, 128])
        f("gc", [128, 16])
        f("par", [128, 4])
        f("toksc", [128, nrt * 2 * 5, 128])
        f("decB", [128, nrt * 2, 128])
        f("diag", [128, 128])
        f("gnB", [128, 128])
        NS = 3
        AS = []
        for k in range(NS):
            B = dict(A)
            g = lambda n, shp, dt=F32, B=B, k=k: B.__setitem__(n, sb("g_%s_s%d" % (n, k), shp, dt))
            B["OSb"] = [sb("g_OSb%d_s%d" % (i, k), [128, 512], F32) for i in range(2)]
            B["sid"] = k
            B["pp"] = self.pp[2 * k:2 * k + 2]
            B["ppi"] = 0
            g("Sst", [128, 4, 128])
            g("Sb", [128, 4, 128], BF16)
            g("diagG", [128, 4, 128])
            g("dd", [128, 4, 128])
            g("dec", [128, 4, 128])
            g("t1", [128, 4, 128])
            g("ktm", [128, 2, 128], BF16)
            for n in ("P", "PT", "N", "kb", "vb", "WTn", "U", "kdk", "y"):
                g(n, [128, 4, 128], BF16)
            for n in ("Q", "Tm", "TTm", "kT", "qT", "v", "z", "ytr", "Lb", "LTb", "Yb", "Y2b"):
                B[n] = [sb("g_%s%d_s%d" % (n, i, k), [128, 2, 128] if n in ("kT", "qT") else [128, 4, 128], BF16)
                        for i in range(2)]
            g("o", [128, 4, 128])
            g("sq", [128, 4, 128])
            g("ssq", [128, 8])
            B["uc"] = 0
            AS.append(B)
        S.dma("sp", A["gnB"][:], ins["g_gd_norm"][0:1, :].partition_broadcast(128), writes=["g_gnB"])
        S.dma("sp", A["par"][:], ins["gd_par"][:, :], writes=["g_par"])
        S.op("act", lambda e: e.activation(out=A["par"][:, 0:2], in_=A["par"][:, 0:2], func=AF.Exp),
             reads=["g_par"], writes=["g_par"])
        for s in self.seqs:
            self.gd_gates(s, A)

            def stream(k, s=s):
                for hg in range(k, 8, NS):
                    for d in (1, 0):
                        yield from self.gd_group_dir(s, hg, d, AS[k])
            run_streams([stream(k) for k in range(NS)], S)

    def gd_gates(self, s, A):
        S, ins = self.S, self.ins
        T, t0 = s["T"], s["t0"]
        nch = T // 128
        ga, sp, ng, nG, bt, kd, gc, par, toksc, decB = (A[k] for k in (
            "ga", "sp", "ng", "nG", "bt", "kd", "gc", "par", "toksc", "decB"))
        pm = self.pmisc
        for rt in range((nch + 3) // 4):
            ncl = min(4, nch - rt * 4)
            R = ncl * 32
            for cl in range(ncl):
                c = rt * 4 + cl
                for typ in range(4):
                    S.dma("sp", ga[cl * 32:(cl + 1) * 32, typ, :], self.GAB[typ, :, t0 + c * 128:t0 + (c + 1) * 128],
                          reads=["GAB"], writes=["g_ga"])
            for d in range(2):
                S.op("act", lambda e, d=d: e.activation(out=sp[0:R, d, :], in_=ga[0:R, d, :], func=AF.Exp,
                                                        bias=par[0:R, 2 + d:3 + d], scale=1.0),
                     reads=["g_ga", "g_par"], writes=["g_sp"])
            S.op("dve", lambda e: e.tensor_scalar(out=sp[0:R], in0=sp[0:R], scalar1=1.0, scalar2=None, op0=ALU.add),
                 reads=["g_sp"], writes=["g_sp"])
            S.op("act", lambda e: e.activation(out=sp[0:R], in_=sp[0:R], func=AF.Ln), reads=["g_sp"],
                 writes=["g_sp"])
            for d in range(2):
                S.op("dve", lambda e, d=d: e.tensor_scalar(out=ng[0:R, d, :], in0=sp[0:R, d, :],
                                                           scalar1=par[0:R, d:d + 1], scalar2=None, op0=ALU.mult),
                     reads=["g_sp", "g_par"], writes=["g_ng"])
            S.op("act", lambda e: e.activation(out=bt[0:R], in_=ga[0:R, 2:4, :], func=AF.Sigmoid),
                 reads=["g_ga"], writes=["g_bt"])
            for d in range(2):
                S.op("dve", lambda e, d=d: e.tensor_tensor_scan(out=nG[0:R, d, :], data0=self.ones[0:R, :],
                                                                data1=ng[0:R, d, :], initial=0.0, op0=ALU.mult,
                                                                op1=ALU.add),
                     reads=["g_ng", "ones_sb"], writes=["g_nG"])
            S.op("dve", lambda e: e.tensor_copy(gc[0:R, 8:9], nG[0:R, 1, 127:128]), reads=["g_nG"], writes=["g_gc"])
            S.op("dve", lambda e: e.tensor_tensor(out=nG[0:R, 1, :], in0=ng[0:R, 1, :], in1=nG[0:R, 1, :],
                                                  op=ALU.subtract), reads=["g_ng", "g_nG"], writes=["g_nG"])
            S.op("dve", lambda e: e.tensor_scalar(out=nG[0:R, 1, :], in0=nG[0:R, 1, :], scalar1=gc[0:R, 8:9],
                                                  scalar2=None, op0=ALU.add), reads=["g_nG", "g_gc"],
                 writes=["g_nG"])
            S.op("dve", lambda e: e.tensor_copy(gc[0:R, 0:1], nG[0:R, 0, 127:128]), reads=["g_nG"], writes=["g_gc"])
            S.op("dve", lambda e: e.tensor_copy(gc[0:R, 1:2], nG[0:R, 1, 0:1]), reads=["g_nG"], writes=["g_gc"])
            S.op("act", lambda e: e.activation(out=gc[0:R, 4:6], in_=gc[0:R, 0:2], func=AF.Exp, scale=-1.0),
                 reads=["g_gc"], writes=["g_gc"])
            S.op("dve", lambda e: e.tensor_scalar(out=gc[0:R, 2:4], in0=gc[0:R, 0:2], scalar1=-1.0, scalar2=None,
                                                  op0=ALU.mult), reads=["g_gc"], writes=["g_gc"])
            for d in range(2):
                S.op("act", lambda e, d=d: e.activation(out=kd[0:R, d, :], in_=nG[0:R, d, :], func=AF.Exp,
                                                        bias=gc[0:R, 2 + d:3 + d], scale=1.0),
                     reads=["g_nG", "g_gc"], writes=["g_kd"])
            for d in range(2):
                base = (rt * 2 + d) * 5
                for q, src in enumerate((nG, bt, kd)):
                    S.op("pe", lambda e, q=q, src=src, d=d: e.transpose(pm[:, q * 128:q * 128 + R], src[0:R, d, :],
                                                                        self.ident[0:R, 0:R]),
                         reads=[src.name, "ident_sb"], writes=["pmisc"])
                S.op("act", lambda e, base=base: e.copy(out=toksc[:, base:base + 3, 0:R],
                                                        in_=pm[:, 0:384].rearrange("p (q r) -> p q r", q=3)[:, :, 0:R]),
                     reads=["pmisc"], writes=["g_toksc"])
                S.op("act", lambda e, base=base: e.activation(out=toksc[:, base + 3, 0:R], in_=toksc[:, base, 0:R],
                                                              func=AF.Exp, scale=-1.0), reads=["g_toksc"],
                     writes=["g_toksc"])
                S.op("dve", lambda e, base=base: e.tensor_tensor(out=toksc[:, base + 4, 0:R], in0=toksc[:, base + 1, 0:R],
                                                                 in1=toksc[:, base + 3, 0:R], op=ALU.mult),
                     reads=["g_toksc"], writes=["g_toksc"])
                S.op("dve", lambda e, d=d: e.tensor_scalar(out=A["diag"][0:R, 0:R], in0=self.ident[0:R, 0:R],
                                                           scalar1=gc[0:R, 4 + d:5 + d], scalar2=None, op0=ALU.mult),
                     reads=["ident_sb", "g_gc"], writes=["g_diag"])
                p = self.psum()
                S.op("pe", lambda e, p=p: e.matmul(p[:, 0:R], self.ones[0:R, :], A["diag"][0:R, 0:R], start=True,
                                                   stop=True), reads=["ones_sb", "g_diag"], writes=[p.name])
                S.op("act", lambda e, p=p, rt=rt, d=d: e.copy(out=decB[:, rt * 2 + d, 0:R], in_=p[:, 0:R]),
                     reads=[p.name], writes=["g_decB"])

    def gd_group_dir(self, s, hg, d, A):
        S, ins = self.S, self.ins
        T, t0 = s["T"], s["t0"]
        nch = T // 128
        Sst, Sb = A["Sst"], A["Sb"]
        toksc, decB = A["toksc"], A["decB"]
        hv0 = hg * 4
        b4 = lambda ap: ap.unsqueeze(2).to_broadcast([128, 4, 128])
        m4 = lambda ap: ap.unsqueeze(1).to_broadcast([128, 4, 128])
        if s["kind"] == "s":
            S.dma("sp", Sst[:], ins["gdS0"][d, hv0:hv0 + 4].rearrange("i k v -> k i v"), writes=[A["Sst"].name])
        else:
            S.op("dve", lambda e: e.memset(Sst[:], 0.0), writes=[A["Sst"].name])
        S.op("act", lambda e: e.copy(out=Sb[:], in_=Sst[:]), reads=[A["Sst"].name], writes=[A["Sb"].name])
        order = list(range(nch)) if d == 0 else list(range(nch - 1, -1, -1))
        mincl = self.triu if d == 0 else self.tril
        mstr = self.triuS if d == 0 else self.trilS
        for c in order:
            u = A["uc"]
            A["uc"] += 1
            tok0 = t0 + c * 128
            rt, cl = c // 4, c % 4
            base = (rt * 2 + d) * 5
            r0 = cl * 32 + hv0
            col = lambda q: toksc[:, base + q, r0:r0 + 4]
            kT, qT, v = A["kT"][u % 2], A["qT"][u % 2], A["v"][u % 2]
            S.dma("sp", kT[:], self.GKT[hg * 256:(hg + 1) * 256, tok0:tok0 + 128].rearrange("(a p) t -> p a t", p=128),
                  reads=["GKT"], writes=[kT.name])
            S.dma("sp", qT[:], self.GQT[hg * 256:(hg + 1) * 256, tok0:tok0 + 128].rearrange("(a p) t -> p a t", p=128),
                  reads=["GQT"], writes=[qT.name])
            S.dma("sp", v[:], self.GV[tok0:tok0 + 128, hv0 * 128:(hv0 + 4) * 128].rearrange("t (i v) -> t i v", i=4),
                  reads=["GV"], writes=[v.name])
            ktm = A["ktm"]

            def trk(e, kT=kT):
                for a in range(2):
                    r = e.transpose(self.ptb[:, a * 128:(a + 1) * 128], kT[:, a, :], self.identb[:])
                return r
            S.op("pe", trk, reads=[kT.name, "identb"], writes=["ptb"])
            S.op("act", lambda e: e.copy(out=ktm[:], in_=self.ptb[:, 0:256].rearrange("p (a k) -> p a k", a=2)),
                 reads=["ptb"], writes=[A["ktm"].name])
            yield
            pK = self.psum_s(A)
            pQ = pK

            def mmG(e, pK=pK, kT=kT, qT=qT):
                for a in range(2):
                    e.matmul(pK[:, a * 128:(a + 1) * 128], kT[:, a, :], kT[:, a, :], start=True, stop=True)
                for a in range(2):
                    r = e.matmul(pK[:, 256 + a * 128:256 + (a + 1) * 128], qT[:, a, :], kT[:, a, :], start=True,
                                 stop=True)
                return r
            S.op("pe", mmG, reads=[kT.name, qT.name], writes=[pK.name])
            yield
            diagG, dd, dec, t1 = A["diagG"], A["dd"], A["dec"], A["t1"]
            S.op("dve", lambda e: e.tensor_tensor(out=diagG[:], in0=m4(self.ident[:, :]), in1=b4(col(0)), op=ALU.mult),
                 reads=["ident_sb", "g_toksc"], writes=[A["diagG"].name])
            pG = self.psum_s(A)
            S.op("pe", lambda e, pG=pG: e.matmul(pG[:, :], self.ones[:, :], diagG[:].rearrange("p i s -> p (i s)"),
                                                 start=True, stop=True), reads=["ones_sb", A["diagG"].name],
                 writes=[pG.name])
            pG3 = pG[:, :].rearrange("p (i s) -> p i s", i=4)
            S.op("dve", lambda e, pG3=pG3: e.tensor_tensor(out=dd[:], in0=pG3, in1=b4(col(0)), op=ALU.subtract),
                 reads=[pG.name, "g_toksc"], writes=[A["dd"].name])
            S.op("dve", lambda e: e.tensor_scalar(out=dd[:], in0=dd[:], scalar1=0.0, scalar2=None, op0=ALU.min),
                 reads=[A["dd"].name], writes=[A["dd"].name])
            S.op("act", lambda e: e.activation(out=dec[:], in_=dd[:], func=AF.Exp), reads=[A["dd"].name],
                 writes=[A["dec"].name])
            yield
            P, PT, N = A["P"], A["PT"], A["N"]
            pQ4 = pQ[:, 256:512].rearrange("p (a s) -> p a s", a=2).unsqueeze(2).to_broadcast([128, 2, 2, 128])
            pK4 = pK[:, 0:256].rearrange("p (a s) -> p a s", a=2).unsqueeze(2).to_broadcast([128, 2, 2, 128])
            v4 = lambda t: t[:].rearrange("p (a b) s -> p a b s", a=2)
            S.op("dve", lambda e, pQ4=pQ4: e.tensor_tensor(out=v4(t1), in0=pQ4, in1=v4(dec), op=ALU.mult),
                 reads=[pQ.name, A["dec"].name], writes=[A["t1"].name])
            S.op("pool", lambda e: e.tensor_tensor(out=P[:], in0=t1[:], in1=m4(mincl[:, :]), op=ALU.mult),
                 reads=[A["t1"].name, mincl.name], writes=[A["P"].name])
            S.op("dve", lambda e, pK4=pK4: e.tensor_tensor(out=v4(t1), in0=pK4, in1=v4(dec), op=ALU.mult),
                 reads=[pK.name, A["dec"].name], writes=[A["t1"].name])
            S.op("dve", lambda e: e.tensor_tensor(out=t1[:], in0=t1[:], in1=m4(mstr[:, :]), op=ALU.mult),
                 reads=[A["t1"].name, mstr.name], writes=[A["t1"].name])
            S.op("dve", lambda e: e.scalar_tensor_tensor(out=N[:], in0=t1[:], scalar=-1.0, in1=b4(col(1)),
                                                         op0=ALU.mult, op1=ALU.mult),
                 reads=[A["t1"].name, "g_toksc"], writes=[A["N"].name])
            yield
            Qs = A["Q"]

            def trN(e):
                for i in range(4):
                    e.transpose(self.ptb[:, i * 128:(i + 1) * 128], N[:, i, :], self.identb[:])
                for i in range(4):
                    r = e.transpose(self.ptb[:, 512 + i * 128:512 + (i + 1) * 128], P[:, i, :], self.identb[:])
                return r
            S.op("pe", trN, reads=[A["N"].name, A["P"].name, "identb"], writes=["ptb"])
            ptN = self.ptb[:, 0:512].rearrange("p (i s) -> p i s", i=4)
            ptP = self.ptb[:, 512:1024].rearrange("p (i s) -> p i s", i=4)
            Q = Qs[u % 2]
            S.op("act", lambda e: e.copy(out=Q[:], in_=ptN), reads=["ptb"], writes=[Q.name])
            S.op("act", lambda e: e.copy(out=PT[:], in_=ptP), reads=["ptb"], writes=[A["PT"].name])
            Tm, TTm = A["Tm"], A["TTm"]
            T, TT = Tm[0], TTm[0]
            for lev in range(7):
                mL = self.lvl[:, lev, :] if d == 0 else self.lvlT[:, lev, :]
                mLT = self.lvlT[:, lev, :] if d == 0 else self.lvl[:, lev, :]
                Tn, TTn = Tm[(lev + 1) % 2], TTm[(lev + 1) % 2]
                Lb, LTb, Yb, Y2b = A["Lb"][lev % 2], A["LTb"][lev % 2], A["Yb"][lev % 2], A["Y2b"][lev % 2]
                last = lev == 6
                if not last:
                    S.op("pool", lambda e, mL=mL: e.tensor_tensor(out=Lb[:], in0=N[:], in1=m4(mL), op=ALU.mult),
                         reads=[A["N"].name, "lvl"], writes=[Lb.name])
                S.op("pool", lambda e, mLT=mLT: e.tensor_tensor(out=LTb[:], in0=Q[:], in1=m4(mLT), op=ALU.mult),
                     reads=[Q.name, "lvl"], writes=[LTb.name])
                if lev == 0:
                    S.op("dve", lambda e, Tn=Tn: e.tensor_tensor(out=Tn[:], in0=Lb[:], in1=m4(self.ident[:, :]),
                                                                 op=ALU.add),
                         reads=[Lb.name, "ident_sb"], writes=[Tn.name])
                    S.op("dve", lambda e, TTn=TTn: e.tensor_tensor(out=TTn[:], in0=LTb[:], in1=m4(self.ident[:, :]),
                                                                   op=ALU.add),
                         reads=[LTb.name, "ident_sb"], writes=[TTn.name])
                    T, TT = Tn, TTn
                    continue
                yield
                py, py2 = self.psum_s(A), self.psum_s(A)

                def mmy(e, py=py, py2=py2, T=T, TT=TT, last=last):
                    for i in range(4):
                        r = e.matmul(py[:, i * 128:(i + 1) * 128], LTb[:, i, :], T[:, i, :], start=True, stop=True)
                    if not last:
                        for i in range(4):
                            r = e.matmul(py2[:, i * 128:(i + 1) * 128], Lb[:, i, :], TT[:, i, :], start=True,
                                         stop=True)
                    return r
                S.op("pe", mmy, reads=[Lb.name, LTb.name, T.name, TT.name], writes=[py.name, py2.name])
                S.op("act", lambda e, py=py: e.copy(out=Yb[:].rearrange("p i s -> p (i s)"), in_=py[:, :]),
                     reads=[py.name], writes=[Yb.name])
                if not last:
                    S.op("act", lambda e, py2=py2: e.copy(out=Y2b[:].rearrange("p i s -> p (i s)"), in_=py2[:, :]),
                         reads=[py2.name], writes=[Y2b.name])
                yield
                px, pxt = self.psum_s(A), self.psum_s(A)

                def mmx(e, px=px, pxt=pxt, T=T, TT=TT, last=last):
                    if not last:
                        for i in range(4):
                            e.matmul(px[:, i * 128:(i + 1) * 128], Y2b[:, i, :], T[:, i, :], start=True, stop=True)
                    for i in range(4):
                        r = e.matmul(pxt[:, i * 128:(i + 1) * 128], Yb[:, i, :], TT[:, i, :], start=True, stop=True)
                    return r
                S.op("pe", mmx, reads=[Yb.name, Y2b.name, T.name, TT.name], writes=[px.name, pxt.name])
                if not last:
                    S.op("dve", lambda e, px=px, T=T, Tn=Tn: e.tensor_tensor(
                        out=Tn[:].rearrange("p i s -> p (i s)"), in0=px[:, :], in1=T[:].rearrange("p i s -> p (i s)"),
                        op=ALU.add), reads=[px.name, T.name], writes=[Tn.name])
                S.op("dve", lambda e, pxt=pxt, TT=TT, TTn=TTn: e.tensor_tensor(
                    out=TTn[:].rearrange("p i s -> p (i s)"), in0=pxt[:, :], in1=TT[:].rearrange("p i s -> p (i s)"),
                    op=ALU.add), reads=[pxt.name, TT.name], writes=[TTn.name])
                T, TT = Tn, TTn
            R = TT
            yield
            kb, vb, WTn, U, kdk = A["kb"], A["vb"], A["WTn"], A["U"], A["kdk"]
            k4 = ktm[:].unsqueeze(2).to_broadcast([128, 2, 2, 128])
            S.op("pool", lambda e: e.tensor_tensor(out=v4(kb), in0=k4,
                                                  in1=col(4).rearrange("p (a b) -> p a b", a=2).unsqueeze(3).to_broadcast(
                                                      [128, 2, 2, 128]), op=ALU.mult),
                 reads=[A["ktm"].name, "g_toksc"], writes=[A["kb"].name])
            S.op("pool", lambda e, v=v: e.tensor_tensor(out=vb[:], in0=v[:], in1=b4(col(1)), op=ALU.mult),
                 reads=[v.name, "g_toksc"], writes=[A["vb"].name])
            S.op("pool", lambda e: e.tensor_tensor(out=v4(kdk), in0=k4,
                                                  in1=col(2).rearrange("p (a b) -> p a b", a=2).unsqueeze(3).to_broadcast(
                                                      [128, 2, 2, 128]), op=ALU.mult),
                 reads=[A["ktm"].name, "g_toksc"], writes=[A["kdk"].name])
            pW = self.psum_s(A)

            def mmW(e, pW=pW, R=R):
                for i in range(4):
                    r = e.matmul(pW[:, i * 128:(i + 1) * 128], kb[:, i, :], R[:, i, :], start=True, stop=True)
                return r
            S.op("pe", mmW, reads=[A["kb"].name, R.name], writes=[pW.name])
            S.op("act", lambda e, pW=pW: e.activation(out=WTn[:].rearrange("p i s -> p (i s)"), in_=pW[:, :],
                                                      func=AF.Copy, scale=-1.0), reads=[pW.name], writes=[A["WTn"].name])
            yield
            pU = self.psum_s(A)

            def mmU(e, pU=pU, R=R):
                for i in range(4):
                    e.matmul(pU[:, i * 128:(i + 1) * 128], R[:, i, :], vb[:, i, :], start=True, stop=False)
                    r = e.matmul(pU[:, i * 128:(i + 1) * 128], WTn[:, i, :], Sb[:, i, :], start=False, stop=True)
                return r
            S.op("pe", mmU, reads=[R.name, A["vb"].name, A["WTn"].name, A["Sb"].name], writes=[pU.name])
            S.op("act", lambda e, pU=pU: e.copy(out=U[:].rearrange("p i s -> p (i s)"), in_=pU[:, :]),
                 reads=[pU.name], writes=[A["U"].name])
            yield
            pO1, pO2 = self.psum_s(A), self.psum_s(A)

            def mmO(e, pO1=pO1, pO2=pO2, qT=qT):
                for i in range(4):
                    e.matmul(pO1[:, i * 128:(i + 1) * 128], qT[:, i // 2, :], Sb[:, i, :], start=True, stop=True)
                for i in range(4):
                    r = e.matmul(pO2[:, i * 128:(i + 1) * 128], PT[:, i, :], U[:, i, :], start=True, stop=True)
                return r
            S.op("pe", mmO, reads=[qT.name, A["Sb"].name, A["PT"].name, A["U"].name], writes=[pO1.name, pO2.name])
            o = A["o"]
            o3 = lambda p: p[:, :].rearrange("p (i s) -> p i s", i=4)
            S.op("dve", lambda e, pO1=pO1: e.tensor_tensor(out=o[:], in0=o3(pO1), in1=b4(col(3)), op=ALU.mult),
                 reads=[pO1.name, "g_toksc"], writes=[A["o"].name])
            osb = A["OSb"][u % 2]
            OSc = osb[:].rearrange("p (i s) -> p i s", i=4)
            osd = self.OSD[A["sid"], c * 128:(c + 1) * 128, :]
            oskey = "OSD%d_%d" % (A["sid"], c)
            if d == 1:
                S.op("dve", lambda e, pO2=pO2, OSc=OSc: e.tensor_tensor(out=OSc, in0=o3(pO2), in1=o[:], op=ALU.add),
                     reads=[pO2.name, A["o"].name], writes=[osb.name])
                S.defer_dma("sp", osd, osb[:], reads=[osb.name], writes=[oskey])
            else:
                S.dma("sp", osb[:], osd, reads=[oskey], writes=[osb.name])
                S.op("dve", lambda e, pO2=pO2: e.tensor_tensor(out=o[:], in0=o3(pO2), in1=o[:], op=ALU.add),
                     reads=[pO2.name, A["o"].name], writes=[A["o"].name])
                S.op("dve", lambda e, OSc=OSc: e.tensor_tensor(out=o[:], in0=o[:], in1=OSc, op=ALU.add),
                     reads=[A["o"].name, osb.name], writes=[A["o"].name])
                self.gd_finalize(hv0, tok0, A, u)
            yield
            pS = self.psum_s(A)

            def mmS(e, pS=pS):
                for i in range(4):
                    r = e.matmul(pS[:, i * 128:(i + 1) * 128], kdk[:, i, :], U[:, i, :], start=True, stop=True)
                return r
            S.op("pe", mmS, reads=[A["kdk"].name, A["U"].name], writes=[pS.name])
            S.op("dve", lambda e, rt=rt, r0=r0: e.tensor_tensor(out=Sst[:], in0=Sst[:],
                                                                in1=b4(decB[:, rt * 2 + d, r0:r0 + 4]), op=ALU.mult),
                 reads=[A["Sst"].name, "g_decB"], writes=[A["Sst"].name])
            S.op("dve", lambda e, pS=pS: e.tensor_tensor(out=Sst[:], in0=Sst[:], in1=o3(pS), op=ALU.add),
                 reads=[A["Sst"].name, pS.name], writes=[A["Sst"].name])
            S.op("act", lambda e: e.copy(out=Sb[:], in_=Sst[:]), reads=[A["Sst"].name], writes=[A["Sb"].name])
        if s["kind"] == "p":
            S.dma("sp", self.st_S[s["pi"], 0, d, hv0:hv0 + 4].rearrange("i k v -> k i v"), Sst[:],
                  reads=[A["Sst"].name], writes=["st_S"])

    def gd_finalize(self, hv0, tok0, A, u):
        S = self.S
        o, sq, ssq, y = A["o"], A["sq"], A["ssq"], A["y"]
        z, ytr = A["z"][u % 2], A["ytr"][u % 2]
        b4 = lambda ap: ap.unsqueeze(2).to_broadcast([128, 4, 128])
        m4 = lambda ap: ap.unsqueeze(1).to_broadcast([128, 4, 128])
        S.dma("sp", z[:], self.GZ[tok0:tok0 + 128, hv0 * 128:(hv0 + 4) * 128].rearrange("t (i v) -> t i v", i=4),
              reads=["GZ"], writes=[z.name])
        S.op("act", lambda e: e.activation(out=sq[:], in_=o[:], func=AF.Square), reads=[A["o"].name], writes=[A["sq"].name])
        S.op("dve", lambda e: e.tensor_reduce(out=ssq[:, 0:4], in_=sq[:], axis=AX.X, op=ALU.add), reads=[A["sq"].name],
             writes=[A["ssq"].name])
        S.op("dve", lambda e: e.tensor_scalar(out=ssq[:, 0:4], in0=ssq[:, 0:4], scalar1=1.0 / 128, scalar2=EPS,
                                              op0=ALU.mult, op1=ALU.add), reads=[A["ssq"].name], writes=[A["ssq"].name])
        S.op("act", lambda e: e.activation(out=ssq[:, 0:4], in_=ssq[:, 0:4], func=AF.Sqrt), reads=[A["ssq"].name],
             writes=[A["ssq"].name])
        S.op("dve", lambda e: e.reciprocal(out=ssq[:, 0:4], in_=ssq[:, 0:4]), reads=[A["ssq"].name], writes=[A["ssq"].name])
        S.op("dve", lambda e: e.tensor_tensor(out=o[:], in0=o[:], in1=b4(ssq[:, 0:4]), op=ALU.mult),
             reads=[A["o"].name, A["ssq"].name], writes=[A["o"].name])
        S.op("dve", lambda e: e.tensor_tensor(out=o[:], in0=o[:], in1=m4(A["gnB"][:, :]), op=ALU.mult),
             reads=[A["o"].name, "g_gnB"], writes=[A["o"].name])
        S.op("act", lambda e, z=z: e.activation(out=sq[:], in_=z[:], func=AF.Silu), reads=[z.name], writes=[A["sq"].name])
        S.op("dve", lambda e: e.tensor_tensor(out=y[:], in0=o[:], in1=sq[:], op=ALU.mult), reads=[A["o"].name, A["sq"].name],
             writes=[A["y"].name])

        def tr(e):
            for i in range(4):
                r = e.transpose(self.ptb[:, i * 128:(i + 1) * 128], y[:, i, :], self.identb[:])
            return r
        S.op("pe", tr, reads=[A["y"].name, "identb"], writes=["ptb"])
        S.op("act", lambda e: e.copy(out=ytr[:].rearrange("p i s -> p (i s)"), in_=self.ptb[:, 0:512]),
             reads=["ptb"], writes=[ytr.name])
        S.defer_dma("sp", self.YT[hv0 * 128:(hv0 + 4) * 128, tok0:tok0 + 128].rearrange("(i p) t -> p i t", p=128),
                    ytr[:], reads=[ytr.name], writes=["YT"])

    def phase_d(self, l, wout, sb):
        S = self.S
        NTT = self.NT // 128
        gateb = sb("gateb", [128, D], F32)
        wd = sb("wd", [128, 32, D], BF16)
        ytile = [sb("ytile%d" % i, [128, 32, 128], BF16) for i in range(3)]
        xbs = [sb("xb%d" % i, [128, D], F32) for i in range(2)]
        tmp = [sb("tmpd%d" % i, [128, 512], F32) for i in range(2)]
        src = wout.rearrange("(kt p) c -> p kt c", p=128)
        for q in range(8):
            S.dma("pool", wd[:, q * 4:(q + 1) * 4, :], src[:, q * 4:(q + 1) * 4, :], writes=["wd_%d" % q])
        wkeys = ["wd_%d" % q for q in range(8)]
        cur = None
        k = 0
        for tt in range(NTT):
            c = self.tile_cond(tt)
            if c != cur:
                cur = c
                S.dma("sp", gateb[:], self.MOD[c:c + 1, 2 * D:3 * D].partition_broadcast(128), reads=["MOD"],
                      writes=["gateb"])
            yt = ytile[tt % 3]
            xb = xbs[tt % 2]
            S.dma("sp", yt[:], self.YT[:, tt * 128:(tt + 1) * 128].rearrange("(kt p) t -> p kt t", p=128),
                  reads=["YT"], writes=[yt.name])
            S.dma("sp", xb[:], self.X[tt * 128:(tt + 1) * 128, :], reads=["X"], writes=[xb.name])
            for cg in range(4):
                p = self.psum()

                def mm(e, p=p, yt=yt, cg=cg):
                    for kt in range(32):
                        r = e.matmul(p[:, :], yt[:, kt, :], wd[:, kt, cg * 512:(cg + 1) * 512], start=(kt == 0),
                                     stop=(kt == 31))
                    return r
                S.op("pe", mm, reads=[yt.name] + wkeys, writes=[p.name])
                tm = tmp[k % 2]
                k += 1
                S.op("dve", lambda e, p=p, cg=cg, tm=tm: e.tensor_tensor(
                    out=tm[:], in0=p[:, :], in1=gateb[:, cg * 512:(cg + 1) * 512], op=ALU.mult),
                    reads=[p.name, "gateb"], writes=[tm.name])
                S.op("pool", lambda e, xb=xb, cg=cg, tm=tm: e.tensor_tensor(
                    out=xb[:, cg * 512:(cg + 1) * 512], in0=xb[:, cg * 512:(cg + 1) * 512], in1=tm[:], op=ALU.add),
                    reads=[xb.name, tm.name], writes=[xb.name])
            S.dma("sp", self.X[tt * 128:(tt + 1) * 128, :], xb[:], reads=[xb.name], writes=["X"])

    def final_norm(self):
        S, ins = self.S, self.ins
        NTT = self.NT // 128
        with ExitStack() as stk:
            sb = self.sbp(stk)
            big = [sb("fbig%d" % i, [128, D], F32) for i in range(3)]
            gB = sb("gfin", [128, D], F32)
            S.dma("sp", gB[:], ins["g_final"][0:1, :].partition_broadcast(128), writes=["gfin"])
            for tt in range(NTT):
                xt = big[tt % 2]
                S.dma("sp", xt[:], self.X[tt * 128:(tt + 1) * 128, :], reads=["X"], writes=[xt.name])
                ss = self.small[:, 0:1]
                rs = self.small[:, 1:2]
                junk = big[2]
                S.op("act", lambda e, xt=xt: e.activation(out=junk[:], in_=xt[:], func=AF.Square, accum_out=ss),
                     reads=[xt.name], writes=["fbig2", "small"])
                self.rstd_col(ss, rs, D)
                S.op("dve", lambda e, xt=xt: e.scalar_tensor_tensor(out=xt[:], in0=xt[:], scalar=rs, in1=gB[:],
                                                                    op0=ALU.mult, op1=ALU.mult),
                     reads=[xt.name, "small", "gfin"], writes=[xt.name])
                S.dma("sp", self.y_out[tt * 128:(tt + 1) * 128, :], xt[:], reads=[xt.name], writes=["y_out"])
            S.barrier()


def host_consts():
    s = np.arange(128)
    tril = (s[:, None] <= s[None, :]).astype(np.float32)
    triu = (s[:, None] >= s[None, :]).astype(np.float32)
    trilS = (s[:, None] < s[None, :]).astype(np.float32)
    triuS = (s[:, None] > s[None, :]).astype(np.float32)
    lvl = np.zeros((128, 7, 128), np.float32)
    for li in range(7):
        b = 1 << li
        for i in range(0, 128, 2 * b):
            lvl[i + b:i + 2 * b, li, i:i + b] = 1.0
    lvlT = np.ascontiguousarray(lvl.transpose(2, 1, 0))
    return {"ident": np.eye(128, dtype=np.float32), "tril": tril, "triu": triu, "trilS": trilS, "triuS": triuS,
            "lvl": lvl, "lvlT": lvlT,
            "ones": np.ones((128, 128), np.float32)}


def make_inputs(inp, core, x_in=None, b=None):
    if b is None:
        b = core // 4
    cond = np.stack([inp["c_ctx"], inp["c"][b]], 0)
    condT = np.ascontiguousarray(cond.reshape(2, 16, 128).transpose(2, 1, 0))
    im = dict(host_consts())
    if x_in is None:
        xp = inp["x_prompt"][2 * core:2 * core + 2].reshape(512, D)
        xs = inp["x_sample"][b]
        x_in = np.concatenate([xp, xs], 0)
    im.update(
        x_in=np.ascontiguousarray(x_in), condT=condT, w_ada=inp["w_ada"], b_ada=inp["b_ada"],
        g_norm=inp["g_norm"], g_final=inp["g_final"][None, :],
        w_sc_in=inp["w_sc_in"][0],
        w_sc_conv=np.ascontiguousarray(inp["w_sc_conv"][0].reshape(3, 32, 128).transpose(2, 1, 0)),
        w_sc_out=inp["w_sc_out"][0],
        w_ml_in=inp["w_ml_in"],
        b_mlg=np.ascontiguousarray(inp["b_ml_gate"].reshape(2, 4, 8).transpose(0, 2, 1)),
        g_ml_head=inp["g_ml_head"], w_ml_out=inp["w_ml_out"],
        mlC0=inp["cache_ml_C"][b], mln0=inp["cache_ml_n"][b], mlm0=inp["cache_ml_m"][b],
        w_gd_in=inp["w_gd_in"][0],
        w_gd_conv=np.ascontiguousarray(inp["w_gd_conv"][0].reshape(3, 64, 128).transpose(2, 1, 0)),
        gd_par=np.ascontiguousarray(np.tile(np.concatenate([inp["gd_A_log"][0].T, inp["gd_dt_bias"][0].T], 1), (4, 1))),
        g_gd_norm=inp["g_gd_norm"], w_gd_out=inp["w_gd_out"][0], gdS0=inp["cache_gd_S"][b, 0],
    )
    return im


SEQS = [dict(T=256, cond=0, roww=256, kind="p"), dict(T=256, cond=0, roww=256, kind="p"),
        dict(T=2048, cond=1, roww=64, kind="s")]


def kernel(**inputs):
    inp = {k: np.ascontiguousarray(np.asarray(v)) for k, v in inputs.items()}
    kb = K(SEQS, [0, 1, 2, 3], do_final=True)
    nc = kb.build()
    in_maps = []
    for core in range(8):
        im = make_inputs(inp, core)
        in_maps.append({k: np.ascontiguousarray(v, dtype=np.float32) for k, v in im.items() if k in kb.ins})
    res = run_bass_kernel_spmd(nc, in_maps, core_ids=list(range(8)))
    r = res.results
    y_prompt = np.concatenate([r[c]["y_out"][:512].reshape(2, 256, D) for c in range(8)], 0)
    y_sample = np.stack([r[0]["y_out"][512:], r[4]["y_out"][512:]], 0)
    st_C = np.concatenate([r[c]["st_C"] for c in range(8)], 0)
    st_n = np.concatenate([r[c]["st_n"] for c in range(8)], 0)
    st_m = np.concatenate([r[c]["st_m"] for c in range(8)], 0)
    st_S = np.concatenate([r[c]["st_S"] for c in range(8)], 0)
    return (y_prompt.astype(np.float32), y_sample.astype(np.float32), st_C.astype(np.float32),
            st_n.astype(np.float32), st_m.astype(np.float32), st_S.astype(np.float32))
```

```python
import math
from contextlib import ExitStack
import numpy as np
import concourse.bass as bass
import concourse.mybir as mybir
from concourse.bass_utils import run_bass_kernel_spmd

F32 = mybir.dt.float32
BF16 = mybir.dt.bfloat16
AF = mybir.ActivationFunctionType
ALU = mybir.AluOpType
AX = mybir.AxisListType

D = 2048
KT = D // 128
EPS = 1e-6
INNER = 4096
ML_H, ML_DK, ML_DV = 8, 256, 512
ML_COLS = 16416
SC_COLS = 16384
GD_COLS = 12416
GD_HV, GD_HQ = 32, 16
CH = 128


class Sched:
    def __init__(self, nc, stack, ndma=30):
        self.nc = nc
        self.E = {"pe": nc.tensor, "dve": nc.vector, "act": nc.scalar, "pool": nc.gpsimd, "sp": nc.sync}
        self.sem = {e: stack.enter_context(nc.semaphore("s_" + e)) for e in ("pe", "dve", "act", "pool")}
        self.cnt = {e: 0 for e in self.sem}
        self.dsem = [stack.enter_context(nc.semaphore("d%d" % i)) for i in range(ndma)]
        self.dcnt = [0] * ndma
        self.drr = 0
        self.drr_pool = 0
        self.known = {e: {} for e in self.E}
        self.last_w = {}
        self.readers = {}
        self.nwaits = 0
        self.ninst = 0
        self.deferred = []
        self.round = 0

    def _semobj(self, key):
        return self.sem[key] if isinstance(key, str) else self.dsem[key]

    def _wait(self, eng, tok):
        key, val = tok
        if eng == "pe" and key == "pe":
            return
        if self.known[eng].get(key, 0) >= val:
            return
        self.E[eng].wait_ge(self._semobj(key), val)
        self.known[eng][key] = val
        self.nwaits += 1

    def _deps(self, eng, reads, writes):
        for r in reads:
            lw = self.last_w.get(r)
            if lw:
                self._wait(eng, lw)
        for w in writes:
            lw = self.last_w.get(w)
            if lw:
                self._wait(eng, lw)
            for key, val in self.readers.get(w, {}).items():
                self._wait(eng, (key, val))

    def _record(self, tok, reads, writes):
        key, val = tok
        for r in reads:
            d = self.readers.setdefault(r, {})
            d[key] = max(d.get(key, 0), val)
        for w in writes:
            self.last_w[w] = tok
            self.readers[w] = {}

    def op(self, eng, fn, reads=(), writes=()):
        self._deps(eng, reads, writes)
        inst = fn(self.E[eng])
        self.cnt[eng] += 1
        inst.then_inc(self.sem[eng], 1)
        self.ninst += 1
        self._record((eng, self.cnt[eng]), reads, writes)

    def dma(self, q, out, in_, reads=(), writes=(), slow=False):
        if self.deferred and reads:
            rs = set(reads)
            hit = [it for it in self.deferred if rs & set(it[2].get("writes", ()))]
            if hit:
                self.deferred = [it for it in self.deferred if not (rs & set(it[2].get("writes", ())))]
                for it in hit:
                    self.dma(*it[1], **it[2])
        self._deps(q, reads, writes)
        npool = 6
        if q == "pool":
            i = self.drr_pool
            self.drr_pool = (self.drr_pool + 1) % npool
        else:
            i = npool + self.drr
            self.drr = (self.drr + 1) % (len(self.dsem) - npool)
        if self.dcnt[i]:
            self._wait(q, (i, self.dcnt[i]))
        self.dcnt[i] += 16
        if slow:
            self.E[q].dma_start(out=out, in_=in_, allow_slow_non_contiguous=True).then_inc(self.dsem[i], 16)
        else:
            self.E[q].dma_start(out=out, in_=in_).then_inc(self.dsem[i], 16)
        self.ninst += 1
        self._record((i, self.dcnt[i]), reads, writes)

    def barrier(self):
        self.flush()
        for e in self.E:
            for i, c in enumerate(self.dcnt):
                if c:
                    self._wait(e, (i, c))
            for k, c in self.cnt.items():
                if c and k != e:
                    self._wait(e, (k, c))
        for e in self.sem:
            if self.cnt[e]:
                self._wait(e, (e, self.cnt[e])) if e != "pe" else None
        self.last_w = {}
        self.readers = {}

    def defer_dma(self, *a, **kw):
        self.deferred.append([self.round + 3, a, kw])

    def tick(self):
        self.round += 1
        keep = []
        for it in self.deferred:
            if it[0] <= self.round:
                self.dma(*it[1], **it[2])
            else:
                keep.append(it)
        self.deferred = keep

    def flush(self):
        for it in self.deferred:
            self.dma(*it[1], **it[2])
        self.deferred = []

    def finish(self, eng="sp"):
        for i, c in enumerate(self.dcnt):
            if c:
                self._wait(eng, (i, c))
        for e, c in self.cnt.items():
            if c:
                self._wait(eng, (e, c))


def run_streams(gens, S=None):
    gens = list(gens)
    while gens:
        for g in list(gens):
            try:
                next(g)
            except StopIteration:
                gens.remove(g)
        if S is not None:
            S.tick()
    if S is not None:
        S.flush()


class TN:
    def __init__(self, name, t):
        self.name = name
        self.t = t

    def __getitem__(self, k):
        return self.t[k]


class K:
    def __init__(self, seqs, layers, do_final=True):
        self.seqs = []
        t0 = 0
        npi = 0
        for s in seqs:
            s = dict(s)
            s["t0"] = t0
            t0 += s["T"]
            if s["kind"] == "p":
                s["pi"] = npi
                npi += 1
            self.seqs.append(s)
        self.NT = t0
        assert self.NT % 512 == 0
        self.layers = layers
        self.do_final = do_final
        self.n_p = npi
        self.Tmax = max(s["T"] for s in self.seqs)

    def build(self):
        nc = bass.Bass("TRN2", target_bir_lowering=False)
        self.nc = nc
        NT = self.NT
        dt = nc.dram_tensor
        ins = {}

        def inp(name, shape, dtype=F32):
            ins[name] = dt(name, list(shape), dtype, kind="ExternalInput").ap()
            return ins[name]

        def scratch(name, shape, dtype):
            return dt(name, list(shape), dtype, kind="Internal").ap()

        def outp(name, shape):
            return dt(name, list(shape), F32, kind="ExternalOutput").ap()

        self.ins = ins
        inp("x_in", [NT, D])
        inp("condT", [128, KT, 2])
        inp("ident", [128, 128])
        inp("tril", [128, 128])
        inp("triu", [128, 128])
        inp("ones", [128, 128])
        inp("w_ada", [4, D, 3 * D])
        inp("b_ada", [4, 3 * D])
        inp("g_norm", [4, D])
        inp("g_final", [1, D])
        kinds = set(l % 3 for l in self.layers)
        self.kinds = kinds
        if 1 in kinds:
            inp("w_sc_in", [D, SC_COLS])
            inp("w_sc_conv", [128, 32, 3])
            inp("w_sc_out", [INNER, D])
        if 0 in kinds:
            inp("w_ml_in", [2, D, ML_COLS])
            inp("b_mlg", [2, 8, 4])
            inp("g_ml_head", [2, INNER])
            inp("w_ml_out", [2, INNER, D])
            inp("mlC0", [2, 2, 8, ML_DK, ML_DV])
            inp("mln0", [2, 2, 8, ML_DK])
            inp("mlm0", [2, 2, 8])
            self.QT = scratch("QT", [2048, NT], BF16)
            self.KTd = scratch("KTd", [2048, NT], BF16)
            self.VOZ = scratch("VOZ", [3, NT, INNER], BF16)
            self.GATES = scratch("GATES", [4, 8, NT], F32)
            self.st_C = outp("st_C", [self.n_p, 2, 2, 8, ML_DK, ML_DV])
            self.st_n = outp("st_n", [self.n_p, 2, 2, 8, ML_DK])
            self.st_m = outp("st_m", [self.n_p, 2, 2, 8])
        if 2 in kinds:
            inp("w_gd_in", [D, GD_COLS])
            inp("w_gd_conv", [128, 64, 3])
            inp("gd_par", [128, 4])
            inp("g_gd_norm", [1, 128])
            inp("w_gd_out", [INNER, D])
            inp("gdS0", [2, 32, 128, 128])
            inp("trilS", [128, 128])
            inp("lvl", [128, 7, 128])
            inp("lvlT", [128, 7, 128])
            inp("triuS", [128, 128])
            self.GQT = scratch("GQT", [2048, NT], BF16)
            self.GKT = scratch("GKT", [2048, NT], BF16)
            self.GV = scratch("GV", [NT, INNER], BF16)
            self.GZ = scratch("GZ", [NT, INNER], BF16)
            self.GAB = scratch("GAB", [4, 32, NT], F32)
            self.st_S = outp("st_S", [self.n_p, 1, 2, 32, 128, 128])
        self.X = scratch("X", [NT, D], F32)
        self.YT = scratch("YT", [INNER, NT], BF16)
        self.MOD = scratch("MOD", [2, 3 * D], F32)
        self.OSD = scratch("OSD", [4, self.Tmax, 512], F32)
        self.y_out = outp("y_out", [NT, D])

        with ExitStack() as st:
            self.S = Sched(nc, st)
            self._uid = [0]

            def _mk(stk, name, shape, dtype):
                self._uid[0] += 1
                return TN(name, stk.enter_context(nc.sbuf_tensor("%s_%d" % (name, self._uid[0]), list(shape), dtype)))
            self.sbp = lambda stk: (lambda name, shape, dtype: _mk(stk, name, shape, dtype))
            sb = self.sbp(st)
            ps = lambda name, shape, dtype: TN(name, st.enter_context(nc.psum_tensor(name, list(shape), dtype)))
            self.ident = sb("ident_sb", [128, 128], F32)
            self.identb = sb("identb", [128, 128], BF16)
            self.tril = sb("tril_sb", [128, 128], F32)
            self.triu = sb("triu_sb", [128, 128], F32)
            self.ones = sb("ones_sb", [128, 128], F32)
            if 2 in self.kinds:
                self.trilS = sb("trilS_sb", [128, 128], F32)
                self.triuS = sb("triuS_sb", [128, 128], F32)
                self.lvl = sb("lvl_sb", [128, 7, 128], BF16)
                self.lvlT = sb("lvlT_sb", [128, 7, 128], BF16)
            self.condT = sb("condT_sb", [128, KT, 2], F32)
            self.scT = sb("scT", [128, KT, 2], BF16)
            self.small = sb("small", [128, 64], F32)
            self.stg = [sb("stg%d" % i, [128, 512], BF16) for i in range(4)]
            self.stgf = [sb("stgf%d" % i, [128, 512], F32) for i in range(2)]
            self.modst = [sb("modst%d" % i, [1, 2, 512], F32) for i in range(2)]
            self.pp = [ps("pp%d" % i, [128, 512], F32) for i in range(6)]
            self.ptb = ps("ptb", [128, 1024], BF16)
            self.pmisc = ps("pmisc", [128, 512], F32)
            self.ppi = 0
            self.stgi = 0
            self.emit()
            self.S.finish("sp")
        return nc

    def psum(self):
        p = self.pp[self.ppi % len(self.pp)]
        self.ppi += 1
        return p

    def psum_s(self, A):
        p = A["pp"][A["ppi"] % len(A["pp"])]
        A["ppi"] += 1
        return p

    def stage(self):
        p = self.stg[self.stgi % len(self.stg)]
        self.stgi += 1
        return p

    def emit(self):
        S, nc, ins = self.S, self.nc, self.ins
        S.dma("sp", self.ident[:], ins["ident"][:, :], writes=["ident_sb"])
        S.dma("sp", self.tril[:], ins["tril"][:, :], writes=["tril_sb"])
        S.dma("sp", self.triu[:], ins["triu"][:, :], writes=["triu_sb"])
        S.dma("sp", self.ones[:], ins["ones"][:, :], writes=["ones_sb"])
        S.dma("sp", self.condT[:], ins["condT"][:, :, :], writes=["condT_sb"])
        if "trilS" in ins:
            S.dma("sp", self.trilS[:], ins["trilS"][:, :], writes=["trilS_sb"])
            S.dma("sp", self.triuS[:], ins["triuS"][:, :], writes=["triuS_sb"])
            S.dma("pool", self.lvl[:], ins["lvl"][:, :, :], writes=["lvl"])
            S.dma("pool", self.lvlT[:], ins["lvlT"][:, :, :], writes=["lvl"])
        S.op("dve", lambda e: e.tensor_copy(self.identb[:], self.ident[:]), reads=["ident_sb"], writes=["identb"])
        S.op("act", lambda e: e.activation(out=self.scT[:], in_=self.condT[:], func=AF.Silu),
             reads=["condT_sb"], writes=["scT"])
        S.dma("sp", self.X[:, :], ins["x_in"][:, :], writes=["X"])
        for l in self.layers:
            self.layer(l)
        if self.do_final:
            self.final_norm()
        else:
            S.dma("sp", self.y_out[:, :], self.X[:, :], reads=["X"], writes=["y_out"])

    def layer(self, l):
        S, ins = self.S, self.ins
        kind = l % 3
        j = l // 3
        with ExitStack() as stk:
            sb = self.sbp(stk)
            self.wbuf = [sb("wbuf%d" % i, [128, KT, 512], BF16) for i in range(2)]
            self.compute_mod(l)
            self.hT = sb("hT", [128, KT, self.NT], BF16)
            self.phase_a(l, sb)
            if kind == 1:
                with ExitStack() as stk2:
                    self.sc_mixer(self.sbp(stk2))
                    S.barrier()
            elif kind == 0:
                self.ml_proj(j)
                S.barrier()
            else:
                with ExitStack() as stk2:
                    self.gd_proj(self.sbp(stk2))
                    S.barrier()
        if kind == 0:
            with ExitStack() as stk:
                self.ml_scan(j, self.sbp(stk))
                S.barrier()
        if kind == 2:
            with ExitStack() as stk:
                self.gd_scan(self.sbp(stk))
                S.barrier()
        wout = {0: lambda: ins["w_ml_out"][j], 1: lambda: ins["w_sc_out"], 2: lambda: ins["w_gd_out"]}[kind]()
        with ExitStack() as stk:
            self.phase_d(l, wout, self.sbp(stk))
            S.barrier()

    def compute_mod(self, l):
        S, ins = self.S, self.ins
        for cb in range(12):
            wb = self.wbuf[cb % 2]
            src = ins["w_ada"][l, :, cb * 512:(cb + 1) * 512].rearrange("(kt p) c -> p kt c", p=128)
            S.dma("pool", wb[:], src, writes=[wb.name])
            bb = self.stgf[cb % 2]
            S.dma("sp", bb[0:1, :], ins["b_ada"][l:l + 1, cb * 512:(cb + 1) * 512], writes=[bb.name])
            ms = self.modst[cb % 2]
            for c in range(2):
                p = self.psum()

                def mm(e, p=p, wb=wb, c=c):
                    for kt in range(KT):
                        r = e.matmul(p[0:1, :], self.scT[:, kt, c:c + 1], wb[:, kt, :], start=(kt == 0),
                                     stop=(kt == KT - 1))
                    return r
                S.op("pe", mm, reads=["scT", wb.name], writes=[p.name])
                S.op("dve", lambda e, p=p, bb=bb, ms=ms, c=c: e.tensor_tensor(
                    out=ms[0:1, c, :], in0=p[0:1, :], in1=bb[0:1, :], op=ALU.add),
                    reads=[p.name, bb.name], writes=[ms.name])
                S.dma("sp", self.MOD[c:c + 1, cb * 512:(cb + 1) * 512], ms[0:1, c, :],
                      reads=[ms.name], writes=["MOD"])

    def tile_cond(self, tt):
        tok = tt * 128
        for s in self.seqs:
            if s["t0"] <= tok < s["t0"] + s["T"]:
                return s["cond"]
        raise AssertionError

    def rstd_col(self, ss, rs, n, key="small"):
        S = self.S
        S.op("dve", lambda e: e.tensor_scalar(out=rs, in0=ss, scalar1=1.0 / n, scalar2=EPS,
                                              op0=ALU.mult, op1=ALU.add), reads=[key], writes=[key])
        S.op("act", lambda e: e.activation(out=rs, in_=rs, func=AF.Sqrt), reads=[key], writes=[key])
        S.op("dve", lambda e: e.reciprocal(out=rs, in_=rs), reads=[key], writes=[key])

    def phase_a(self, l, sb):
        S, ins = self.S, self.ins
        big = [sb("big%d" % i, [128, D], F32) for i in range(3)]
        hbf = sb("hbf", [128, D], BF16)
        amod = sb("amod", [128, D], F32)
        shiftb = sb("shiftb", [128, D], F32)
        NTT = self.NT // 128
        cur = None
        for tt in range(NTT):
            c = self.tile_cond(tt)
            if c != cur:
                cur = c
                S.dma("sp", shiftb[:], self.MOD[c:c + 1, 0:D].partition_broadcast(128), reads=["MOD"],
                      writes=["shiftb"])
                S.dma("sp", amod[:], self.MOD[c:c + 1, D:2 * D].partition_broadcast(128), reads=["MOD"],
                      writes=["amod"])
                S.dma("sp", big[2][:], ins["g_norm"][l:l + 1, :].partition_broadcast(128), writes=["big2"])
                S.op("dve", lambda e: e.scalar_tensor_tensor(out=amod[:], in0=amod[:], scalar=1.0, in1=big[2][:],
                                                             op0=ALU.add, op1=ALU.mult),
                     reads=["amod", "big2"], writes=["amod"])
            xt = big[tt % 2]
            S.dma("sp", xt[:], self.X[tt * 128:(tt + 1) * 128, :], reads=["X"], writes=[xt.name])
            ss = self.small[:, 0:1]
            rs = self.small[:, 1:2]
            junk = big[2]
            S.op("act", lambda e, xt=xt: e.activation(out=junk[:], in_=xt[:], func=AF.Square, accum_out=ss),
                 reads=[xt.name], writes=["big2", "small"])
            self.rstd_col(ss, rs, D)
            S.op("dve", lambda e, xt=xt: e.scalar_tensor_tensor(out=junk[:], in0=xt[:], scalar=rs, in1=amod[:],
                                                                op0=ALU.mult, op1=ALU.mult),
                 reads=[xt.name, "small", "amod"], writes=["big2"])
            S.op("dve", lambda e: e.tensor_tensor(out=hbf[:], in0=junk[:], in1=shiftb[:], op=ALU.add),
                 reads=["big2", "shiftb"], writes=["hbf"])
            for g in range(KT // 8):
                def tr(e, g=g):
                    for jj in range(8):
                        kt = g * 8 + jj
                        r = e.transpose(self.ptb[:, jj * 128:(jj + 1) * 128], hbf[:, kt * 128:(kt + 1) * 128],
                                        self.identb[:])
                    return r
                S.op("pe", tr, reads=["hbf", "identb"], writes=["ptb"])
                S.op("act", lambda e, g=g, tt=tt: e.copy(
                    out=self.hT[:, g * 8:(g + 1) * 8, tt * 128:(tt + 1) * 128],
                    in_=self.ptb[:, :].rearrange("p (j t) -> p j t", j=8)), reads=["ptb"], writes=["hT"])

    def load_w(self, W, col0, ncols, i):
        wb = self.wbuf[i % 2]
        src = W[:, col0:col0 + ncols].rearrange("(kt p) c -> p kt c", p=128)
        self.S.dma("pool", wb[:, :, 0:ncols], src, writes=[wb.name])
        return wb

    def proj_fm(self, wb, c0, m, tb, p):
        def mm(e):
            for kt in range(KT):
                r = e.matmul(p[0:m, :], wb[:, kt, c0:c0 + m], self.hT[:, kt, tb * 512:(tb + 1) * 512],
                             start=(kt == 0), stop=(kt == KT - 1))
            return r
        self.S.op("pe", mm, reads=[wb.name, "hT"], writes=[p.name])

    def proj_tm(self, wb, tt, p):
        def mm(e):
            for kt in range(KT):
                r = e.matmul(p[:, :], self.hT[:, kt, tt * 128:(tt + 1) * 128], wb[:, kt, :],
                             start=(kt == 0), stop=(kt == KT - 1))
            return r
        self.S.op("pe", mm, reads=[wb.name, "hT"], writes=[p.name])

    def evac(self, i, out, in_, reads, writes, scale=None):
        S = self.S
        if i % 2 == 0:
            if scale is None:
                S.op("act", lambda e: e.copy(out=out, in_=in_), reads=reads, writes=writes)
            else:
                S.op("act", lambda e: e.activation(out=out, in_=in_, func=AF.Copy, scale=scale), reads=reads,
                     writes=writes)
        else:
            if scale is None:
                S.op("dve", lambda e: e.tensor_copy(out, in_), reads=reads, writes=writes)
            else:
                S.op("dve", lambda e: e.tensor_scalar(out=out, in0=in_, scalar1=scale, scalar2=None, op0=ALU.mult),
                     reads=reads, writes=writes)

    def sc_mixer(self, sb):
        S, ins = self.S, self.ins
        NTB = self.NT // 512
        W = ins["w_sc_in"]
        wconv = sb("wconv", [128, 32, 3], F32)
        t2k = [sb("t2k%d" % i, [128, 512], F32) for i in range(4)]
        S.dma("sp", wconv[:], ins["w_sc_conv"][:, :, :], writes=["wconv"])
        for ct in range(32):
            wb = self.wbuf[ct % 2]
            wk = wb.name
            for jj in range(4):
                src = W[:, jj * INNER + ct * 128: jj * INNER + (ct + 1) * 128].rearrange("(kt p) c -> p kt c", p=128)
                S.dma("pool", wb[:, :, jj * 128:(jj + 1) * 128], src, writes=[wk])
            for tb in range(NTB):
                roww = self.roww_of_block(tb)
                pl = [self.psum() for _ in range(4)]
                pu, pB, pC, pz = pl
                for jj, p in enumerate(pl):
                    self.proj_fm(wb, jj * 128, 128, tb, p)
                u_sb, cu, acc, sz = t2k
                S.op("act", lambda e, pu=pu: e.copy(out=u_sb[:], in_=pu[:, :]), reads=[pu.name], writes=["t2k0"])
                S.op("act", lambda e, pz=pz: e.activation(out=sz[:], in_=pz[:, :], func=AF.Silu),
                     reads=[pz.name], writes=["t2k3"])
                S.op("dve", lambda e, pC=pC: e.tensor_tensor(out=cu[:], in0=pC[:, :], in1=u_sb[:], op=ALU.mult),
                     reads=[pC.name, "t2k0"], writes=["t2k1"])
                w0 = wconv[:, ct, 0:1]
                w1 = wconv[:, ct, 1:2]
                w2 = wconv[:, ct, 2:3]
                S.op("dve", lambda e, w1=w1: e.tensor_scalar(out=acc[:], in0=cu[:], scalar1=w1, scalar2=None,
                                                             op0=ALU.mult), reads=["t2k1", "wconv"], writes=["t2k2"])
                cu3 = cu[:].rearrange("p (r w) -> p r w", w=roww)
                acc3 = acc[:].rearrange("p (r w) -> p r w", w=roww)
                S.op("dve", lambda e, w0=w0, cu3=cu3, acc3=acc3: e.scalar_tensor_tensor(
                    out=acc3[:, :, 1:], in0=cu3[:, :, :-1], scalar=w0, in1=acc3[:, :, 1:], op0=ALU.mult, op1=ALU.add),
                    reads=["t2k1", "t2k2", "wconv"], writes=["t2k2"])
                S.op("dve", lambda e, w2=w2, cu3=cu3, acc3=acc3: e.scalar_tensor_tensor(
                    out=acc3[:, :, :-1], in0=cu3[:, :, 1:], scalar=w2, in1=acc3[:, :, :-1], op0=ALU.mult, op1=ALU.add),
                    reads=["t2k1", "t2k2", "wconv"], writes=["t2k2"])
                S.op("dve", lambda e, pB=pB: e.tensor_tensor(out=acc[:], in0=pB[:, :], in1=acc[:], op=ALU.mult),
                     reads=[pB.name, "t2k2"], writes=["t2k2"])
                yb = self.stage()
                S.op("dve", lambda e, yb=yb: e.tensor_tensor(out=yb[:], in0=acc[:], in1=sz[:], op=ALU.mult),
                     reads=["t2k2", "t2k3"], writes=[yb.name])
                S.dma("sp", self.YT[ct * 128:(ct + 1) * 128, tb * 512:(tb + 1) * 512], yb[:],
                      reads=[yb.name], writes=["YT"])

    def roww_of_block(self, tb):
        tok = tb * 512
        for s in self.seqs:
            if s["t0"] <= tok < s["t0"] + s["T"]:
                assert 512 % s["roww"] == 0 and (tok - s["t0"]) % s["roww"] == 0
                return s["roww"]
        raise AssertionError

    def ml_proj(self, j):
        S, ins = self.S, self.ins
        W = ins["w_ml_in"][j]
        NTB = self.NT // 512
        NTT = self.NT // 128
        wi = 0
        ei = 0
        for g in range(8):
            wb = self.load_w(W, g * 512, 512, wi)
            wi += 1
            for ci in range(4):
                col = g * 512 + ci * 128
                dst = self.QT if col < 2048 else self.KTd
                r0 = col % 2048
                for tb in range(NTB):
                    p = self.psum()
                    self.proj_fm(wb, ci * 128, 128, tb, p)
                    sg = self.stage()
                    self.evac(ei, sg[:], p[:, :], [p.name], [sg.name], scale=(ML_DK ** -0.5) if col < 2048 else None)
                    ei += 1
                    S.dma("sp", dst[r0:r0 + 128, tb * 512:(tb + 1) * 512], sg[:], reads=[sg.name],
                          writes=[dst.name])
        for g in range(24):
            wb = self.load_w(W, 4096 + g * 512, 512, wi)
            wi += 1
            which = g // 8
            c0 = (g % 8) * 512
            for tt in range(NTT):
                p = self.psum()
                self.proj_tm(wb, tt, p)
                sg = self.stage()
                self.evac(ei, sg[:], p[:, :], [p.name], [sg.name])
                ei += 1
                S.dma("sp", self.VOZ[which, tt * 128:(tt + 1) * 128, c0:c0 + 512], sg[:], reads=[sg.name],
                      writes=["VOZ"])
        wb = self.load_w(W, 16384, 32, wi)
        S.dma("sp", self.small[0:8, 8:12], ins["b_mlg"][j], writes=["small_bg"])
        for typ in range(4):
            for tb in range(NTB):
                p = self.psum()
                self.proj_fm(wb, typ * 8, 8, tb, p)
                sg = self.stgf[(typ * NTB + tb) % 2]
                S.op("dve", lambda e, p=p, sg=sg, typ=typ: e.tensor_scalar(
                    out=sg[0:8, :], in0=p[0:8, :], scalar1=self.small[0:8, 8 + typ:9 + typ], scalar2=None,
                    op0=ALU.add), reads=[p.name, "small_bg"], writes=[sg.name])
                S.dma("sp", self.GATES[typ, :, tb * 512:(tb + 1) * 512], sg[0:8, :], reads=[sg.name],
                      writes=["GATES"])

    def ml_scan(self, j, sb):
        S, ins = self.S, self.ins
        A = {}
        A["gt"] = sb("gt", [128, 4, 128], F32)
        A["ee"] = sb("ee", [128, 2, 128], F32)
        A["nb"] = sb("nb", [128, 2, 128], F32)
        A["uu"] = sb("uu", [128, 2, 128], F32)
        A["ww"] = sb("ww", [128, 2, 128], F32)
        A["fl"] = sb("fl", [128, 2, 128], F32)
        A["cols"] = sb("cols", [128, 16], F32)
        A["rows"] = sb("rows", [1, 12, 128], F32)
        A["mh"] = sb("mh", [1, 2, 17, 8], F32)
        A["toksc"] = sb("toksc", [128, 4, 128], F32)
        A["decB"] = sb("decB", [128, 2, 128], F32)
        A["diag"] = sb("diag", [128, 128], F32)
        NS = 4
        AS = []
        for k in range(NS):
            B = dict(A)
            g = lambda n, shp, dt=F32, B=B, k=k: B.__setitem__(n, sb("%s_s%d" % (n, k), shp, dt))
            B["HSb"] = [sb("HSb%d_s%d" % (i, k), [128, 512], F32) for i in range(2)]
            B["sid"] = k
            B["Cst"] = [sb("Cst%d_s%d" % (d, k), [128, 2, 513], F32) for d in range(2)]
            g("Cb", [128, 2, 513], BF16)
            for n in ("qT", "kT"):
                B[n] = [sb("%s%d_s%d" % (n, i, k), [128, 2, 128], BF16) for i in range(2)]
            g("ktm", [128, 256], BF16)
            for n in ("v", "o", "z", "ytr"):
                B[n] = [sb("%s%d_s%d" % (n, i, k), [128, 512], BF16) for i in range(2)]
            g("vx", [128, 520], BF16)
            g("sm", [128, 128], BF16)
            g("hs", [128, 512])
            g("sig", [128, 512])
            g("sz", [128, 512])
            g("ybf", [128, 512], BF16)
            g("sc8", [128, 8])
            g("ghead", [128, 512])
            B["uc"] = 0
            AS.append(B)
        for s in self.seqs:
            self.ml_gates(j, s, A)

            def stream(k, s=s):
                for h in range(k, ML_H, NS):
                    S.dma("sp", AS[k]["ghead"][:], ins["g_ml_head"][j:j + 1, h * 512:(h + 1) * 512].partition_broadcast(128),
                          writes=[AS[k]["ghead"].name])
                    for d in (1, 0):
                        yield from self.ml_head_dir(j, s, h, d, AS[k])
            run_streams([stream(k) for k in range(NS)], S)

    def ml_gates(self, j, s, A):
        S, ins = self.S, self.ins
        T, t0 = s["T"], s["t0"]
        nch = T // 128
        R = nch * 8
        gt, ee, nb, uu, ww, fl, cols, rows, mh = (A[k] for k in ("gt", "ee", "nb", "uu", "ww", "fl", "cols", "rows", "mh"))
        for c in range(nch):
            for typ in range(4):
                S.dma("sp", gt[c * 8:(c + 1) * 8, typ, :], self.GATES[typ, :, t0 + c * 128: t0 + (c + 1) * 128],
                      reads=["GATES"], writes=["gt"])
        S.op("act", lambda e: e.activation(out=ee[0:R], in_=gt[0:R, 2:4, :], func=AF.Exp, scale=-1.0),
             reads=["gt"], writes=["ee"])
        S.op("dve", lambda e: e.tensor_scalar(out=ee[0:R], in0=ee[0:R], scalar1=1.0, scalar2=None, op0=ALU.add),
             reads=["ee"], writes=["ee"])
        S.op("act", lambda e: e.activation(out=ee[0:R], in_=ee[0:R], func=AF.Ln), reads=["ee"], writes=["ee"])
        S.op("dve", lambda e: e.tensor_tensor_scan(out=nb[0:R, 0, :], data0=self.ones[0:R, :], data1=ee[0:R, 0, :],
                                                   initial=0.0, op0=ALU.mult, op1=ALU.add),
             reads=["ee", "ones_sb"], writes=["nb"])
        S.op("dve", lambda e: e.tensor_tensor_scan(out=nb[0:R, 1, :], data0=self.ones[0:R, :], data1=ee[0:R, 1, :],
                                                   initial=0.0, op0=ALU.mult, op1=ALU.add),
             reads=["ee", "ones_sb"], writes=["nb"])
        S.op("dve", lambda e: e.tensor_copy(cols[0:R, 8:9], nb[0:R, 1, 127:128]), reads=["nb"], writes=["cols"])
        S.op("dve", lambda e: e.tensor_tensor(out=nb[0:R, 1, :], in0=ee[0:R, 1, :], in1=nb[0:R, 1, :],
                                              op=ALU.subtract), reads=["ee", "nb"], writes=["nb"])
        S.op("dve", lambda e: e.tensor_scalar(out=nb[0:R, 1, :], in0=nb[0:R, 1, :], scalar1=cols[0:R, 8:9],
                                              scalar2=None, op0=ALU.add), reads=["nb", "cols"], writes=["nb"])
        S.op("dve", lambda e: e.tensor_tensor(out=uu[0:R], in0=gt[0:R, 0:2, :], in1=nb[0:R], op=ALU.add),
             reads=["gt", "nb"], writes=["uu"])
        S.op("dve", lambda e: e.tensor_reduce(out=cols[0:R, 0:2], in_=uu[0:R], axis=AX.X, op=ALU.max),
             reads=["uu"], writes=["cols"])
        S.op("dve", lambda e: e.tensor_copy(cols[0:R, 2:3], nb[0:R, 0, 127:128]), reads=["nb"], writes=["cols"])
        S.op("dve", lambda e: e.tensor_copy(cols[0:R, 3:4], nb[0:R, 1, 0:1]), reads=["nb"], writes=["cols"])
        pm = self.pmisc
        for q in range(4):
            S.op("pe", lambda e, q=q: e.transpose(pm[0:1, q * 128:q * 128 + R], cols[0:R, q:q + 1],
                                                  self.ident[0:R, 0:R]), reads=["cols", "ident_sb"],
                 writes=["pmisc"])
        S.op("dve", lambda e: e.tensor_copy(rows[0:1, 0:4, 0:R], pm[0:1, :].rearrange("p (q r) -> p q r", q=4)[:, :, 0:R]),
             reads=["pmisc"], writes=["rows"])
        for d in range(2):
            if s["kind"] == "s":
                S.dma("sp", mh[0:1, d, 0, :], ins["mlm0"][j:j + 1, d, :], writes=["mh"])
            else:
                S.op("dve", lambda e, d=d: e.memset(mh[0:1, d, 0, :], 0.0), writes=["mh"])
            order = list(range(nch)) if d == 0 else list(range(nch - 1, -1, -1))
            for idx, c in enumerate(order):
                cs = slice(c * 8, (c + 1) * 8)
                S.op("dve", lambda e, d=d, idx=idx, cs=cs: e.tensor_tensor(
                    out=rows[0:1, 4 + d, cs], in0=mh[0:1, d, idx, :], in1=rows[0:1, d, cs], op=ALU.max),
                    reads=["mh", "rows"], writes=["rows"])
                S.op("dve", lambda e, d=d, idx=idx, cs=cs: e.tensor_tensor(
                    out=rows[0:1, 6 + d, cs], in0=mh[0:1, d, idx, :], in1=rows[0:1, 4 + d, cs], op=ALU.subtract),
                    reads=["mh", "rows"], writes=["rows"])
                S.op("dve", lambda e, d=d, idx=idx, cs=cs: e.tensor_tensor(
                    out=mh[0:1, d, idx + 1, :], in0=rows[0:1, 4 + d, cs], in1=rows[0:1, 2 + d, cs], op=ALU.subtract),
                    reads=["mh", "rows"], writes=["mh"])
            if s["kind"] == "p":
                S.dma("sp", self.st_m[s["pi"]:s["pi"] + 1, j, d, :], mh[0:1, d, nch, :], reads=["mh"],
                      writes=["st_m"])
        S.op("act", lambda e: e.activation(out=rows[0:1, 6:8, 0:R], in_=rows[0:1, 6:8, 0:R], func=AF.Exp),
             reads=["rows"], writes=["rows"])
        S.op("dve", lambda e: e.tensor_scalar(out=rows[0:1, 4:6, 0:R], in0=rows[0:1, 4:6, 0:R], scalar1=-1.0,
                                              scalar2=None, op0=ALU.mult), reads=["rows"], writes=["rows"])
        for q in range(4):
            S.op("pe", lambda e, q=q: e.transpose(pm[0:R, 256 + q:257 + q], rows[0:1, 4 + q, 0:R],
                                                  self.ident[0:1, 0:1]), reads=["rows", "ident_sb"],
                 writes=["pmisc"])
        S.op("dve", lambda e: e.tensor_copy(cols[0:R, 4:8], pm[0:R, 256:260]), reads=["pmisc"], writes=["cols"])
        for d in range(2):
            S.op("act", lambda e, d=d: e.activation(out=ww[0:R, d, :], in_=uu[0:R, d, :], func=AF.Exp,
                                                    bias=cols[0:R, 4 + d:5 + d], scale=1.0),
                 reads=["uu", "cols"], writes=["ww"])
            S.op("act", lambda e, d=d: e.activation(out=fl[0:R, d, :], in_=nb[0:R, d, :], func=AF.Exp,
                                                    bias=cols[0:R, 4 + d:5 + d], scale=1.0),
                 reads=["nb", "cols"], writes=["fl"])
        for q in range(4):
            src = ww if q < 2 else fl
            S.op("pe", lambda e, q=q, src=src: e.transpose(pm[:, q * 128:q * 128 + R], src[0:R, q % 2, :],
                                                           self.ident[0:R, 0:R]),
                 reads=[src.name, "ident_sb"], writes=["pmisc"])
        S.op("act", lambda e: e.copy(out=A["toksc"][:, :, 0:R],
                                     in_=pm[:, :].rearrange("p (q r) -> p q r", q=4)[:, :, 0:R]),
             reads=["pmisc"], writes=["toksc"])
        for d in range(2):
            S.op("dve", lambda e, d=d: e.tensor_scalar(out=A["diag"][0:R, 0:R], in0=self.ident[0:R, 0:R],
                                                       scalar1=cols[0:R, 6 + d:7 + d], scalar2=None, op0=ALU.mult),
                 reads=["ident_sb", "cols"], writes=["diag"])
            p = self.psum()
            S.op("pe", lambda e, p=p: e.matmul(p[:, 0:R], self.ones[0:R, :], A["diag"][0:R, 0:R], start=True,
                                               stop=True), reads=["ones_sb", "diag"], writes=[p.name])
            S.op("act", lambda e, p=p, d=d: e.copy(out=A["decB"][:, d, 0:R], in_=p[:, 0:R]), reads=[p.name],
                 writes=["decB"])

    def ml_head_dir(self, j, s, h, d, A):
        S, ins = self.S, self.ins
        T, t0 = s["T"], s["t0"]
        nch = T // 128
        Cst = A["Cst"][d]
        Cb = A["Cb"]
        if s["kind"] == "s":
            S.dma("sp", Cst[:, :, 0:512], ins["mlC0"][j, d, h].rearrange("(dt p) v -> p dt v", p=128),
                  writes=[Cst.name])
            S.dma("sp", Cst[:, :, 512], ins["mln0"][j, d, h].rearrange("(dt p) -> p dt", p=128),
                  writes=[Cst.name], slow=True)
        else:
            S.op("dve", lambda e: e.memset(Cst[:], 0.0), writes=[Cst.name])
        order = list(range(nch)) if d == 0 else list(range(nch - 1, -1, -1))
        mask = self.tril if d == 0 else self.triu
        for c in order:
            u = A["uc"]
            A["uc"] += 1
            tok0 = t0 + c * 128
            col = c * 8 + h
            qT, kT, v = A["qT"][u % 2], A["kT"][u % 2], A["v"][u % 2]
            ktm, vx, sm = A["ktm"], A["vx"], A["sm"]
            toksc, decB = A["toksc"], A["decB"]
            S.dma("sp", qT[:], self.QT[h * 256:(h + 1) * 256, tok0:tok0 + 128].rearrange("(dt p) t -> p dt t", p=128),
                  reads=["QT"], writes=[qT.name])
            S.dma("sp", kT[:], self.KTd[h * 256:(h + 1) * 256, tok0:tok0 + 128].rearrange("(dt p) t -> p dt t", p=128),
                  reads=["KTd"], writes=[kT.name])
            S.dma("sp", v[:], self.VOZ[0, tok0:tok0 + 128, h * 512:(h + 1) * 512], reads=["VOZ"], writes=[v.name])
            yield
            def trk(e, kT=kT):
                for dt_ in range(2):
                    r = e.transpose(self.ptb[:, dt_ * 128:(dt_ + 1) * 128], kT[:, dt_, :], self.identb[:])
                return r
            S.op("pe", trk, reads=[kT.name, "identb"], writes=["ptb"])
            S.op("act", lambda e: e.copy(out=ktm[:], in_=self.ptb[:, 0:256]), reads=["ptb"], writes=[A["ktm"].name])
            wcol = toksc[:, d, col:col + 1]
            S.op("dve", lambda e, v=v, wcol=wcol: e.tensor_scalar(out=vx[:, 0:512], in0=v[:], scalar1=wcol,
                                                                  scalar2=None, op0=ALU.mult),
                 reads=[v.name, "toksc"], writes=[A["vx"].name])
            S.op("dve", lambda e, wcol=wcol: e.tensor_copy(vx[:, 512:513], wcol), reads=["toksc"], writes=[A["vx"].name])
            yield
            S.op("dve", lambda e, col=col: e.tensor_scalar(out=Cst[:], in0=Cst[:], scalar1=decB[:, d, col:col + 1],
                                                           scalar2=None, op0=ALU.mult),
                 reads=[Cst.name, "decB"], writes=[Cst.name])
            S.op("act", lambda e: e.copy(out=Cb[:], in_=Cst[:]), reads=[Cst.name], writes=[A["Cb"].name])
            yield
            pS = self.psum()

            def mmS(e, pS=pS, kT=kT, qT=qT):
                for dt_ in range(2):
                    r = e.matmul(pS[:, 0:128], kT[:, dt_, :], qT[:, dt_, :], start=(dt_ == 0), stop=(dt_ == 1))
                return r
            S.op("pe", mmS, reads=[kT.name, qT.name], writes=[pS.name])
            S.op("dve", lambda e, pS=pS: e.tensor_tensor(out=sm[:], in0=pS[:, 0:128], in1=mask[:], op=ALU.mult),
                 reads=[pS.name, mask.name], writes=[A["sm"].name])
            yield
            pnum = self.psum()
            pden = self.psum()

            def mmN(e, pnum=pnum, pden=pden, qT=qT):
                e.matmul(pnum[:, :], sm[:], vx[:, 0:512], start=True, stop=False)
                e.matmul(pnum[:, :], qT[:, 0, :], Cb[:, 0, 0:512], start=False, stop=False)
                e.matmul(pnum[:, :], qT[:, 1, :], Cb[:, 1, 0:512], start=False, stop=True)
                e.matmul(pden[:, 0:1], sm[:], vx[:, 512:513], start=True, stop=False)
                e.matmul(pden[:, 0:1], qT[:, 0, :], Cb[:, 0, 512:513], start=False, stop=False)
                return e.matmul(pden[:, 0:1], qT[:, 1, :], Cb[:, 1, 512:513], start=False, stop=True)
            S.op("pe", mmN, reads=[A["sm"].name, A["vx"].name, qT.name, A["Cb"].name], writes=[pnum.name, pden.name])
            rr = A["sc8"][:, 4:5]
            S.op("act", lambda e, pden=pden: e.activation(out=rr, in_=pden[:, 0:1], func=AF.Abs),
                 reads=[pden.name], writes=[A["sc8"].name + "_rr"])
            S.op("dve", lambda e, col=col: e.tensor_tensor(out=rr, in0=rr, in1=toksc[:, 2 + d, col:col + 1],
                                                           op=ALU.max), reads=[A["sc8"].name + "_rr", "toksc"],
                 writes=[A["sc8"].name + "_rr"])
            S.op("dve", lambda e: e.reciprocal(out=rr, in_=rr), reads=[A["sc8"].name + "_rr"], writes=[A["sc8"].name + "_rr"])
            hsb = A["HSb"][u % 2]
            hsd = self.OSD[A["sid"], c * 128:(c + 1) * 128, :]
            hskey = "OSD%d_%d" % (A["sid"], c)
            if d == 1:
                S.op("dve", lambda e, pnum=pnum, hsb=hsb: e.tensor_scalar(out=hsb[:], in0=pnum[:, :], scalar1=rr,
                                                                          scalar2=None, op0=ALU.mult),
                     reads=[pnum.name, A["sc8"].name + "_rr"], writes=[hsb.name])
                S.defer_dma("pool", hsd, hsb[:], reads=[hsb.name], writes=[hskey])
            else:
                self.ml_finalize(j, h, c, tok0, pnum, rr, A, u)
            yield
            pD0, pD1 = self.psum(), self.psum()

            def mmD(e, pD0=pD0, pD1=pD1):
                e.matmul(pD0[:, :], ktm[:, 0:128], vx[:, 0:512], start=True, stop=True)
                e.matmul(pD1[:, :], ktm[:, 128:256], vx[:, 0:512], start=True, stop=True)
                e.matmul(self.pmisc[:, 300:301], ktm[:, 0:128], vx[:, 512:513], start=True, stop=True)
                return e.matmul(self.pmisc[:, 301:302], ktm[:, 128:256], vx[:, 512:513], start=True, stop=True)
            S.op("pe", mmD, reads=[A["ktm"].name, A["vx"].name], writes=[pD0.name, pD1.name, "pmisc"])
            S.op("dve", lambda e, pD0=pD0: e.tensor_tensor(out=Cst[:, 0, 0:512], in0=Cst[:, 0, 0:512], in1=pD0[:, :],
                                                           op=ALU.add), reads=[pD0.name, Cst.name], writes=[Cst.name])
            S.op("dve", lambda e, pD1=pD1: e.tensor_tensor(out=Cst[:, 1, 0:512], in0=Cst[:, 1, 0:512], in1=pD1[:, :],
                                                           op=ALU.add), reads=[pD1.name, Cst.name], writes=[Cst.name])
            S.op("dve", lambda e: e.tensor_tensor(out=Cst[:, :, 512], in0=Cst[:, :, 512], in1=self.pmisc[:, 300:302],
                                                  op=ALU.add), reads=["pmisc", Cst.name], writes=[Cst.name])
        if s["kind"] == "p":
            pi = s["pi"]
            S.dma("sp", self.st_C[pi, j, d, h].rearrange("(dt p) v -> p dt v", p=128), Cst[:, :, 0:512],
                  reads=[Cst.name], writes=["st_C"])
            S.dma("sp", self.st_n[pi, j, d, h].rearrange("(dt p) -> p dt", p=128), Cst[:, :, 512],
                  reads=[Cst.name], writes=["st_n"], slow=True)

    def ml_finalize(self, j, h, c, tok0, pnum, rr, A, u):
        S = self.S
        hs, sig, sz, ybf = A["hs"], A["sig"], A["sz"], A["ybf"]
        o, z = A["o"][u % 2], A["z"][u % 2]
        ytr = A["ytr"][u % 2]
        S.dma("sp", o[:], self.VOZ[1, tok0:tok0 + 128, h * 512:(h + 1) * 512], reads=["VOZ"], writes=[o.name])
        S.dma("sp", z[:], self.VOZ[2, tok0:tok0 + 128, h * 512:(h + 1) * 512], reads=["VOZ"], writes=[z.name])
        hsb = A["HSb"][u % 2]
        hskey = "OSD%d_%d" % (A["sid"], c)
        S.dma("sp", hsb[:], self.OSD[A["sid"], c * 128:(c + 1) * 128, :], reads=[hskey], writes=[hsb.name])
        S.op("dve", lambda e: e.scalar_tensor_tensor(out=hs[:], in0=pnum[:, :], scalar=rr, in1=hsb[:],
                                                     op0=ALU.mult, op1=ALU.add),
             reads=[pnum.name, A["sc8"].name + "_rr", hsb.name], writes=[A["hs"].name])
        ss = A["sc8"][:, 0:1]
        rs = A["sc8"][:, 1:2]
        skey = A["sc8"].name
        S.op("act", lambda e: e.activation(out=sig[:], in_=hs[:], func=AF.Square, accum_out=ss),
             reads=[A["hs"].name], writes=[A["sig"].name, skey])
        self.rstd_col(ss, rs, ML_DV, key=skey)
        S.op("dve", lambda e: e.scalar_tensor_tensor(out=hs[:], in0=hs[:], scalar=rs,
                                                     in1=A["ghead"][:, :], op0=ALU.mult,
                                                     op1=ALU.mult), reads=[A["hs"].name, skey, A["ghead"].name],
             writes=[A["hs"].name])
        S.op("act", lambda e: e.activation(out=sig[:], in_=o[:], func=AF.Sigmoid), reads=[o.name], writes=[A["sig"].name])
        S.op("act", lambda e: e.activation(out=sz[:], in_=z[:], func=AF.Silu), reads=[z.name], writes=[A["sz"].name])
        S.op("dve", lambda e: e.tensor_tensor(out=hs[:], in0=hs[:], in1=sig[:], op=ALU.mult),
             reads=[A["hs"].name, A["sig"].name], writes=[A["hs"].name])
        S.op("dve", lambda e: e.tensor_tensor(out=ybf[:], in0=hs[:], in1=sz[:], op=ALU.mult),
             reads=[A["hs"].name, A["sz"].name], writes=[A["ybf"].name])

        def tr(e):
            for i in range(4):
                r = e.transpose(self.ptb[:, 512 + i * 128:512 + (i + 1) * 128], ybf[:, i * 128:(i + 1) * 128],
                                self.identb[:])
            return r
        S.op("pe", tr, reads=[A["ybf"].name, "identb"], writes=["ptb"])
        S.op("act", lambda e: e.copy(out=ytr[:], in_=self.ptb[:, 512:1024]), reads=["ptb"], writes=[ytr.name])
        S.defer_dma("pool", self.YT[h * 512:(h + 1) * 512, tok0:tok0 + 128].rearrange("(i p) t -> p i t", p=128),
                    ytr[:].rearrange("p (i t) -> p i t", i=4), reads=[ytr.name], writes=["YT"])

    def gd_proj(self, sb):
        S, ins = self.S, self.ins
        W = ins["w_gd_in"]
        NTB = self.NT // 512
        NTT = self.NT // 128
        wconv = sb("gwconv", [128, 64, 3], F32)
        accs = [sb("gacc%d" % i, [128, 512], F32) for i in range(2)]
        sacts = [sb("gsact%d" % i, [128, 512], F32) for i in range(2)]
        sqs = [sb("gsq%d" % i, [128, 512], F32) for i in range(2)]
        rns = [sb("grn%d" % i, [128, 512], F32) for i in range(2)]
        sbfs = [sb("gsbf%d" % i, [128, 512], BF16) for i in range(2)]
        gi = 0
        S.dma("sp", wconv[:], ins["w_gd_conv"][:, :, :], writes=["gwconv"])
        wi = 0
        ei = 0
        for g in range(16):
            wb = self.load_w(W, g * 512, 512, wi)
            wi += 1
            for ci in range(4):
                ct = g * 4 + ci
                for tb in range(NTB):
                    roww = self.roww_of_block(tb)
                    acc, sact, sq, rn, sbf = accs[gi % 2], sacts[gi % 2], sqs[gi % 2], rns[gi % 2], sbfs[gi % 2]
                    gi += 1
                    p = self.psum()
                    self.proj_fm(wb, ci * 128, 128, tb, p)
                    w0, w1, w2 = (wconv[:, ct, i:i + 1] for i in range(3))
                    p3 = p[:, :].rearrange("p (r w) -> p r w", w=roww)
                    acc3 = acc[:].rearrange("p (r w) -> p r w", w=roww)
                    S.op("dve", lambda e, p=p, w1=w1, acc=acc: e.tensor_scalar(out=acc[:], in0=p[:, :], scalar1=w1, scalar2=None,
                                                                     op0=ALU.mult), reads=[p.name, "gwconv"],
                         writes=[acc.name])
                    S.op("dve", lambda e, p3=p3, acc3=acc3, w0=w0: e.scalar_tensor_tensor(
                        out=acc3[:, :, 1:], in0=p3[:, :, :-1], scalar=w0, in1=acc3[:, :, 1:], op0=ALU.mult,
                        op1=ALU.add), reads=[p.name, acc.name, "gwconv"], writes=[acc.name])
                    S.op("dve", lambda e, p3=p3, acc3=acc3, w2=w2: e.scalar_tensor_tensor(
                        out=acc3[:, :, :-1], in0=p3[:, :, 1:], scalar=w2, in1=acc3[:, :, :-1], op0=ALU.mult,
                        op1=ALU.add), reads=[p.name, acc.name, "gwconv"], writes=[acc.name])
                    if ct < 32:
                        S.op("act", lambda e, sact=sact, acc=acc: e.activation(out=sact[:], in_=acc[:], func=AF.Silu), reads=[acc.name],
                             writes=[sact.name])
                        S.op("act", lambda e, sq=sq, sact=sact: e.activation(out=sq[:], in_=sact[:], func=AF.Square), reads=[sact.name],
                             writes=[sq.name])
                        p2 = self.psum()
                        S.op("pe", lambda e, p2=p2, sq=sq: e.matmul(p2[:, :], self.ones[:, :], sq[:], start=True, stop=True),
                             reads=["ones_sb", sq.name], writes=[p2.name])
                        S.op("dve", lambda e, p2=p2, rn=rn: e.tensor_scalar(out=rn[:], in0=p2[:, :], scalar1=EPS, scalar2=None,
                                                                     op0=ALU.add), reads=[p2.name], writes=[rn.name])
                        S.op("act", lambda e, rn=rn: e.activation(out=rn[:], in_=rn[:], func=AF.Ln), reads=[rn.name],
                             writes=[rn.name])
                        S.op("act", lambda e, rn=rn: e.activation(out=rn[:], in_=rn[:], func=AF.Exp, scale=-0.5),
                             reads=[rn.name], writes=[rn.name])
                        sg = self.stage()
                        scl = (128 ** -0.5) if ct < 16 else 1.0
                        S.op("dve", lambda e, sg=sg, scl=scl, sact=sact, rn=rn: e.scalar_tensor_tensor(
                            out=sg[:], in0=sact[:], scalar=scl, in1=rn[:], op0=ALU.mult, op1=ALU.mult),
                            reads=[sact.name, rn.name], writes=[sg.name])
                        dst = self.GQT if ct < 16 else self.GKT
                        r0 = (ct % 16) * 128
                        S.dma("sp", dst[r0:r0 + 128, tb * 512:(tb + 1) * 512], sg[:], reads=[sg.name],
                              writes=[dst.name])
                    else:
                        hv = ct - 32
                        S.op("act", lambda e, sbf=sbf, acc=acc: e.activation(out=sbf[:], in_=acc[:], func=AF.Silu), reads=[acc.name],
                             writes=[sbf.name])

                        def tr(e, sbf=sbf):
                            for i in range(4):
                                r = e.transpose(self.ptb[:, i * 128:(i + 1) * 128], sbf[:, i * 128:(i + 1) * 128],
                                                self.identb[:])
                            return r
                        S.op("pe", tr, reads=[sbf.name, "identb"], writes=["ptb"])
                        sg = self.stage()
                        self.evac(ei, sg[:], self.ptb[:, 0:512], ["ptb"], [sg.name])
                        ei += 1
                        S.dma("sp", self.GV[tb * 512:(tb + 1) * 512, hv * 128:(hv + 1) * 128].rearrange(
                            "(a p) v -> p a v", p=128), sg[:].rearrange("p (a v) -> p a v", a=4), reads=[sg.name],
                            writes=["GV"])
        for g in range(8):
            wb = self.load_w(W, 8192 + g * 512, 512, wi)
            wi += 1
            for tt in range(NTT):
                p = self.psum()
                self.proj_tm(wb, tt, p)
                sg = self.stage()
                self.evac(ei, sg[:], p[:, :], [p.name], [sg.name])
                ei += 1
                S.dma("sp", self.GZ[tt * 128:(tt + 1) * 128, g * 512:(g + 1) * 512], sg[:], reads=[sg.name],
                      writes=["GZ"])
        wb = self.load_w(W, 12288, 128, wi)
        for typ in range(4):
            for tb in range(NTB):
                p = self.psum()
                self.proj_fm(wb, typ * 32, 32, tb, p)
                sg = self.stgf[(typ * NTB + tb) % 2]
                S.op("act", lambda e, p=p, sg=sg: e.copy(out=sg[0:32, :], in_=p[0:32, :]), reads=[p.name],
                     writes=[sg.name])
                S.dma("sp", self.GAB[typ, :, tb * 512:(tb + 1) * 512], sg[0:32, :], reads=[sg.name],
                      writes=["GAB"])

    def gd_scan(self, sb):
        S, ins = self.S, self.ins
        A = {}
        nrt = max(1, self.Tmax // 512)
        f = lambda n, shp, dt=F32: A.__setitem__(n, sb("g_" + n, shp, dt))
        f("ga", [128, 4, 128])
        f("sp", [128, 2, 128])
        f("ng", [128, 2, 128])
        f("nG", [128, 2, 128])
        f("bt", [128, 2, 128])
        f("kd", [128, 2, 128])
        f("gc", [128, 16])
        f("par", [128, 4])
        f("toksc", [128, nrt * 2 * 5, 128])
        f("decB", [128, nrt * 2, 128])
        f("diag", [128, 128])
        f("gnB", [128, 128])
        NS = 3
        AS = []
        for k in range(NS):
            B = dict(A)
            g = lambda n, shp, dt=F32, B=B, k=k: B.__setitem__(n, sb("g_%s_s%d" % (n, k), shp, dt))
            B["OSb"] = [sb("g_OSb%d_s%d" % (i, k), [128, 512], F32) for i in range(2)]
            B["sid"] = k
            B["pp"] = self.pp[2 * k:2 * k + 2]
            B["ppi"] = 0
            g("Sst", [128, 4, 128])
            g("Sb", [128, 4, 128], BF16)
            g("diagG", [128, 4, 128])
            g("dd", [128, 4, 128])
            g("dec", [128, 4, 128])
            g("t1", [128, 4, 128])
            g("ktm", [128, 2, 128], BF16)
            for n in ("P", "PT", "N", "kb", "vb", "WTn", "U", "kdk", "y"):
                g(n, [128, 4, 128], BF16)
            for n in ("Q", "Tm", "TTm", "kT", "qT", "v", "z", "ytr", "Lb", "LTb", "Yb", "Y2b"):
                B[n] = [sb("g_%s%d_s%d" % (n, i, k), [128, 2, 128] if n in ("kT", "qT") else [128, 4, 128], BF16)
                        for i in range(2)]
            g("o", [128, 4, 128])
            g("sq", [128, 4, 128])
            g("ssq", [128, 8])
            B["uc"] = 0
            AS.append(B)
        S.dma("sp", A["gnB"][:], ins["g_gd_norm"][0:1, :].partition_broadcast(128), writes=["g_gnB"])
        S.dma("sp", A["par"][:], ins["gd_par"][:, :], writes=["g_par"])
        S.op("act", lambda e: e.activation(out=A["par"][:, 0:2], in_=A["par"][:, 0:2], func=AF.Exp),
             reads=["g_par"], writes=["g_par"])
        for s in self.seqs:
            self.gd_gates(s, A)

            def stream(k, s=s):
                for hg in range(k, 8, NS):
                    for d in (1, 0):
                        yield from self.gd_group_dir(s, hg, d, AS[k])
            run_streams([stream(k) for k in range(NS)], S)

    def gd_gates(self, s, A):
        S, ins = self.S, self.ins
        T, t0 = s["T"], s["t0"]
        nch = T // 128
        ga, sp, ng, nG, bt, kd, gc, par, toksc, decB = (A[k] for k in (
            "ga", "sp", "ng", "nG", "bt", "kd", "gc", "par", "toksc", "decB"))
        pm = self.pmisc
        for rt in range((nch + 3) // 4):
            ncl = min(4, nch - rt * 4)
            R = ncl * 32
            for cl in range(ncl):
                c = rt * 4 + cl
                for typ in range(4):
                    S.dma("sp", ga[cl * 32:(cl + 1) * 32, typ, :], self.GAB[typ, :, t0 + c * 128:t0 + (c + 1) * 128],
                          reads=["GAB"], writes=["g_ga"])
            for d in range(2):
                S.op("act", lambda e, d=d: e.activation(out=sp[0:R, d, :], in_=ga[0:R, d, :], func=AF.Exp,
                                                        bias=par[0:R, 2 + d:3 + d], scale=1.0),
                     reads=["g_ga", "g_par"], writes=["g_sp"])
            S.op("dve", lambda e: e.tensor_scalar(out=sp[0:R], in0=sp[0:R], scalar1=1.0, scalar2=None, op0=ALU.add),
                 reads=["g_sp"], writes=["g_sp"])
            S.op("act", lambda e: e.activation(out=sp[0:R], in_=sp[0:R], func=AF.Ln), reads=["g_sp"],
                 writes=["g_sp"])
            for d in range(2):
                S.op("dve", lambda e, d=d: e.tensor_scalar(out=ng[0:R, d, :], in0=sp[0:R, d, :],
                                                           scalar1=par[0:R, d:d + 1], scalar2=None, op0=ALU.mult),
                     reads=["g_sp", "g_par"], writes=["g_ng"])
            S.op("act", lambda e: e.activation(out=bt[0:R], in_=ga[0:R, 2:4, :], func=AF.Sigmoid),
                 reads=["g_ga"], writes=["g_bt"])
            for d in range(2):
                S.op("dve", lambda e, d=d: e.tensor_tensor_scan(out=nG[0:R, d, :], data0=self.ones[0:R, :],
                                                                data1=ng[0:R, d, :], initial=0.0, op0=ALU.mult,
                                                                op1=ALU.add),
                     reads=["g_ng", "ones_sb"], writes=["g_nG"])
            S.op("dve", lambda e: e.tensor_copy(gc[0:R, 8:9], nG[0:R, 1, 127:128]), reads=["g_nG"], writes=["g_gc"])
            S.op("dve", lambda e: e.tensor_tensor(out=nG[0:R, 1, :], in0=ng[0:R, 1, :], in1=nG[0:R, 1, :],
                                                  op=ALU.subtract), reads=["g_ng", "g_nG"], writes=["g_nG"])
            S.op("dve", lambda e: e.tensor_scalar(out=nG[0:R, 1, :], in0=nG[0:R, 1, :], scalar1=gc[0:R, 8:9],
                                                  scalar2=None, op0=ALU.add), reads=["g_nG", "g_gc"],
                 writes=["g_nG"])
            S.op("dve", lambda e: e.tensor_copy(gc[0:R, 0:1], nG[0:R, 0, 127:128]), reads=["g_nG"], writes=["g_gc"])
            S.op("dve", lambda e: e.tensor_copy(gc[0:R, 1:2], nG[0:R, 1, 0:1]), reads=["g_nG"], writes=["g_gc"])
            S.op("act", lambda e: e.activation(out=gc[0:R, 4:6], in_=gc[0:R, 0:2], func=AF.Exp, scale=-1.0),
                 reads=["g_gc"], writes=["g_gc"])
            S.op("dve", lambda e: e.tensor_scalar(out=gc[0:R, 2:4], in0=gc[0:R, 0:2], scalar1=-1.0, scalar2=None,
                                                  op0=ALU.mult), reads=["g_gc"], writes=["g_gc"])
            for d in range(2):
                S.op("act", lambda e, d=d: e.activation(out=kd[0:R, d, :], in_=nG[0:R, d, :], func=AF.Exp,
                                                        bias=gc[0:R, 2 + d:3 + d], scale=1.0),
                     reads=["g_nG", "g_gc"], writes=["g_kd"])
            for d in range(2):
                base = (rt * 2 + d) * 5
                for q, src in enumerate((nG, bt, kd)):
                    S.op("pe", lambda e, q=q, src=src, d=d: e.transpose(pm[:, q * 128:q * 128 + R], src[0:R, d, :],
                                                                        self.ident[0:R, 0:R]),
                         reads=[src.name, "ident_sb"], writes=["pmisc"])
                S.op("act", lambda e, base=base: e.copy(out=toksc[:, base:base + 3, 0:R],
                                                        in_=pm[:, 0:384].rearrange("p (q r) -> p q r", q=3)[:, :, 0:R]),
                     reads=["pmisc"], writes=["g_toksc"])
                S.op("act", lambda e, base=base: e.activation(out=toksc[:, base + 3, 0:R], in_=toksc[:, base, 0:R],
                                                              func=AF.Exp, scale=-1.0), reads=["g_toksc"],
                     writes=["g_toksc"])
                S.op("dve", lambda e, base=base: e.tensor_tensor(out=toksc[:, base + 4, 0:R], in0=toksc[:, base + 1, 0:R],
                                                                 in1=toksc[:, base + 3, 0:R], op=ALU.mult),
                     reads=["g_toksc"], writes=["g_toksc"])
                S.op("dve", lambda e, d=d: e.tensor_scalar(out=A["diag"][0:R, 0:R], in0=self.ident[0:R, 0:R],
                                                           scalar1=gc[0:R, 4 + d:5 + d], scalar2=None, op0=ALU.mult),
                     reads=["ident_sb", "g_gc"], writes=["g_diag"])
                p = self.psum()
                S.op("pe", lambda e, p=p: e.matmul(p[:, 0:R], self.ones[0:R, :], A["diag"][0:R, 0:R], start=True,
                                                   stop=True), reads=["ones_sb", "g_diag"], writes=[p.name])
                S.op("act", lambda e, p=p, rt=rt, d=d: e.copy(out=decB[:, rt * 2 + d, 0:R], in_=p[:, 0:R]),
                     reads=[p.name], writes=["g_decB"])

    def gd_group_dir(self, s, hg, d, A):
        S, ins = self.S, self.ins
        T, t0 = s["T"], s["t0"]
        nch = T // 128
        Sst, Sb = A["Sst"], A["Sb"]
        toksc, decB = A["toksc"], A["decB"]
        hv0 = hg * 4
        b4 = lambda ap: ap.unsqueeze(2).to_broadcast([128, 4, 128])
        m4 = lambda ap: ap.unsqueeze(1).to_broadcast([128, 4, 128])
        if s["kind"] == "s":
            S.dma("sp", Sst[:], ins["gdS0"][d, hv0:hv0 + 4].rearrange("i k v -> k i v"), writes=[A["Sst"].name])
        else:
            S.op("dve", lambda e: e.memset(Sst[:], 0.0), writes=[A["Sst"].name])
        S.op("act", lambda e: e.copy(out=Sb[:], in_=Sst[:]), reads=[A["Sst"].name], writes=[A["Sb"].name])
        order = list(range(nch)) if d == 0 else list(range(nch - 1, -1, -1))
        mincl = self.triu if d == 0 else self.tril
        mstr = self.triuS if d == 0 else self.trilS
        for c in order:
            u = A["uc"]
            A["uc"] += 1
            tok0 = t0 + c * 128
            rt, cl = c // 4, c % 4
            base = (rt * 2 + d) * 5
            r0 = cl * 32 + hv0
            col = lambda q: toksc[:, base + q, r0:r0 + 4]
            kT, qT, v = A["kT"][u % 2], A["qT"][u % 2], A["v"][u % 2]
            S.dma("sp", kT[:], self.GKT[hg * 256:(hg + 1) * 256, tok0:tok0 + 128].rearrange("(a p) t -> p a t", p=128),
                  reads=["GKT"], writes=[kT.name])
            S.dma("sp", qT[:], self.GQT[hg * 256:(hg + 1) * 256, tok0:tok0 + 128].rearrange("(a p) t -> p a t", p=128),
                  reads=["GQT"], writes=[qT.name])
            S.dma("sp", v[:], self.GV[tok0:tok0 + 128, hv0 * 128:(hv0 + 4) * 128].rearrange("t (i v) -> t i v", i=4),
                  reads=["GV"], writes=[v.name])
            ktm = A["ktm"]

            def trk(e, kT=kT):
                for a in range(2):
                    r = e.transpose(self.ptb[:, a * 128:(a + 1) * 128], kT[:, a, :], self.identb[:])
                return r
            S.op("pe", trk, reads=[kT.name, "identb"], writes=["ptb"])
            S.op("act", lambda e: e.copy(out=ktm[:], in_=self.ptb[:, 0:256].rearrange("p (a k) -> p a k", a=2)),
                 reads=["ptb"], writes=[A["ktm"].name])
            yield
            pK = self.psum_s(A)
            pQ = pK

            def mmG(e, pK=pK, kT=kT, qT=qT):
                for a in range(2):
                    e.matmul(pK[:, a * 128:(a + 1) * 128], kT[:, a, :], kT[:, a, :], start=True, stop=True)
                for a in range(2):
                    r = e.matmul(pK[:, 256 + a * 128:256 + (a + 1) * 128], qT[:, a, :], kT[:, a, :], start=True,
                                 stop=True)
                return r
            S.op("pe", mmG, reads=[kT.name, qT.name], writes=[pK.name])
            yield
            diagG, dd, dec, t1 = A["diagG"], A["dd"], A["dec"], A["t1"]
            S.op("dve", lambda e: e.tensor_tensor(out=diagG[:], in0=m4(self.ident[:, :]), in1=b4(col(0)), op=ALU.mult),
                 reads=["ident_sb", "g_toksc"], writes=[A["diagG"].name])
            pG = self.psum_s(A)
            S.op("pe", lambda e, pG=pG: e.matmul(pG[:, :], self.ones[:, :], diagG[:].rearrange("p i s -> p (i s)"),
                                                 start=True, stop=True), reads=["ones_sb", A["diagG"].name],
                 writes=[pG.name])
            pG3 = pG[:, :].rearrange("p (i s) -> p i s", i=4)
            S.op("dve", lambda e, pG3=pG3: e.tensor_tensor(out=dd[:], in0=pG3, in1=b4(col(0)), op=ALU.subtract),
                 reads=[pG.name, "g_toksc"], writes=[A["dd"].name])
            S.op("dve", lambda e: e.tensor_scalar(out=dd[:], in0=dd[:], scalar1=0.0, scalar2=None, op0=ALU.min),
                 reads=[A["dd"].name], writes=[A["dd"].name])
            S.op("act", lambda e: e.activation(out=dec[:], in_=dd[:], func=AF.Exp), reads=[A["dd"].name],
                 writes=[A["dec"].name])
            yield
            P, PT, N = A["P"], A["PT"], A["N"]
            pQ4 = pQ[:, 256:512].rearrange("p (a s) -> p a s", a=2).unsqueeze(2).to_broadcast([128, 2, 2, 128])
            pK4 = pK[:, 0:256].rearrange("p (a s) -> p a s", a=2).unsqueeze(2).to_broadcast([128, 2, 2, 128])
            v4 = lambda t: t[:].rearrange("p (a b) s -> p a b s", a=2)
            S.op("dve", lambda e, pQ4=pQ4: e.tensor_tensor(out=v4(t1), in0=pQ4, in1=v4(dec), op=ALU.mult),
                 reads=[pQ.name, A["dec"].name], writes=[A["t1"].name])
            S.op("pool", lambda e: e.tensor_tensor(out=P[:], in0=t1[:], in1=m4(mincl[:, :]), op=ALU.mult),
                 reads=[A["t1"].name, mincl.name], writes=[A["P"].name])
            S.op("dve", lambda e, pK4=pK4: e.tensor_tensor(out=v4(t1), in0=pK4, in1=v4(dec), op=ALU.mult),
                 reads=[pK.name, A["dec"].name], writes=[A["t1"].name])
            S.op("dve", lambda e: e.tensor_tensor(out=t1[:], in0=t1[:], in1=m4(mstr[:, :]), op=ALU.mult),
                 reads=[A["t1"].name, mstr.name], writes=[A["t1"].name])
            S.op("dve", lambda e: e.scalar_tensor_tensor(out=N[:], in0=t1[:], scalar=-1.0, in1=b4(col(1)),
                                                         op0=ALU.mult, op1=ALU.mult),
                 reads=[A["t1"].name, "g_toksc"], writes=[A["N"].name])
            yield
            Qs = A["Q"]

            def trN(e):
                for i in range(4):
                    e.transpose(self.ptb[:, i * 128:(i + 1) * 128], N[:, i, :], self.identb[:])
                for i in range(4):
                    r = e.transpose(self.ptb[:, 512 + i * 128:512 + (i + 1) * 128], P[:, i, :], self.identb[:])
                return r
            S.op("pe", trN, reads=[A["N"].name, A["P"].name, "identb"], writes=["ptb"])
            ptN = self.ptb[:, 0:512].rearrange("p (i s) -> p i s", i=4)
            ptP = self.ptb[:, 512:1024].rearrange("p (i s) -> p i s", i=4)
            Q = Qs[u % 2]
            S.op("act", lambda e: e.copy(out=Q[:], in_=ptN), reads=["ptb"], writes=[Q.name])
            S.op("act", lambda e: e.copy(out=PT[:], in_=ptP), reads=["ptb"], writes=[A["PT"].name])
            Tm, TTm = A["Tm"], A["TTm"]
            T, TT = Tm[0], TTm[0]
            for lev in range(7):
                mL = self.lvl[:, lev, :] if d == 0 else self.lvlT[:, lev, :]
                mLT = self.lvlT[:, lev, :] if d == 0 else self.lvl[:, lev, :]
                Tn, TTn = Tm[(lev + 1) % 2], TTm[(lev + 1) % 2]
                Lb, LTb, Yb, Y2b = A["Lb"][lev % 2], A["LTb"][lev % 2], A["Yb"][lev % 2], A["Y2b"][lev % 2]
                last = lev == 6
                if not last:
                    S.op("pool", lambda e, mL=mL: e.tensor_tensor(out=Lb[:], in0=N[:], in1=m4(mL), op=ALU.mult),
                         reads=[A["N"].name, "lvl"], writes=[Lb.name])
                S.op("pool", lambda e, mLT=mLT: e.tensor_tensor(out=LTb[:], in0=Q[:], in1=m4(mLT), op=ALU.mult),
                     reads=[Q.name, "lvl"], writes=[LTb.name])
                if lev == 0:
                    S.op("dve", lambda e, Tn=Tn: e.tensor_tensor(out=Tn[:], in0=Lb[:], in1=m4(self.ident[:, :]),
                                                                 op=ALU.add),
                         reads=[Lb.name, "ident_sb"], writes=[Tn.name])
                    S.op("dve", lambda e, TTn=TTn: e.tensor_tensor(out=TTn[:], in0=LTb[:], in1=m4(self.ident[:, :]),
                                                                   op=ALU.add),
                         reads=[LTb.name, "ident_sb"], writes=[TTn.name])
                    T, TT = Tn, TTn
                    continue
                yield
                py, py2 = self.psum_s(A), self.psum_s(A)

                def mmy(e, py=py, py2=py2, T=T, TT=TT, last=last):
                    for i in range(4):
                        r = e.matmul(py[:, i * 128:(i + 1) * 128], LTb[:, i, :], T[:, i, :], start=True, stop=True)
                    if not last:
                        for i in range(4):
                            r = e.matmul(py2[:, i * 128:(i + 1) * 128], Lb[:, i, :], TT[:, i, :], start=True,
                                         stop=True)
                    return r
                S.op("pe", mmy, reads=[Lb.name, LTb.name, T.name, TT.name], writes=[py.name, py2.name])
                S.op("act", lambda e, py=py: e.copy(out=Yb[:].rearrange("p i s -> p (i s)"), in_=py[:, :]),
                     reads=[py.name], writes=[Yb.name])
                if not last:
                    S.op("act", lambda e, py2=py2: e.copy(out=Y2b[:].rearrange("p i s -> p (i s)"), in_=py2[:, :]),
                         reads=[py2.name], writes=[Y2b.name])
                yield
                px, pxt = self.psum_s(A), self.psum_s(A)

                def mmx(e, px=px, pxt=pxt, T=T, TT=TT, last=last):
                    if not last:
                        for i in range(4):
                            e.matmul(px[:, i * 128:(i + 1) * 128], Y2b[:, i, :], T[:, i, :], start=True, stop=True)
                    for i in range(4):
                        r = e.matmul(pxt[:, i * 128:(i + 1) * 128], Yb[:, i, :], TT[:, i, :], start=True, stop=True)
                    return r
                S.op("pe", mmx, reads=[Yb.name, Y2b.name, T.name, TT.name], writes=[px.name, pxt.name])
                if not last:
                    S.op("dve", lambda e, px=px, T=T, Tn=Tn: e.tensor_tensor(
                        out=Tn[:].rearrange("p i s -> p (i s)"), in0=px[:, :], in1=T[:].rearrange("p i s -> p (i s)"),
                        op=ALU.add), reads=[px.name, T.name], writes=[Tn.name])
                S.op("dve", lambda e, pxt=pxt, TT=TT, TTn=TTn: e.tensor_tensor(
                    out=TTn[:].rearrange("p i s -> p (i s)"), in0=pxt[:, :], in1=TT[:].rearrange("p i s -> p (i s)"),
                    op=ALU.add), reads=[pxt.name, TT.name], writes=[TTn.name])
                T, TT = Tn, TTn
            R = TT
            yield
            kb, vb, WTn, U, kdk = A["kb"], A["vb"], A["WTn"], A["U"], A["kdk"]
            k4 = ktm[:].unsqueeze(2).to_broadcast([128, 2, 2, 128])
            S.op("pool", lambda e: e.tensor_tensor(out=v4(kb), in0=k4,
                                                  in1=col(4).rearrange("p (a b) -> p a b", a=2).unsqueeze(3).to_broadcast(
                                                      [128, 2, 2, 128]), op=ALU.mult),
                 reads=[A["ktm"].name, "g_toksc"], writes=[A["kb"].name])
            S.op("pool", lambda e, v=v: e.tensor_tensor(out=vb[:], in0=v[:], in1=b4(col(1)), op=ALU.mult),
                 reads=[v.name, "g_toksc"], writes=[A["vb"].name])
            S.op("pool", lambda e: e.tensor_tensor(out=v4(kdk), in0=k4,
                                                  in1=col(2).rearrange("p (a b) -> p a b", a=2).unsqueeze(3).to_broadcast(
                                                      [128, 2, 2, 128]), op=ALU.mult),
                 reads=[A["ktm"].name, "g_toksc"], writes=[A["kdk"].name])
            pW = self.psum_s(A)

            def mmW(e, pW=pW, R=R):
                for i in range(4):
                    r = e.matmul(pW[:, i * 128:(i + 1) * 128], kb[:, i, :], R[:, i, :], start=True, stop=True)
                return r
            S.op("pe", mmW, reads=[A["kb"].name, R.name], writes=[pW.name])
            S.op("act", lambda e, pW=pW: e.activation(out=WTn[:].rearrange("p i s -> p (i s)"), in_=pW[:, :],
                                                      func=AF.Copy, scale=-1.0), reads=[pW.name], writes=[A["WTn"].name])
            yield
            pU = self.psum_s(A)

            def mmU(e, pU=pU, R=R):
                for i in range(4):
                    e.matmul(pU[:, i * 128:(i + 1) * 128], R[:, i, :], vb[:, i, :], start=True, stop=False)
                    r = e.matmul(pU[:, i * 128:(i + 1) * 128], WTn[:, i, :], Sb[:, i, :], start=False, stop=True)
                return r
            S.op("pe", mmU, reads=[R.name, A["vb"].name, A["WTn"].name, A["Sb"].name], writes=[pU.name])
            S.op("act", lambda e, pU=pU: e.copy(out=U[:].rearrange("p i s -> p (i s)"), in_=pU[:, :]),
                 reads=[pU.name], writes=[A["U"].name])
            yield
            pO1, pO2 = self.psum_s(A), self.psum_s(A)

            def mmO(e, pO1=pO1, pO2=pO2, qT=qT):
                for i in range(4):
                    e.matmul(pO1[:, i * 128:(i + 1) * 128], qT[:, i // 2, :], Sb[:, i, :], start=True, stop=True)
                for i in range(4):
                    r = e.matmul(pO2[:, i * 128:(i + 1) * 128], PT[:, i, :], U[:, i, :], start=True, stop=True)
                return r
            S.op("pe", mmO, reads=[qT.name, A["Sb"].name, A["PT"].name, A["U"].name], writes=[pO1.name, pO2.name])
            o = A["o"]
            o3 = lambda p: p[:, :].rearrange("p (i s) -> p i s", i=4)
            S.op("dve", lambda e, pO1=pO1: e.tensor_tensor(out=o[:], in0=o3(pO1), in1=b4(col(3)), op=ALU.mult),
                 reads=[pO1.name, "g_toksc"], writes=[A["o"].name])
            osb = A["OSb"][u % 2]
            OSc = osb[:].rearrange("p (i s) -> p i s", i=4)
            osd = self.OSD[A["sid"], c * 128:(c + 1) * 128, :]
            oskey = "OSD%d_%d" % (A["sid"], c)
            if d == 1:
                S.op("dve", lambda e, pO2=pO2, OSc=OSc: e.tensor_tensor(out=OSc, in0=o3(pO2), in1=o[:], op=ALU.add),
                     reads=[pO2.name, A["o"].name], writes=[osb.name])
                S.defer_dma("sp", osd, osb[:], reads=[osb.name], writes=[oskey])
            else:
                S.dma("sp", osb[:], osd, reads=[oskey], writes=[osb.name])
                S.op("dve", lambda e, pO2=pO2: e.tensor_tensor(out=o[:], in0=o3(pO2), in1=o[:], op=ALU.add),
                     reads=[pO2.name, A["o"].name], writes=[A["o"].name])
                S.op("dve", lambda e, OSc=OSc: e.tensor_tensor(out=o[:], in0=o[:], in1=OSc, op=ALU.add),
                     reads=[A["o"].name, osb.name], writes=[A["o"].name])
                self.gd_finalize(hv0, tok0, A, u)
            yield
            pS = self.psum_s(A)

            def mmS(e, pS=pS):
                for i in range(4):
                    r = e.matmul(pS[:, i * 128:(i + 1) * 128], kdk[:, i, :], U[:, i, :], start=True, stop=True)
                return r
            S.op("pe", mmS, reads=[A["kdk"].name, A["U"].name], writes=[pS.name])
            S.op("dve", lambda e, rt=rt, r0=r0: e.tensor_tensor(out=Sst[:], in0=Sst[:],
                                                                in1=b4(decB[:, rt * 2 + d, r0:r0 + 4]), op=ALU.mult),
                 reads=[A["Sst"].name, "g_decB"], writes=[A["Sst"].name])
            S.op("dve", lambda e, pS=pS: e.tensor_tensor(out=Sst[:], in0=Sst[:], in1=o3(pS), op=ALU.add),
                 reads=[A["Sst"].name, pS.name], writes=[A["Sst"].name])
            S.op("act", lambda e: e.copy(out=Sb[:], in_=Sst[:]), reads=[A["Sst"].name], writes=[A["Sb"].name])
        if s["kind"] == "p":
            S.dma("sp", self.st_S[s["pi"], 0, d, hv0:hv0 + 4].rearrange("i k v -> k i v"), Sst[:],
                  reads=[A["Sst"].name], writes=["st_S"])

    def gd_finalize(self, hv0, tok0, A, u):
        S = self.S
        o, sq, ssq, y = A["o"], A["sq"], A["ssq"], A["y"]
        z, ytr = A["z"][u % 2], A["ytr"][u % 2]
        b4 = lambda ap: ap.unsqueeze(2).to_broadcast([128, 4, 128])
        m4 = lambda ap: ap.unsqueeze(1).to_broadcast([128, 4, 128])
        S.dma("sp", z[:], self.GZ[tok0:tok0 + 128, hv0 * 128:(hv0 + 4) * 128].rearrange("t (i v) -> t i v", i=4),
              reads=["GZ"], writes=[z.name])
        S.op("act", lambda e: e.activation(out=sq[:], in_=o[:], func=AF.Square), reads=[A["o"].name], writes=[A["sq"].name])
        S.op("dve", lambda e: e.tensor_reduce(out=ssq[:, 0:4], in_=sq[:], axis=AX.X, op=ALU.add), reads=[A["sq"].name],
             writes=[A["ssq"].name])
        S.op("dve", lambda e: e.tensor_scalar(out=ssq[:, 0:4], in0=ssq[:, 0:4], scalar1=1.0 / 128, scalar2=EPS,
                                              op0=ALU.mult, op1=ALU.add), reads=[A["ssq"].name], writes=[A["ssq"].name])
        S.op("act", lambda e: e.activation(out=ssq[:, 0:4], in_=ssq[:, 0:4], func=AF.Sqrt), reads=[A["ssq"].name],
             writes=[A["ssq"].name])
        S.op("dve", lambda e: e.reciprocal(out=ssq[:, 0:4], in_=ssq[:, 0:4]), reads=[A["ssq"].name], writes=[A["ssq"].name])
        S.op("dve", lambda e: e.tensor_tensor(out=o[:], in0=o[:], in1=b4(ssq[:, 0:4]), op=ALU.mult),
             reads=[A["o"].name, A["ssq"].name], writes=[A["o"].name])
        S.op("dve", lambda e: e.tensor_tensor(out=o[:], in0=o[:], in1=m4(A["gnB"][:, :]), op=ALU.mult),
             reads=[A["o"].name, "g_gnB"], writes=[A["o"].name])
        S.op("act", lambda e, z=z: e.activation(out=sq[:], in_=z[:], func=AF.Silu), reads=[z.name], writes=[A["sq"].name])
        S.op("dve", lambda e: e.tensor_tensor(out=y[:], in0=o[:], in1=sq[:], op=ALU.mult), reads=[A["o"].name, A["sq"].name],
             writes=[A["y"].name])

        def tr(e):
            for i in range(4):
                r = e.transpose(self.ptb[:, i * 128:(i + 1) * 128], y[:, i, :], self.identb[:])
            return r
        S.op("pe", tr, reads=[A["y"].name, "identb"], writes=["ptb"])
        S.op("act", lambda e: e.copy(out=ytr[:].rearrange("p i s -> p (i s)"), in_=self.ptb[:, 0:512]),
             reads=["ptb"], writes=[ytr.name])
        S.defer_dma("sp", self.YT[hv0 * 128:(hv0 + 4) * 128, tok0:tok0 + 128].rearrange("(i p) t -> p i t", p=128),
                    ytr[:], reads=[ytr.name], writes=["YT"])

    def phase_d(self, l, wout, sb):
        S = self.S
        NTT = self.NT // 128
        gateb = sb("gateb", [128, D], F32)
        wd = sb("wd", [128, 32, D], BF16)
        ytile = [sb("ytile%d" % i, [128, 32, 128], BF16) for i in range(3)]
        xbs = [sb("xb%d" % i, [128, D], F32) for i in range(2)]
        tmp = [sb("tmpd%d" % i, [128, 512], F32) for i in range(2)]
        src = wout.rearrange("(kt p) c -> p kt c", p=128)
        for q in range(8):
            S.dma("pool", wd[:, q * 4:(q + 1) * 4, :], src[:, q * 4:(q + 1) * 4, :], writes=["wd_%d" % q])
        wkeys = ["wd_%d" % q for q in range(8)]
        cur = None
        k = 0
        for tt in range(NTT):
            c = self.tile_cond(tt)
            if c != cur:
                cur = c
                S.dma("sp", gateb[:], self.MOD[c:c + 1, 2 * D:3 * D].partition_broadcast(128), reads=["MOD"],
                      writes=["gateb"])
            yt = ytile[tt % 3]
            xb = xbs[tt % 2]
            S.dma("sp", yt[:], self.YT[:, tt * 128:(tt + 1) * 128].rearrange("(kt p) t -> p kt t", p=128),
                  reads=["YT"], writes=[yt.name])
            S.dma("sp", xb[:], self.X[tt * 128:(tt + 1) * 128, :], reads=["X"], writes=[xb.name])
            for cg in range(4):
                p = self.psum()

                def mm(e, p=p, yt=yt, cg=cg):
                    for kt in range(32):
                        r = e.matmul(p[:, :], yt[:, kt, :], wd[:, kt, cg * 512:(cg + 1) * 512], start=(kt == 0),
                                     stop=(kt == 31))
                    return r
                S.op("pe", mm, reads=[yt.name] + wkeys, writes=[p.name])
                tm = tmp[k % 2]
                k += 1
                S.op("dve", lambda e, p=p, cg=cg, tm=tm: e.tensor_tensor(
                    out=tm[:], in0=p[:, :], in1=gateb[:, cg * 512:(cg + 1) * 512], op=ALU.mult),
                    reads=[p.name, "gateb"], writes=[tm.name])
                S.op("pool", lambda e, xb=xb, cg=cg, tm=tm: e.tensor_tensor(
                    out=xb[:, cg * 512:(cg + 1) * 512], in0=xb[:, cg * 512:(cg + 1) * 512], in1=tm[:], op=ALU.add),
                    reads=[xb.name, tm.name], writes=[xb.name])
            S.dma("pool", self.X[tt * 128:(tt + 1) * 128, :], xb[:], reads=[xb.name], writes=["X"])

    def final_norm(self):
        S, ins = self.S, self.ins
        NTT = self.NT // 128
        with ExitStack() as stk:
            sb = self.sbp(stk)
            big = [sb("fbig%d" % i, [128, D], F32) for i in range(3)]
            gB = sb("gfin", [128, D], F32)
            S.dma("sp", gB[:], ins["g_final"][0:1, :].partition_broadcast(128), writes=["gfin"])
            for tt in range(NTT):
                xt = big[tt % 2]
                S.dma("sp", xt[:], self.X[tt * 128:(tt + 1) * 128, :], reads=["X"], writes=[xt.name])
                ss = self.small[:, 0:1]
                rs = self.small[:, 1:2]
                junk = big[2]
                S.op("act", lambda e, xt=xt: e.activation(out=junk[:], in_=xt[:], func=AF.Square, accum_out=ss),
                     reads=[xt.name], writes=["fbig2", "small"])
                self.rstd_col(ss, rs, D)
                S.op("dve", lambda e, xt=xt: e.scalar_tensor_tensor(out=xt[:], in0=xt[:], scalar=rs, in1=gB[:],
                                                                    op0=ALU.mult, op1=ALU.mult),
                     reads=[xt.name, "small", "gfin"], writes=[xt.name])
                S.dma("pool", self.y_out[tt * 128:(tt + 1) * 128, :], xt[:], reads=[xt.name], writes=["y_out"])
            S.barrier()


def host_consts():
    s = np.arange(128)
    tril = (s[:, None] <= s[None, :]).astype(np.float32)
    triu = (s[:, None] >= s[None, :]).astype(np.float32)
    trilS = (s[:, None] < s[None, :]).astype(np.float32)
    triuS = (s[:, None] > s[None, :]).astype(np.float32)
    lvl = np.zeros((128, 7, 128), np.float32)
    for li in range(7):
        b = 1 << li
        for i in range(0, 128, 2 * b):
            lvl[i + b:i + 2 * b, li, i:i + b] = 1.0
    lvlT = np.ascontiguousarray(lvl.transpose(2, 1, 0))
    return {"ident": np.eye(128, dtype=np.float32), "tril": tril, "triu": triu, "trilS": trilS, "triuS": triuS,
            "lvl": lvl, "lvlT": lvlT,
            "ones": np.ones((128, 128), np.float32)}


def make_inputs(inp, core, x_in=None, b=None):
    if b is None:
        b = core // 4
    cond = np.stack([inp["c_ctx"], inp["c"][b]], 0)
    condT = np.ascontiguousarray(cond.reshape(2, 16, 128).transpose(2, 1, 0))
    im = dict(host_consts())
    if x_in is None:
        xp = inp["x_prompt"][2 * core:2 * core + 2].reshape(512, D)
        xs = inp["x_sample"][b]
        x_in = np.concatenate([xp, xs], 0)
    im.update(
        x_in=np.ascontiguousarray(x_in), condT=condT, w_ada=inp["w_ada"], b_ada=inp["b_ada"],
        g_norm=inp["g_norm"], g_final=inp["g_final"][None, :],
        w_sc_in=inp["w_sc_in"][0],
        w_sc_conv=np.ascontiguousarray(inp["w_sc_conv"][0].reshape(3, 32, 128).transpose(2, 1, 0)),
        w_sc_out=inp["w_sc_out"][0],
        w_ml_in=inp["w_ml_in"],
        b_mlg=np.ascontiguousarray(inp["b_ml_gate"].reshape(2, 4, 8).transpose(0, 2, 1)),
        g_ml_head=inp["g_ml_head"], w_ml_out=inp["w_ml_out"],
        mlC0=inp["cache_ml_C"][b], mln0=inp["cache_ml_n"][b], mlm0=inp["cache_ml_m"][b],
        w_gd_in=inp["w_gd_in"][0],
        w_gd_conv=np.ascontiguousarray(inp["w_gd_conv"][0].reshape(3, 64, 128).transpose(2, 1, 0)),
        gd_par=np.ascontiguousarray(np.tile(np.concatenate([inp["gd_A_log"][0].T, inp["gd_dt_bias"][0].T], 1), (4, 1))),
        g_gd_norm=inp["g_gd_norm"], w_gd_out=inp["w_gd_out"][0], gdS0=inp["cache_gd_S"][b, 0],
    )
    return im


SEQS = [dict(T=256, cond=0, roww=256, kind="p"), dict(T=256, cond=0, roww=256, kind="p"),
        dict(T=2048, cond=1, roww=64, kind="s")]


def kernel(**inputs):
    inp = {k: np.ascontiguousarray(np.asarray(v)) for k, v in inputs.items()}
    kb = K(SEQS, [0, 1, 2, 3], do_final=True)
    nc = kb.build()
    in_maps = []
    for core in range(8):
        im = make_inputs(inp, core)
        in_maps.append({k: np.ascontiguousarray(v, dtype=np.float32) for k, v in im.items() if k in kb.ins})
    res = run_bass_kernel_spmd(nc, in_maps, core_ids=list(range(8)))
    r = res.results
    y_prompt = np.concatenate([r[c]["y_out"][:512].reshape(2, 256, D) for c in range(8)], 0)
    y_sample = np.stack([r[0]["y_out"][512:], r[4]["y_out"][512:]], 0)
    st_C = np.concatenate([r[c]["st_C"] for c in range(8)], 0)
    st_n = np.concatenate([r[c]["st_n"] for c in range(8)], 0)
    st_m = np.concatenate([r[c]["st_m"] for c in range(8)], 0)
    st_S = np.concatenate([r[c]["st_S"] for c in range(8)], 0)
    return (y_prompt.astype(np.float32), y_sample.astype(np.float32), st_C.astype(np.float32),
            st_n.astype(np.float32), st_m.astype(np.float32), st_S.astype(np.float32))
```
